# Optimizing a Trainium2 kernel written in Bass

```python
import jax
import jax.numpy as jnp
from jax import lax
import numpy as np

D_MODEL = 4096
BATCH = 4
SEQ = 4096
DEPTH = 2

GRID_W = 64
EPS = 1e-6
A_HEADS = 8
A_WIDTH = D_MODEL
A_V_DIM = A_WIDTH // A_HEADS
A_QK_DIM = A_V_DIM // 2
A_QK_WIDTH = A_HEADS * A_QK_DIM
A_CONV_W = 3
MLSTM_CHUNK = 128
B_WIDTH = D_MODEL
B_HEAD_DIM = 128
B_HEADS = B_WIDTH // B_HEAD_DIM
NA_WIN_R = 8
NA_WIN_C = 16
C_WIDTH = 2 * D_MODEL
C_GROUPS = 8
C_CHUNK = 128
LAYER0_SPLITS = (A_QK_WIDTH, A_QK_WIDTH, A_WIDTH, A_WIDTH, A_WIDTH, 4 * A_HEADS, B_WIDTH, B_WIDTH, B_WIDTH, B_WIDTH)
LAYER0_COLS = 2 * A_QK_WIDTH + 3 * A_WIDTH + 4 * A_HEADS + 4 * B_WIDTH

kernel_name = 'hybrid_mlstm_natten_gmlp_encoder'


def rms_norm(x, g):
    xf = x.astype(jnp.float32)
    y = xf * lax.rsqrt(jnp.mean(xf * xf, axis=-1, keepdims=True) + EPS)
    return (y * g.astype(jnp.float32)).astype(x.dtype)


def modulation(c, w_ada, b_ada):
    mod = jax.nn.silu(c) @ w_ada + b_ada
    shift, scale, gate = jnp.split(mod, 3, axis=-1)
    return shift[:, None, :], scale[:, None, :], gate[:, None, :]


def centred_dwconv(x, w):
    taps = w.shape[0]
    pad = taps // 2
    seq = x.shape[1]
    xp = jnp.pad(x, ((0, 0), (pad, pad), (0, 0)))
    y = xp[:, 0:seq] * w[0]
    for j in range(1, taps):
        y = y + xp[:, j:j + seq] * w[j]
    return y


def mlstm_one_direction(q, k, v, ig, lf):
    bsz, nh, seq, dk = q.shape
    dv = v.shape[-1]
    L = MLSTM_CHUNK
    nc = seq // L

    def to_chunks(a):
        a = a.reshape(bsz, nh, nc, L, *a.shape[3:])
        return jnp.moveaxis(a, 2, 0)

    xs = (to_chunks(q), to_chunks(k), to_chunks(v), to_chunks(ig), to_chunks(lf))
    tril = jnp.tril(jnp.ones((L, L), dtype=bool))

    def step(carry, chunk):
        C, n, m = carry
        qc, kc, vc, igc, lfc = chunk
        g = jnp.cumsum(lfc, axis=-1)
        D = g[..., :, None] - g[..., None, :] + igc[..., None, :]
        D = jnp.where(tril, D, -jnp.inf)
        m_t = jnp.maximum(g + m[..., None], jnp.max(D, axis=-1))
        P = jnp.exp(D - m_t[..., None])
        inter = jnp.exp(g + m[..., None] - m_t)
        S_ = jnp.einsum('bhtd,bhsd->bhts', qc, kc) * P
        num = jnp.einsum('bhts,bhsv->bhtv', S_, vc) + inter[..., None] * jnp.einsum('bhtd,bhdv->bhtv', qc, C)
        den = jnp.sum(S_, axis=-1) + inter * jnp.einsum('bhtd,bhd->bht', qc, n)
        h = num / jnp.maximum(jnp.abs(den), jnp.exp(-m_t))[..., None]
        gL = g[..., -1]
        ds = gL[..., None] - g + igc
        m_new = jnp.maximum(gL + m, jnp.max(ds, axis=-1))
        w = jnp.exp(ds - m_new[..., None])
        decay = jnp.exp(gL + m - m_new)
        C_new = decay[..., None, None] * C + jnp.einsum('bhs,bhsd,bhsv->bhdv', w, kc, vc)
        n_new = decay[..., None] * n + jnp.einsum('bhs,bhsd->bhd', w, kc)
        return (C_new, n_new, m_new), h

    init = (jnp.zeros((bsz, nh, dk, dv), jnp.float32),
            jnp.zeros((bsz, nh, dk), jnp.float32),
            jnp.zeros((bsz, nh), jnp.float32))
    _, h = lax.scan(step, init, xs)
    return jnp.moveaxis(h, 0, 2).reshape(bsz, nh, seq, dv)


def mlstm_mixer(a_q, a_k, a_v, a_o, a_gates, a_conv_w, a_gate_b, a_norm_g):
    bsz, seq, _ = a_v.shape
    qk = jax.nn.silu(centred_dwconv(jnp.concatenate([a_q, a_k], axis=-1), a_conv_w))

    def heads(t, d):
        return t.reshape(bsz, seq, A_HEADS, d).transpose(0, 2, 1, 3).astype(jnp.float32)

    q = heads(qk[..., :A_QK_WIDTH], A_QK_DIM) * (A_QK_DIM ** -0.5)
    k = heads(qk[..., A_QK_WIDTH:], A_QK_DIM)
    v = heads(a_v, A_V_DIM)
    g = (a_gates.astype(jnp.float32) + a_gate_b.astype(jnp.float32)).reshape(bsz, seq, 4, A_HEADS).transpose(2, 0, 3, 1)
    h_f = mlstm_one_direction(q, k, v, g[0], jax.nn.log_sigmoid(g[1]))
    fl = lambda t: jnp.flip(t, axis=2)
    h_b = fl(mlstm_one_direction(fl(q), fl(k), fl(v), fl(g[2]), fl(jax.nn.log_sigmoid(g[3]))))
    h = (h_f + h_b).transpose(0, 2, 1, 3)
    h = rms_norm(h, a_norm_g).reshape(bsz, seq, A_WIDTH).astype(a_v.dtype)
    return jax.nn.sigmoid(a_o) * h


def neighbourhood_attention(q, k, v, q_gain, k_gain, rpb):
    bsz, seq, _ = q.shape
    rows = seq // GRID_W
    win_r = min(NA_WIN_R, rows)

    def to_grid(a):
        return a.reshape(bsz, rows, GRID_W, B_HEADS, B_HEAD_DIM).transpose(0, 3, 1, 2, 4)

    q = rms_norm(q.reshape(bsz, seq, B_HEADS, B_HEAD_DIM), q_gain) * (B_HEAD_DIM ** -0.5)
    k = rms_norm(k.reshape(bsz, seq, B_HEADS, B_HEAD_DIM), k_gain)
    qg = jnp.moveaxis(to_grid(q), 2, 0)
    kg = to_grid(k)
    vg = to_grid(v)
    qc = jnp.arange(GRID_W)
    cs = jnp.clip(qc - NA_WIN_C // 2, 0, GRID_W - NA_WIN_C)
    col_mask = (qc[None, :] >= cs[:, None]) & (qc[None, :] < cs[:, None] + NA_WIN_C)
    dc_idx = jnp.clip(qc[None, :] - qc[:, None] + NA_WIN_C - 1, 0, 2 * NA_WIN_C - 2)

    def row_block(args):
        r, q_r = args
        rs = jnp.clip(r - win_r // 2, 0, rows - win_r)
        k_r = lax.dynamic_slice_in_dim(kg, rs, win_r, axis=2)
        v_r = lax.dynamic_slice_in_dim(vg, rs, win_r, axis=2)
        s = jnp.einsum('bhqd,bhjkd->bhqjk', q_r, k_r).astype(jnp.float32)
        dr = rs + jnp.arange(win_r) - r + NA_WIN_R - 1
        bias = rpb[:, dr[None, :, None], dc_idx[:, None, :]]
        s = jnp.where(col_mask[:, None, :], s + bias.astype(jnp.float32), -jnp.inf)
        p = jax.nn.softmax(s, axis=(-2, -1)).astype(v_r.dtype)
        return jnp.einsum('bhqjk,bhjkd->bhqd', p, v_r)

    out = lax.map(row_block, (jnp.arange(rows), qg))
    return out.transpose(1, 0, 3, 2, 4).reshape(bsz, seq, B_WIDTH)


def even_mixer(h, w_in0, a_conv_w, a_gate_b, a_norm_g, b_q_gain, b_k_gain, b_rpb, w_out0):
    proj = h @ w_in0
    offsets = np.cumsum(np.array(LAYER0_SPLITS))[:-1].tolist()
    a_q, a_k, a_v, a_o, a_z, a_gates, b_q, b_k, b_v, b_z = jnp.split(proj, offsets, axis=-1)
    y_a = mlstm_mixer(a_q, a_k, a_v, a_o, a_gates, a_conv_w, a_gate_b, a_norm_g) * jax.nn.silu(a_z)
    y_b = neighbourhood_attention(b_q, b_k, b_v, b_q_gain, b_k_gain, b_rpb) * jax.nn.silu(b_z)
    return jnp.concatenate([y_a, y_b], axis=-1) @ w_out0


def odd_mixer(h, w_in1, c_v_norm_g, c_w_s, c_b_s, w_out1):
    bsz, seq, _ = h.shape
    u, v, z = jnp.split(h @ w_in1, 3, axis=-1)
    u = jax.nn.gelu(u)
    v = rms_norm(jax.nn.gelu(v), c_v_norm_g)
    v = v.reshape(bsz, seq // C_CHUNK, C_CHUNK, C_GROUPS, C_WIDTH // C_GROUPS)
    sv = jnp.einsum('gts,bnsgc->bntgc', c_w_s, v) + c_b_s.T[:, :, None]
    y = u * sv.reshape(bsz, seq, C_WIDTH) * jax.nn.silu(z)
    return y @ w_out1


def setup_inputs(seed: int = 0) -> dict:
    key = jax.random.key(seed)
    ks = jax.random.split(key, 24)
    nrm = lambda k, shape, scale: jax.random.normal(k, shape, jnp.float32) * scale
    D = D_MODEL
    fg_bias = jnp.linspace(3.0, 6.0, A_HEADS, dtype=jnp.float32)
    a_gate_b = jnp.concatenate([
        nrm(ks[7], (A_HEADS,), 0.1),
        fg_bias + nrm(ks[8], (A_HEADS,), 0.1),
        nrm(ks[9], (A_HEADS,), 0.1),
        fg_bias + nrm(ks[10], (A_HEADS,), 0.1)])
    return {
        'x': nrm(ks[0], (BATCH, SEQ, D), 1.0),
        'c': nrm(ks[1], (BATCH, D), 1.0),
        'norm_g0': 1.0 + nrm(ks[2], (D,), 0.02),
        'ada_w0': nrm(ks[3], (D, 3 * D), D ** -0.5),
        'ada_b0': nrm(ks[4], (3 * D,), 0.02),
        'w_in0': nrm(ks[5], (D, LAYER0_COLS), D ** -0.5),
        'a_conv_w': nrm(ks[6], (A_CONV_W, 2 * A_QK_WIDTH), A_CONV_W ** -0.5),
        'a_gate_b': a_gate_b,
        'a_norm_g': 1.0 + nrm(ks[11], (A_HEADS, A_V_DIM), 0.02),
        'b_q_gain': 1.0 + nrm(ks[12], (B_HEAD_DIM,), 0.02),
        'b_k_gain': 1.0 + nrm(ks[13], (B_HEAD_DIM,), 0.02),
        'b_rpb': nrm(ks[14], (B_HEADS, 2 * NA_WIN_R - 1, 2 * NA_WIN_C - 1), 0.1),
        'w_out0': nrm(ks[15], (A_WIDTH + B_WIDTH, D), (A_WIDTH + B_WIDTH) ** -0.5),
        'norm_g1': 1.0 + nrm(ks[16], (D,), 0.02),
        'ada_w1': nrm(ks[17], (D, 3 * D), D ** -0.5),
        'ada_b1': nrm(ks[18], (3 * D,), 0.02),
        'w_in1': nrm(ks[19], (D, 3 * C_WIDTH), D ** -0.5),
        'c_v_norm_g': 1.0 + nrm(ks[20], (C_WIDTH,), 0.02),
        'c_w_s': nrm(ks[21], (C_GROUPS, C_CHUNK, C_CHUNK), C_CHUNK ** -0.5),
        'c_b_s': 1.0 + nrm(ks[22], (C_GROUPS, C_CHUNK), 0.02),
        'w_out1': nrm(ks[23], (C_WIDTH, D), C_WIDTH ** -0.5),
    }


def reference(x, c, norm_g0, ada_w0, ada_b0, w_in0, a_conv_w, a_gate_b, a_norm_g, b_q_gain, b_k_gain, b_rpb, w_out0,
              norm_g1, ada_w1, ada_b1, w_in1, c_v_norm_g, c_w_s, c_b_s, w_out1):
    norm_gs = (norm_g0, norm_g1)
    ada_ws = (ada_w0, ada_w1)
    ada_bs = (ada_b0, ada_b1)
    mixers = (
        lambda h: even_mixer(h, w_in0, a_conv_w, a_gate_b, a_norm_g, b_q_gain, b_k_gain, b_rpb, w_out0),
        lambda h: odd_mixer(h, w_in1, c_v_norm_g, c_w_s, c_b_s, w_out1),
    )
    for layer in range(DEPTH):
        shift, scale, gate = modulation(c, ada_ws[layer], ada_bs[layer])
        h = rms_norm(x, norm_gs[layer]) * (1.0 + scale) + shift
        x = x + gate * mixers[layer % 2](h)
    return x
```

```python
import numpy as np
from contextlib import ExitStack
import concourse.bass as bass
import concourse.mybir as mybir
from concourse.bass_utils import run_bass_kernel_spmd

F32 = mybir.dt.float32
BF16 = mybir.dt.bfloat16
AF = mybir.ActivationFunctionType
ALU = mybir.AluOpType
AX = mybir.AxisListType

D = 4096
TH = 2048
NK = 32
L0C = 32800
EPS = 1e-6
N_DMA_SEMS = 24
SAFE_SAME_ENGINE = True

C_AQ, C_AK, C_AV, C_AO, C_AZ, C_G, C_BQ, C_BK, C_BV, C_BZ = 0, 2048, 4096, 8192, 12288, 16384, 16416, 20512, 24608, 28704


class Prog:
    def __init__(self, nc):
        self.nc = nc
        self.eng = {"pe": nc.tensor, "act": nc.scalar, "dve": nc.vector, "pool": nc.gpsimd, "sp": nc.sync}
        self.esem = {e: nc.alloc_semaphore("es_" + e) for e in self.eng}
        self.ecnt = {e: 0 for e in self.eng}
        self.dsem = [nc.alloc_semaphore("ds%d" % i) for i in range(N_DMA_SEMS)]
        self.dcnt = [0] * N_DMA_SEMS
        self.dnext = 0
        self.seen = {e: {} for e in self.eng}
        self.ops = []
        self.last_writer = {}
        self.readers = {}
        self.barrier_set = []
        self.n_emitted = 0
        self.exclusive = set()

    def op(self, eng, fn, reads=(), writes=(), dma=False):
        idx = len(self.ops)
        deps = set()
        if self.exclusive:
            ex = [b for b in reads if b in self.exclusive]
            if ex:
                reads = [b for b in reads if b not in self.exclusive]
                writes = list(writes) + ex
        for b in reads:
            w = self.last_writer.get(b)
            if w is not None:
                deps.add(w)
        for b in writes:
            w = self.last_writer.get(b)
            if w is not None:
                deps.add(w)
            for r in self.readers.get(b, {}).values():
                deps.add(r)
        for b in reads:
            d = self.readers.setdefault(b, {})
            key = ("dma", idx) if dma else eng
            d[key] = idx
        for b in writes:
            self.last_writer[b] = idx
            self.readers[b] = {}
        deps.discard(idx)
        self.ops.append(dict(eng=eng, fn=fn, deps=deps, dma=dma, sig=None))
        return idx

    def flush(self):
        ops = self.ops
        need = [False] * len(ops)
        for o in ops:
            for d in o["deps"]:
                p = ops[d]
                if p["dma"]:
                    need[d] = True
                elif p["eng"] == o["eng"] and not o["dma"]:
                    if p["eng"] != "pe" and SAFE_SAME_ENGINE:
                        need[d] = True
                else:
                    need[d] = True
        last = {}
        for i, o in enumerate(ops):
            if o["dma"]:
                need[i] = True
            else:
                last[o["eng"]] = i
        for i in last.values():
            need[i] = True
        for i, o in enumerate(ops):
            e = o["eng"]
            h = self.eng[e]
            waits = list(self.barrier_set)
            for d in sorted(o["deps"]):
                p = ops[d]
                if not need[d]:
                    continue
                if (not p["dma"]) and p["eng"] == e and not o["dma"] and (e == "pe" or not SAFE_SAME_ENGINE):
                    continue
                waits.append(p["sig"])
            seen = self.seen[e]
            for (k, sem, val) in waits:
                if seen.get(k, 0) >= val:
                    continue
                h.wait_ge(sem, val)
                seen[k] = val
            ins = o["fn"](h)
            if need[i]:
                if o["dma"]:
                    s = self.dnext
                    self.dnext = (self.dnext + 1) % N_DMA_SEMS
                    self.dcnt[s] += 16
                    ins.then_inc(self.dsem[s], 16)
                    o["sig"] = (("d", s), self.dsem[s], self.dcnt[s])
                else:
                    self.ecnt[e] += 1
                    ins.then_inc(self.esem[e], 1)
                    o["sig"] = (("e", e), self.esem[e], self.ecnt[e])
            self.n_emitted += 1
        bs = {}
        for o in ops:
            if o["sig"] is not None:
                k, sem, val = o["sig"]
                if k not in bs or bs[k][2] < val:
                    bs[k] = (k, sem, val)
        for (k, sem, val) in self.barrier_set:
            if k not in bs or bs[k][2] < val:
                bs[k] = (k, sem, val)
        self.barrier_set = list(bs.values())
        self.ops = []
        self.last_writer = {}
        self.readers = {}

    def final_wait(self):
        for e, h in self.eng.items():
            seen = self.seen[e]
            for (k, sem, val) in self.barrier_set:
                if seen.get(k, 0) >= val:
                    continue
                h.wait_ge(sem, val)
                seen[k] = val


class Ctx:
    pass


class Alloc:
    def __init__(self, nc):
        self.nc = nc
        self.es = ExitStack()

    def __enter__(self):
        self.es.__enter__()
        return self

    def __exit__(self, *a):
        return self.es.__exit__(*a)

    def sb(self, name, shape, dt):
        return self.es.enter_context(self.nc.sbuf_tensor(U(name), shape, dt))

    def ps(self, name, shape, dt):
        return self.es.enter_context(self.nc.psum_tensor(U(name), shape, dt))


_UC = [0]


def U(name):
    _UC[0] += 1
    return "%s_%d" % (name, _UC[0])


T = 4096
TO = 2048
NOWN = 16
NCH = 32
HD = 8
NB = 32
CW = 8192


def build(taps=(), stop_after=None, quick=False):
    nc = bass.Bass("TRN2", target_bir_lowering=False)
    P = Prog(nc)
    K = Ctx()
    K.nc, K.P, K.taps, K.quick = nc, P, set(taps), quick

    order = ["mod", "in0", "na", "out0", "in1", "gmlp", "out1"]
    last = order.index(stop_after) if stop_after else len(order) - 1
    K.in_shapes = {}

    def din(name, shape, dt=F32, first="mod"):
        if order.index(first) > last:
            shape = [128, 128]
        K.in_shapes[name] = tuple(shape)
        return nc.dram_tensor(name, list(shape), dt, kind="ExternalInput").ap()

    def dscr(name, shape, dt=BF16):
        kind = "ExternalOutput" if name in K.taps else "Internal"
        return nc.dram_tensor(name, list(shape), dt, kind=kind).ap()

    I = Ctx()
    K.I = I
    I.x = din("x", [T, D])
    I.c_fm = din("c_fm", [128, NK])
    I.g_rep = [din("g_rep%d" % l, [128, D]) for l in range(2)]
    I.ada_w = [din("ada_w%d" % l, [D, 3 * D]) for l in range(2)]
    I.ada_b_rep = [din("ada_b_rep%d" % l, [128, 3 * D]) for l in range(2)]
    I.w_in0 = din("w_in0", [D, L0C], first="in0")
    I.ident = din("ident", [128, 128])
    I.w_gates = din("w_gates", [D, 32])
    I.ab = din("ab", [128, 2])
    I.convw = din("convw", [128, 32, 3])
    I.gate_b = din("gate_b", [8, 4])
    I.ang_rep = din("ang_rep", [128, D])
    I.gqk = din("gqk", [128, 2])
    I.B2 = din("B2", [NB, 15, 64, 64])
    I.cmask = din("cmask", [128, 64])
    I.masks = din("masks", [2, 128, 128])
    I.w_out = [din("w_out%d" % l, [CW, D], first=("out0", "out1")[l]) for l in range(2)]
    I.w_in1 = din("w_in1", [D, 3 * CW], first="in1")
    I.vg_rep = din("vg_rep", [128, CW])
    I.WsT = din("WsT", [8, 128, 128])
    I.bs_fm = din("bs_fm", [128, 8])
    K.out = nc.dram_tensor("out", [TO, D], F32, kind="ExternalOutput").ap()

    S = Ctx()
    K.S = S
    S.modrep = [dscr("modrep%d" % l, [3, 128, D], F32) for l in range(2)]
    S.aqT = dscr("aqT", [2048, T])
    S.akT = dscr("akT", [2048, T])
    S.av = dscr("av", [T, D])
    S.ao = dscr("ao", [T, D])
    S.az = dscr("az", [T, D])
    S.gT = dscr("gT", [32, T], F32)
    S.bqT = dscr("bqT", [D, T])
    S.bzT = dscr("bzT", [D, T])
    S.bkT = dscr("bkT", [D, T])
    S.bv = dscr("bv", [T, D])
    S.gsc = dscr("gsc", [2, 2, 8, T], F32)
    S.hfs = dscr("hfs", [TO, 512], F32)
    S.yT = [dscr("yT%d" % l, [CW, TO]) for l in range(2)]
    S.x1 = dscr("x1", [TO, D], F32)
    S.gu = dscr("gu", [TO, CW])
    S.gv = dscr("gv", [TO, CW])
    S.sz = dscr("sz", [TO, CW])

    with Alloc(nc) as A_:
        ident_f = A_.sb("ident_f", [128, 128], F32)
        ident_b = A_.sb("ident_b", [128, 128], BF16)
        eps_t = A_.sb("eps_t", [128, 1], F32)
        K.ident_f, K.ident_b, K.eps_t = ident_f, ident_b, eps_t
        P.op("sp", lambda h: h.dma_start(out=ident_f[:], in_=I.ident), writes=["ident_f"], dma=True)
        P.op("dve", lambda h: h.tensor_copy(ident_b[:], ident_f[:]), reads=["ident_f"], writes=["ident_b"])
        P.op("pool", lambda h: h.memset(eps_t[:], EPS), writes=["eps_t"])
        P.flush()
        stages = [
            ("mod", lambda: phase_mod(K)),
            ("in0", lambda: phase_l0_inproj(K)),
            ("na", lambda: phase_mixers(K)),
            ("out0", lambda: phase_outproj(K, 0, I.x, S.x1)),
            ("in1", lambda: phase_l1_inproj(K)),
            ("gmlp", lambda: phase_gmlp(K)),
            ("out1", lambda: phase_outproj(K, 1, S.x1, K.out)),
        ]
        for name, fn in stages:
            fn()
            if stop_after == name:
                break
    K.P.final_wait()
    nc._in_shapes = K.in_shapes
    return nc


def phase_mod(K):
    nc, P, I, S = K.nc, K.P, K.I, K.S
    with Alloc(nc) as A_:
        cf = A_.sb("cf", [128, NK], F32)
        cs = A_.sb("cs", [128, NK], F32)
        csb = A_.sb("csb", [128, NK, 128], BF16)
        mw0 = A_.sb("mw0", [128, NK, 512], BF16)
        mw1 = A_.sb("mw1", [128, NK, 512], BF16)
        mb = A_.sb("mb", [128, 512], F32)
        mg = A_.sb("mg", [128, 512], F32)
        mo0 = A_.sb("mo0", [128, 512], F32)
        mo1 = A_.sb("mo1", [128, 512], F32)
        mps0 = A_.ps("mps0", [128, 512], F32)
        mps1 = A_.ps("mps1", [128, 512], F32)
        mw = [mw0, mw1]
        mo = [mo0, mo1]
        mps = [mps0, mps1]
        P.op("sp", lambda h: h.dma_start(out=cf[:], in_=I.c_fm), writes=["cf"], dma=True)
        P.op("act", lambda h: h.activation(cs[:], cf[:], AF.Silu), reads=["cf"], writes=["cs"])
        P.op("dve", lambda h: h.tensor_copy(csb[:], cs[:].unsqueeze(2).to_broadcast([128, NK, 128])), reads=["cs"], writes=["csb"])
        it = 0
        for l in range(2):
            for n in range(24):
                wb = mw[it % 2]
                wk = "mw%d" % (it % 2)
                src = I.ada_w[l][:, n * 512:(n + 1) * 512].rearrange("(k p) c -> p k c", p=128)
                for q in range(4):
                    P.op("pool", lambda h, wb=wb, src=src, q=q: h.dma_start(out=wb[:, q * 8:(q + 1) * 8, :], in_=src[:, q * 8:(q + 1) * 8, :]),
                         writes=[wk], dma=True)
                ps = mps[it % 2]
                pk = "mps%d" % (it % 2)
                for k in range(NK):
                    P.op("pe", lambda h, ps=ps, wb=wb, k=k: h.matmul(ps[:], csb[:, k, :], wb[:, k, :], start=(k == 0), stop=(k == NK - 1)),
                         reads=[wk, "csb"], writes=[pk])
                P.op("sp", lambda h, l=l, n=n: h.dma_start(out=mb[:], in_=I.ada_b_rep[l][:, n * 512:(n + 1) * 512]), writes=["mb"], dma=True)
                o = mo[it % 2]
                ok = "mo%d" % (it % 2)
                which = n // 8
                cb = (n % 8) * 512
                P.op("dve", lambda h, ps=ps, o=o: h.tensor_tensor(o[:], ps[:], mb[:], ALU.add), reads=[pk, "mb"], writes=[ok])
                if which == 1:
                    P.op("sp", lambda h, l=l, cb=cb: h.dma_start(out=mg[:], in_=I.g_rep[l][:, cb:cb + 512]), writes=["mg"], dma=True)
                    P.op("dve", lambda h, o=o: h.scalar_tensor_tensor(o[:], o[:], 1.0, mg[:], ALU.add, ALU.mult), reads=[ok, "mg"], writes=[ok])
                P.op("sp", lambda h, o=o, l=l, which=which, cb=cb: h.dma_start(out=S.modrep[l][which, :, cb:cb + 512], in_=o[:]),
                     reads=[ok], dma=True)
                it += 1
        P.flush()


def norm_pass(K, x_dram, ntok, layer, hT):
    nc, P, S = K.nc, K.P, K.S
    ntile = ntok // 128
    with Alloc(nc) as A_:
        nG = A_.sb("nG", [128, D], F32)
        nS = A_.sb("nS", [128, D], F32)
        nx = A_.sb("nx", [128, D], F32)
        nsq = A_.sb("nsq", [128, D], BF16)
        nh = A_.sb("nh", [128, D], BF16)
        nss = A_.sb("nss", [128, 4], F32)
        ntp0 = A_.ps("ntp0", [128, 512], BF16)
        ntp1 = A_.ps("ntp1", [128, 512], BF16)
        ntp = [ntp0, ntp1]
        eps_t = K.eps_t
        P.op("sp", lambda h: h.dma_start(out=nS[:], in_=S.modrep[layer][0]), writes=["nS"], dma=True)
        P.op("sp", lambda h: h.dma_start(out=nG[:], in_=S.modrep[layer][1]), writes=["nG"], dma=True)
        j = 0
        for t in range(ntile):
            P.op("sp", lambda h, t=t: h.dma_start(out=nx[:], in_=x_dram[t * 128:(t + 1) * 128, :]), writes=["nx"], dma=True)
            P.op("act", lambda h: h.activation(nsq[:], nx[:], AF.Square), reads=["nx"], writes=["nsq"])
            P.op("dve", lambda h: h.tensor_reduce(nss[:, 0:1], nsq[:], AX.X, ALU.add), reads=["nsq"], writes=["nss"])
            P.op("act", lambda h: h.activation(nss[:, 1:2], nss[:, 0:1], AF.Sqrt, bias=eps_t[:, 0:1], scale=1.0 / D), reads=["nss", "eps_t"], writes=["nss"])
            P.op("dve", lambda h: h.reciprocal(nss[:, 2:3], nss[:, 1:2]), reads=["nss"], writes=["nss"])
            P.op("dve", lambda h: h.scalar_tensor_tensor(nx[:], nx[:], nss[:, 2:3], nG[:], ALU.mult, ALU.mult), reads=["nx", "nss", "nG"], writes=["nx"])
            P.op("pool", lambda h: h.tensor_tensor(nh[:], nx[:], nS[:], ALU.add), reads=["nx", "nS"], writes=["nh"])
            for kk in range(0, NK, 4):
                tp = ntp[j % 2]
                tk = "ntp%d" % (j % 2)
                for q in range(4):
                    P.op("pe", lambda h, tp=tp, q=q, kk=kk: h.transpose(tp[:, q * 128:(q + 1) * 128], nh[:, (kk + q) * 128:(kk + q + 1) * 128], K.ident_b[:]),
                         reads=["nh", "ident_b"], writes=[tk])
                if j % 2:
                    P.op("act", lambda h, tp=tp, kk=kk, t=t: h.copy(hT[:, kk:kk + 4, t * 128:(t + 1) * 128], tp[:].rearrange("p (a b) -> p a b", a=4)),
                         reads=[tk], writes=["hT"])
                else:
                    P.op("dve", lambda h, tp=tp, kk=kk, t=t: h.tensor_copy(hT[:, kk:kk + 4, t * 128:(t + 1) * 128], tp[:].rearrange("p (a b) -> p a b", a=4)),
                         reads=[tk], writes=["hT"])
                j += 1
        P.flush()


def inproj(K, hT, jobs):
    nc, P = K.nc, K.P
    with Alloc(nc) as A_:
        iw0 = A_.sb("iw0", [128, NK, 512], BF16)
        iw1 = A_.sb("iw1", [128, NK, 512], BF16)
        ips0 = A_.ps("ips0", [128, 512], F32)
        ips1 = A_.ps("ips1", [128, 512], F32)
        ips2 = A_.ps("ips2", [128, 512], F32)
        ips3 = A_.ps("ips3", [128, 512], F32)
        iw = [iw0, iw1]
        ips = [ips0, ips1, ips2, ips3]
        it = 0
        pj = 0
        for job in jobs:
            ncols = job["ncols"]
            for c0 in range(0, ncols, 512):
                cw = min(512, ncols - c0)
                wb = iw[it % 2]
                wk = "iw%d" % (it % 2)
                it += 1
                src = job["w"][:, c0:c0 + cw].rearrange("(k p) c -> p k c", p=128)
                for q in range(4):
                    P.op("pool", lambda h, wb=wb, src=src, q=q, cw=cw: h.dma_start(out=wb[:, q * 8:(q + 1) * 8, 0:cw], in_=src[:, q * 8:(q + 1) * 8, :]),
                         writes=[wk], dma=True)
                if job["mode"] == "FM":
                    for cc in range(0, cw, 128):
                        m = min(128, cw - cc)
                        for tt in job["tiles"]:
                            ps = ips[pj % 4]
                            pk = "ips%d" % (pj % 4)
                            pj += 1
                            for k in range(NK):
                                P.op("pe", lambda h, ps=ps, wb=wb, k=k, cc=cc, m=m, tt=tt: h.matmul(ps[0:m, :], wb[:, k, cc:cc + m], hT[:, k, tt * 512:(tt + 1) * 512], start=(k == 0), stop=(k == NK - 1)),
                                     reads=[wk, "hT"], writes=[pk])
                            job["evac"](ps, pk, c0 + cc, tt, m)
                else:
                    for tt in job["tiles"]:
                        ps = ips[pj % 4]
                        pk = "ips%d" % (pj % 4)
                        pj += 1
                        for k in range(NK):
                            P.op("pe", lambda h, ps=ps, wb=wb, k=k, cw=cw, tt=tt: h.matmul(ps[:, 0:cw], hT[:, k, tt * 128:(tt + 1) * 128], wb[:, k, 0:cw], start=(k == 0), stop=(k == NK - 1)),
                                 reads=[wk, "hT"], writes=[pk])
                        job["evac"](ps, pk, c0, tt, cw)
        P.flush()


class Stager:
    def __init__(self, tiles, tag):
        self.tiles, self.tag, self.i = tiles, tag, 0

    def next(self):
        n = len(self.tiles)
        t = self.tiles[self.i % n]
        k = "%s%d" % (self.tag, self.i % n)
        self.i += 1
        return t, k


def phase_l0_inproj(K):
    nc, P, I, S = K.nc, K.P, K.I, K.S
    with nc.sbuf_tensor(U("hT"), [128, NK, 2048], BF16) as hT:
        for ps_i in range(2):
            tok0 = ps_i * 2048
            norm_pass(K, I.x[tok0:tok0 + 2048, :], 2048, 0, hT)
            with Alloc(nc) as A_:
                sb0 = A_.sb("sb0", [128, 512], BF16)
                sb1 = A_.sb("sb1", [128, 512], BF16)
                sb2 = A_.sb("sb2", [128, 512], BF16)
                sb3 = A_.sb("sb3", [128, 512], BF16)
                sf0 = A_.sb("sf0", [128, 512], F32)
                sf1 = A_.sb("sf1", [128, 512], F32)
                stb = Stager([sb0, sb1, sb2, sb3], "sb")
                stf = Stager([sf0, sf1], "sf")
                cnt = [0]

                def ev_fm(dst, f32=False):
                    def f(ps, pk, c, tt, m):
                        st, sk = (stf if f32 else stb).next()
                        cnt[0] += 1
                        if cnt[0] % 2 and not f32:
                            P.op("act", lambda h: h.copy(st[0:m, :], ps[0:m, :]), reads=[pk], writes=[sk])
                        else:
                            P.op("dve", lambda h: h.tensor_copy(st[0:m, :], ps[0:m, :]), reads=[pk], writes=[sk])
                        P.op("sp", lambda h: h.dma_start(out=dst[c:c + m, tok0 + tt * 512:tok0 + (tt + 1) * 512], in_=st[0:m, :]), reads=[sk], dma=True)
                    return f

                def ev_tm(dst):
                    def f(ps, pk, c, tt, w):
                        st, sk = stb.next()
                        cnt[0] += 1
                        if cnt[0] % 2:
                            P.op("act", lambda h: h.copy(st[:, 0:w], ps[:, 0:w]), reads=[pk], writes=[sk])
                        else:
                            P.op("dve", lambda h: h.tensor_copy(st[:, 0:w], ps[:, 0:w]), reads=[pk], writes=[sk])
                        P.op("sp", lambda h: h.dma_start(out=dst[tok0 + tt * 128:tok0 + (tt + 1) * 128, c:c + w], in_=st[:, 0:w]), reads=[sk], dma=True)
                    return f

                W = I.w_in0
                if ps_i == 0:
                    jobs = [
                        dict(mode="FM", w=I.w_gates, ncols=32, tiles=range(4), evac=ev_fm(S.gT, True)),
                        dict(mode="FM", w=W[:, C_AQ:C_AQ + 2048], ncols=2048, tiles=range(4), evac=ev_fm(S.aqT)),
                        dict(mode="FM", w=W[:, C_AK:C_AK + 2048], ncols=2048, tiles=range(4), evac=ev_fm(S.akT)),
                        dict(mode="TM", w=W[:, C_AV:C_AV + 4096], ncols=4096, tiles=range(16), evac=ev_tm(S.av)),
                        dict(mode="TM", w=W[:, C_AO:C_AO + 4096], ncols=4096, tiles=range(16), evac=ev_tm(S.ao)),
                        dict(mode="TM", w=W[:, C_AZ:C_AZ + 4096], ncols=4096, tiles=range(16), evac=ev_tm(S.az)),
                        dict(mode="FM", w=W[:, C_BQ:C_BQ + 4096], ncols=4096, tiles=range(4), evac=ev_fm(S.bqT)),
                        dict(mode="FM", w=W[:, C_BK:C_BK + 4096], ncols=4096, tiles=range(4), evac=ev_fm(S.bkT)),
                        dict(mode="TM", w=W[:, C_BV:C_BV + 4096], ncols=4096, tiles=range(16), evac=ev_tm(S.bv)),
                        dict(mode="FM", w=W[:, C_BZ:C_BZ + 4096], ncols=4096, tiles=range(4), evac=ev_fm(S.bzT)),
                    ]
                else:
                    jobs = [
                        dict(mode="FM", w=I.w_gates, ncols=32, tiles=range(4), evac=ev_fm(S.gT, True)),
                        dict(mode="FM", w=W[:, C_AQ:C_AQ + 2048], ncols=2048, tiles=range(1), evac=ev_fm(S.aqT)),
                        dict(mode="FM", w=W[:, C_AK:C_AK + 2048], ncols=2048, tiles=range(4), evac=ev_fm(S.akT)),
                        dict(mode="TM", w=W[:, C_AV:C_AV + 4096], ncols=4096, tiles=range(16), evac=ev_tm(S.av)),
                        dict(mode="FM", w=W[:, C_BK:C_BK + 4096], ncols=4096, tiles=range(1), evac=ev_fm(S.bkT)),
                        dict(mode="TM", w=W[:, C_BV:C_BV + 4096], ncols=4096, tiles=range(4), evac=ev_tm(S.bv)),
                    ]
                inproj(K, hT, jobs)


NQR = 33
NKT = TO + 256


def phase_mixers(K):
    nc, P, I, S = K.nc, K.P, K.I, K.S
    heads = range(1) if K.quick else range(HD)
    with nc.sbuf_tensor(U("Ecol"), [128, 2, NCH, 8], F32) as Ecol:
        with Alloc(nc) as A_:
            gb = A_.sb("gb", [8, 4], F32)
            gi = A_.sb("gi", [8, T], F32)
            gf = A_.sb("gf", [8, T], F32)
            gG = A_.sb("gG", [8, T], F32)
            gbt = A_.sb("gbt", [8, T], F32)
            gM = A_.sb("gM", [8, T], F32)
            gMp = A_.sb("gMp", [8, T], F32)
            gz = A_.sb("gz", [8, T], F32)
            go1 = A_.sb("go1", [8, T], F32)
            go2 = A_.sb("go2", [8, T], F32)
            gE = A_.sb("gE", [8, T], F32)
            gps = A_.ps("gps", [128, 512], F32)
            P.op("sp", lambda h: h.dma_start(out=gb[:], in_=I.gate_b), writes=["gb"], dma=True)
            P.op("pool", lambda h: h.memset(gz[:], 0.0), writes=["gz"])
            for d in range(2):
                rv = (lambda ap: ap[:, ::-1]) if d == 1 else (lambda ap: ap)
                P.op("sp", lambda h, d=d: h.dma_start(out=gi[:], in_=S.gT[16 * d:16 * d + 8, :]), writes=["gi"], dma=True)
                P.op("sp", lambda h, d=d: h.dma_start(out=gf[:], in_=S.gT[16 * d + 8:16 * d + 16, :]), writes=["gf"], dma=True)
                P.op("dve", lambda h, d=d: h.tensor_scalar(gi[:], gi[:], gb[:, 2 * d:2 * d + 1], None, ALU.add), reads=["gi", "gb"], writes=["gi"])
                P.op("dve", lambda h, d=d: h.tensor_scalar(gf[:], gf[:], gb[:, 2 * d + 1:2 * d + 2], None, ALU.add), reads=["gf", "gb"], writes=["gf"])
                P.op("act", lambda h: h.activation(gf[:], gf[:], AF.Exp, scale=-1.0), reads=["gf"], writes=["gf"])
                P.op("act", lambda h: h.activation(gf[:], gf[:], AF.Ln, bias=1.0), reads=["gf"], writes=["gf"])
                P.op("dve", lambda h: h.tensor_scalar(gf[:], gf[:], -1.0, None, ALU.mult), reads=["gf"], writes=["gf"])
                P.op("dve", lambda h, rv=rv: h.tensor_tensor_scan(rv(gG[:, :]), rv(gf[:, :]), rv(gz[:, :]), 0.0, ALU.add, ALU.add), reads=["gf", "gz"], writes=["gG"])
                P.op("dve", lambda h: h.tensor_tensor(gbt[:], gi[:], gG[:], ALU.subtract), reads=["gi", "gG"], writes=["gbt"])
                P.op("dve", lambda h, rv=rv: h.tensor_tensor_scan(rv(gM[:, :]), rv(gbt[:, :]), rv(gbt[:, :]), 0.0, ALU.max, ALU.max), reads=["gbt"], writes=["gM"])
                M3 = gM[:, :].rearrange("p (c t) -> p c t", t=128)
                Mp3 = gMp[:, :].rearrange("p (c t) -> p c t", t=128)
                if d == 0:
                    P.op("pool", lambda h, Mp3=Mp3: h.memset(Mp3[:, 0:1, :], 0.0), writes=["gMp"])
                    P.op("dve", lambda h, M3=M3, Mp3=Mp3: h.tensor_copy(Mp3[:, 1:NCH, :], M3[:, 0:NCH - 1, 127:128].to_broadcast([8, NCH - 1, 128])), reads=["gM"], writes=["gMp"])
                else:
                    P.op("pool", lambda h, Mp3=Mp3: h.memset(Mp3[:, NCH - 1:NCH, :], 0.0), writes=["gMp"])
                    P.op("dve", lambda h, M3=M3, Mp3=Mp3: h.tensor_copy(Mp3[:, 0:NCH - 1, :], M3[:, 1:NCH, 0:1].to_broadcast([8, NCH - 1, 128])), reads=["gM"], writes=["gMp"])
                P.op("dve", lambda h: h.tensor_tensor(go1[:], gMp[:], gM[:], ALU.subtract), reads=["gMp", "gM"], writes=["go1"])
                P.op("act", lambda h: h.activation(go1[:], go1[:], AF.Exp), reads=["go1"], writes=["go1"])
                P.op("sp", lambda h, d=d: h.dma_start(out=S.gsc[d, 0], in_=go1[:]), reads=["go1"], dma=True)
                P.op("dve", lambda h: h.tensor_tensor(go2[:], gbt[:], gMp[:], ALU.subtract), reads=["gbt", "gMp"], writes=["go2"])
                P.op("act", lambda h: h.activation(go2[:], go2[:], AF.Exp), reads=["go2"], writes=["go2"])
                P.op("sp", lambda h, d=d: h.dma_start(out=S.gsc[d, 1], in_=go2[:]), reads=["go2"], dma=True)
                P.op("dve", lambda h: h.tensor_tensor(gE[:], gG[:], gM[:], ALU.add), reads=["gG", "gM"], writes=["gE"])
                P.op("act", lambda h: h.activation(gE[:], gE[:], AF.Exp, scale=-1.0), reads=["gE"], writes=["gE"])
                for c8 in range(0, NCH, 8):
                    for cc in range(8):
                        c = c8 + cc
                        P.op("pe", lambda h, c=c, cc=cc: h.transpose(gps[:, cc * 8:(cc + 1) * 8], gE[:, c * 128:(c + 1) * 128], K.ident_f[0:8, 0:8]),
                             reads=["gE", "ident_f"], writes=["gps"])
                    P.op("dve", lambda h, d=d, c8=c8: h.tensor_copy(Ecol[:, d, c8:c8 + 8, :], gps[:, 0:64].rearrange("p (c h) -> p c h", h=8)), reads=["gps"], writes=["Ecol"])
            P.flush()

        with Alloc(nc) as A_:
            qraw = A_.sb("qraw", [128, 2, TO + 514], BF16)
            kraw = A_.sb("kraw", [128, 2, T + 2], BF16)
            cv = A_.sb("cv", [128, 1024], F32)
            qc = A_.sb("qc", [128, 2, TO], BF16)
            kc = A_.sb("kc", [128, 2, T], BF16)
            vx = A_.sb("vx", [128, NCH, 516], BF16)
            rep = A_.sb("rep", [128, 2, 2, T], BF16)
            cw = A_.sb("cw", [128, 32, 3], F32)
            ang = A_.sb("ang", [128, 512], F32)
            mk = A_.sb("mk", [128, 2, 128], F32)
            Pst = A_.sb("Pst", [128, 2, 512], F32)
            Pn = A_.sb("Pn", [128, 2], F32)
            Cb = A_.sb("Cb", [128, 2, 512], BF16)
            nb = A_.sb("nb", [128, 2], BF16)
            qs = A_.sb("qs", [128, 2, 128], BF16)
            ks = A_.sb("ks", [128, 2, 128], BF16)
            ST = A_.sb("ST", [128, 128], BF16)
            kst = A_.sb("kst", [128, 256], BF16)
            sm = A_.sb("sm", [128, 8], F32)
            hh = A_.sb("hh", [128, 512], F32)
            hfl = A_.sb("hf", [128, 512], F32)
            hsq = A_.sb("hsq", [128, 512], BF16)
            ot = A_.sb("ot", [128, 512], BF16)
            zt = A_.sb("zt", [128, 512], BF16)
            of_ = A_.sb("of", [128, 512], F32)
            zf = A_.sb("zf", [128, 512], F32)
            yb = A_.sb("yb", [128, 512], BF16)
            yTt = A_.sb("yT", [128, 4, 128], BF16)
            p_num = A_.ps("p_num", [128, 512], F32)
            p_dc0 = A_.ps("p_dc0", [128, 512], F32)
            p_dc1 = A_.ps("p_dc1", [128, 512], F32)
            p_dc = [p_dc0, p_dc1]
            P.op("sp", lambda h: h.dma_start(out=cw[:], in_=I.convw), writes=["cw"], dma=True)
            P.op("sp", lambda h: h.dma_start(out=mk[:], in_=I.masks.rearrange("d p t -> p d t")), writes=["mk"], dma=True)
            P.exclusive = {"pmisc", "pbf", "pod"}
            pmisc = A_.ps("pmisc", [128, 512], F32)
            pbf = A_.ps("pbf", [128, 1024], BF16)
            p_st = pmisc[:, 0:128]
            p_den = pmisc[:, 128:136]
            p_dn = pmisc[:, 136:144]
            p_kt = pbf[:, 0:256]
            p_tr = pbf[:, 512:1024]
            QW = NQR * 64
            nqn = A_.sb("nqn", [128, 64 + TO + 64], BF16)
            nqs = A_.sb("nqs", [128, QW], BF16)
            nkn = A_.sb("nkn", [128, NKT], BF16)
            nzs = A_.sb("nzs", [128, TO], BF16)
            ve = A_.sb("ve", [128, 18, 128], BF16)
            vo = A_.sb("vo", [128, 17, 128], BF16)
            EBe = A_.sb("EBe", [128, 7, 64], F32)
            EBo = A_.sb("EBo", [128, 7, 64], F32)
            cm = A_.sb("cm", [128, 64], F32)
            gq = A_.sb("gq", [128, 2], F32)
            ab = A_.sb("ab", [128, 2], F32)
            onesS = A_.sb("onesS", [128, 128], BF16)
            ones1 = A_.sb("ones1", [128, 128], BF16)
            sq = A_.sb("sq", [128, 512], BF16)
            sd = A_.sb("sd", [128, 512], F32)
            pe0 = A_.sb("pe0", [128, 4, 64], F32)
            pe1 = A_.sb("pe1", [128, 4, 64], F32)
            pt0 = A_.sb("pt0", [128, 4, 64], BF16)
            pt1 = A_.sb("pt1", [128, 4, 64], BF16)
            rd = A_.sb("rd", [128, 512], F32)
            Ys = A_.sb("Ys", [128, QW], F32)
            yo = A_.sb("yo", [128, 512], F32)
            yob = A_.sb("yob", [128, 512], BF16)
            pod = A_.ps("pod", [128, 512], F32)
            p_o = pod[:, 0:256]
            p_d = pod[:, 256:512]
            p_ms = pod
            psb0 = A_.ps("psb0", [128, 8, 64], F32)
            psb1 = A_.ps("psb1", [128, 8, 64], F32)
            p_s = [psb0[:, 0:4, :], psb1[:, 0:4, :]]
            pe_ = [pe0, pe1]
            pt_ = [pt0, pt1]
            P.op("sp", lambda h: h.dma_start(out=cm[:], in_=I.cmask), writes=["cm"], dma=True)
            P.op("sp", lambda h: h.dma_start(out=gq[:], in_=I.gqk), writes=["gq"], dma=True)
            P.op("sp", lambda h: h.dma_start(out=ab[:], in_=I.ab), writes=["ab"], dma=True)
            P.op("dve", lambda h: h.tensor_scalar(gq[:, 0:1], gq[:, 0:1], 128.0 ** -0.5, None, ALU.mult), reads=["gq"], writes=["gq"])
            P.op("pool", lambda h: h.memset(onesS[:], 1.0 / 128.0), writes=["onesS"])
            P.op("pool", lambda h: h.memset(ones1[:], 1.0), writes=["ones1"])
            P.op("pool", lambda h: h.memset(nqn[:], 0.0), writes=["nqn"])

            def gen_ml():
                for hd in heads:
                    for (raw, rk, src) in ((qraw, "qraw", S.aqT), (kraw, "kraw", S.akT)):
                        P.op("pool", lambda h, raw=raw: h.memset(raw[:, :, 0:1], 0.0), writes=[rk])
                        if rk == "kraw":
                            P.op("pool", lambda h, raw=raw: h.memset(raw[:, :, T + 1:T + 2], 0.0), writes=[rk])
                        nv = (TO + 512) if rk == "qraw" else T
                        P.op("sp", lambda h, raw=raw, src=src, hd=hd, nv=nv: h.dma_start(out=raw[:, :, 1:nv + 1], in_=src[hd * 256:(hd + 1) * 256, 0:nv].rearrange("(c p) t -> p c t", p=128)),
                             writes=[rk], dma=True)
                    P.op("sp", lambda h, hd=hd: h.dma_start(out=vx[:, :, 0:512], in_=S.av[:, hd * 512:(hd + 1) * 512].rearrange("(c p) v -> p c v", p=128)),
                         writes=["vx"], dma=True)
                    P.op("pool", lambda h: h.memset(vx[:, :, 512:516], 1.0), writes=["vx"])
                    for d in range(2):
                        for j in range(2):
                            P.op("pool", lambda h, d=d, j=j, hd=hd: h.dma_start(out=rep[:, d, j, :], in_=S.gsc[d, j, hd:hd + 1, :].to_broadcast([128, T])),
                                 writes=["rep"], dma=True)
                    P.op("sp", lambda h, hd=hd: h.dma_start(out=ang[:], in_=I.ang_rep[:, hd * 512:(hd + 1) * 512]), writes=["ang"], dma=True)
                    for (raw, rk, dst, dk, cbase) in ((qraw, "qraw", qc, "qc", 0), (kraw, "kraw", kc, "kc", 16)):
                        for dc in range(2):
                            ci = cbase + hd * 2 + dc
                            Lc = TO if rk == "qraw" else T
                            for cb0 in range(0, Lc, 1024):
                                P.op("dve", lambda h, raw=raw, dc=dc, ci=ci, cb0=cb0: h.tensor_scalar(cv[:], raw[:, dc, cb0 + 1:cb0 + 1025], cw[:, ci, 1:2], None, ALU.mult), reads=[rk, "cw"], writes=["cv"])
                                P.op("dve", lambda h, raw=raw, dc=dc, ci=ci, cb0=cb0: h.scalar_tensor_tensor(cv[:], raw[:, dc, cb0:cb0 + 1024], cw[:, ci, 0:1], cv[:], ALU.mult, ALU.add), reads=[rk, "cw", "cv"], writes=["cv"])
                                P.op("dve", lambda h, raw=raw, dc=dc, ci=ci, cb0=cb0: h.scalar_tensor_tensor(cv[:], raw[:, dc, cb0 + 2:cb0 + 1026], cw[:, ci, 2:3], cv[:], ALU.mult, ALU.add), reads=[rk, "cw", "cv"], writes=["cv"])
                                P.op("act", lambda h, dst=dst, dc=dc, cb0=cb0: h.activation(dst[:, dc, cb0:cb0 + 1024], cv[:], AF.Silu), reads=["cv"], writes=[dk])
                                yield
                    for d in range(2):
                        order = range(NOWN) if d == 0 else range(NCH - 1, -1, -1)
                        P.op("pool", lambda h: h.memset(Pst[:], 0.0), writes=["Pst"])
                        P.op("pool", lambda h: h.memset(Pn[:], 0.0), writes=["Pn"])
                        P.op("pool", lambda h: h.memset(Cb[:], 0.0), writes=["Cb"])
                        P.op("pool", lambda h: h.memset(nb[:], 0.0), writes=["nb"])
                        prev_dec = None
                        for c in order:
                            t0 = c * 128
                            ir = rep[:, d, 0, t0:t0 + 128]
                            wr = rep[:, d, 1, t0:t0 + 128]
                            dec_col = (t0 + 127) if d == 0 else t0
                            dec = rep[:, d, 0, dec_col:dec_col + 1]
                            full = c < NOWN
                            if full:
                                P.op("dve", lambda h, ir=ir, t0=t0: h.scalar_tensor_tensor(qs[:], qc[:, :, t0:t0 + 128], 1.0 / 16.0, ir.unsqueeze(1).to_broadcast([128, 2, 128]), ALU.mult, ALU.mult),
                                     reads=["qc", "rep"], writes=["qs"])
                            P.op("pool", lambda h, wr=wr, t0=t0: h.tensor_tensor(ks[:], kc[:, :, t0:t0 + 128], wr.unsqueeze(1).to_broadcast([128, 2, 128]), ALU.mult),
                                 reads=["kc", "rep"], writes=["ks"])
                            yield
                            if full:
                                for dc in range(2):
                                    P.op("pe", lambda h, dc=dc: h.matmul(p_st[:], ks[:, dc, :], qs[:, dc, :], start=(dc == 0), stop=(dc == 1)), reads=["ks", "qs"], writes=["pmisc"])
                                P.op("dve", lambda h, d=d: h.tensor_tensor(ST[:], p_st[:], mk[:, d, :], ALU.mult), reads=["pmisc", "mk"], writes=["ST"])
                            yield
                            for dc in range(2):
                                P.op("pe", lambda h, dc=dc: h.transpose(p_kt[:, dc * 128:(dc + 1) * 128], ks[:, dc, :], K.ident_b[:]), reads=["ks", "ident_b"], writes=["pbf"])
                            P.op("act", lambda h: h.copy(kst[:], p_kt[:]), reads=["pbf"], writes=["kst"])
                            yield
                            if full:
                                P.op("pe", lambda h, c=c: h.matmul(p_num[:], ST[:], vx[:, c, 0:512], start=True, stop=False), reads=["ST", "vx"], writes=["p_num"])
                                for dc in range(2):
                                    P.op("pe", lambda h, dc=dc: h.matmul(p_num[:], qs[:, dc, :], Cb[:, dc, :], start=False, stop=(dc == 1)), reads=["qs", "Cb"], writes=["p_num"])
                                P.op("pe", lambda h, c=c: h.matmul(p_den[:, 0:1], ST[:], vx[:, c, 512:513], start=True, stop=False), reads=["ST", "vx"], writes=["pmisc"])
                                for dc in range(2):
                                    P.op("pe", lambda h, dc=dc: h.matmul(p_den[:, 0:1], qs[:, dc, :], nb[:, dc:dc + 1], start=False, stop=(dc == 1)), reads=["qs", "nb"], writes=["pmisc"])
                                P.op("act", lambda h: h.activation(sm[:, 5:6], p_den[:, 0:1], AF.Abs), reads=["pmisc"], writes=["sm"])
                                yield
                                P.op("dve", lambda h, d=d, c=c, hd=hd: h.tensor_tensor(sm[:, 0:1], sm[:, 5:6], Ecol[:, d, c, hd:hd + 1], ALU.max), reads=["sm", "Ecol"], writes=["sm"])
                                P.op("dve", lambda h: h.reciprocal(sm[:, 1:2], sm[:, 0:1]), reads=["sm"], writes=["sm"])
                                P.op("act", lambda h: h.activation(hh[:], p_num[:], AF.Copy, scale=sm[:, 1:2]), reads=["p_num", "sm"], writes=["hh"])
                                yield
                            for dc in range(2):
                                P.op("pe", lambda h, dc=dc, c=c: h.matmul(p_dc[dc][:], kst[:, dc * 128:(dc + 1) * 128], vx[:, c, 0:512], start=True, stop=True), reads=["kst", "vx"], writes=["p_dc%d" % dc])
                                P.op("pe", lambda h, dc=dc, c=c: h.matmul(p_dn[:, dc:dc + 1], kst[:, dc * 128:(dc + 1) * 128], vx[:, c, 512:513], start=True, stop=True), reads=["kst", "vx"], writes=["pmisc"])
                            pd = prev_dec if prev_dec is not None else 1.0
                            yield
                            for dc in range(2):
                                P.op("dve", lambda h, dc=dc, pd=pd: h.scalar_tensor_tensor(Pst[:, dc, :], Pst[:, dc, :], pd, p_dc[dc][:], ALU.mult, ALU.add), reads=["Pst", "rep", "p_dc%d" % dc], writes=["Pst"])
                                P.op("pool", lambda h, dc=dc, dec=dec: h.tensor_scalar(Cb[:, dc, :], Pst[:, dc, :], dec, None, ALU.mult), reads=["Pst", "rep"], writes=["Cb"])
                            P.op("dve", lambda h, pd=pd: h.scalar_tensor_tensor(Pn[:], Pn[:], pd, p_dn[:, 0:2], ALU.mult, ALU.add), reads=["Pn", "rep", "pmisc"], writes=["Pn"])
                            P.op("pool", lambda h, dec=dec: h.tensor_scalar(nb[:], Pn[:], dec, None, ALU.mult), reads=["Pn", "rep"], writes=["nb"])
                            prev_dec = dec
                            yield
                            if not full:
                                continue
                            if d == 0:
                                P.op("sp", lambda h, t0=t0: h.dma_start(out=S.hfs[t0:t0 + 128, :], in_=hh[:]), reads=["hh"], writes=["hfs_d%d" % c], dma=True)
                            else:
                                P.op("sp", lambda h, t0=t0: h.dma_start(out=hfl[:], in_=S.hfs[t0:t0 + 128, :]), reads=["hfs_d%d" % c], writes=["hf"], dma=True)
                                P.op("sp", lambda h, t0=t0, hd=hd: h.dma_start(out=ot[:], in_=S.ao[t0:t0 + 128, hd * 512:(hd + 1) * 512]), writes=["ot"], dma=True)
                                P.op("sp", lambda h, t0=t0, hd=hd: h.dma_start(out=zt[:], in_=S.az[t0:t0 + 128, hd * 512:(hd + 1) * 512]), writes=["zt"], dma=True)
                                P.op("dve", lambda h: h.tensor_tensor(hh[:], hh[:], hfl[:], ALU.add), reads=["hh", "hf"], writes=["hh"])
                                yield
                                P.op("act", lambda h: h.activation(hsq[:], hh[:], AF.Square), reads=["hh"], writes=["hsq"])
                                P.op("dve", lambda h: h.tensor_reduce(sm[:, 2:3], hsq[:], AX.X, ALU.add), reads=["hsq"], writes=["sm"])
                                P.op("act", lambda h: h.activation(sm[:, 3:4], sm[:, 2:3], AF.Sqrt, bias=K.eps_t[:, 0:1], scale=1.0 / 512), reads=["sm", "eps_t"], writes=["sm"])
                                P.op("dve", lambda h: h.reciprocal(sm[:, 4:5], sm[:, 3:4]), reads=["sm"], writes=["sm"])
                                yield
                                P.op("dve", lambda h: h.scalar_tensor_tensor(hh[:], hh[:], sm[:, 4:5], ang[:], ALU.mult, ALU.mult), reads=["hh", "sm", "ang"], writes=["hh"])
                                P.op("act", lambda h: h.activation(of_[:], ot[:], AF.Sigmoid), reads=["ot"], writes=["of"])
                                P.op("act", lambda h: h.activation(zf[:], zt[:], AF.Silu), reads=["zt"], writes=["zf"])
                                P.op("pool", lambda h: h.tensor_tensor(of_[:], of_[:], zf[:], ALU.mult), reads=["of", "zf"], writes=["of"])
                                P.op("dve", lambda h: h.tensor_tensor(yb[:], hh[:], of_[:], ALU.mult), reads=["hh", "of"], writes=["yb"])
                                yield
                                for q in range(4):
                                    P.op("pe", lambda h, q=q: h.transpose(p_tr[:, q * 128:(q + 1) * 128], yb[:, q * 128:(q + 1) * 128], K.ident_b[:]), reads=["yb", "ident_b"], writes=["pbf"])
                                P.op("act", lambda h: h.copy(yTt[:], p_tr[:].rearrange("p (a b) -> p a b", a=4)), reads=["pbf"], writes=["yTt"])
                                P.op("sp", lambda h, t0=t0, hd=hd: h.dma_start(out=S.yT[0][hd * 512:(hd + 1) * 512, t0:t0 + 128].rearrange("(a p) t -> p a t", p=128), in_=yTt[:]),
                                     reads=["yTt"], dma=True)

            def gen_na():
                heads = range(1) if K.quick else range(NB)
                for hd in heads:
                    r0 = hd * 128
                    P.op("sp", lambda h, r0=r0: h.dma_start(out=nqn[:, 64:64 + TO], in_=S.bqT[r0:r0 + 128, 0:TO]), writes=["nqn"], dma=True)
                    P.op("sp", lambda h, r0=r0: h.dma_start(out=nkn[:], in_=S.bkT[r0:r0 + 128, 0:NKT]), writes=["nkn"], dma=True)
                    P.op("sp", lambda h, r0=r0: h.dma_start(out=nzs[:], in_=S.bzT[r0:r0 + 128, 0:TO]), writes=["nzs"], dma=True)
                    P.op("sp", lambda h, r0=r0: h.dma_start(out=ve[:], in_=S.bv[0:18 * 128, r0:r0 + 128].rearrange("(t p) d -> p t d", p=128)), writes=["ve"], dma=True)
                    P.op("sp", lambda h, r0=r0: h.dma_start(out=vo[:], in_=S.bv[64:64 + 17 * 128, r0:r0 + 128].rearrange("(t p) d -> p t d", p=128)), writes=["vo"], dma=True)
                    for (EB, ek, o0) in ((EBe, "EBe", 0), (EBo, "EBo", 1)):
                        for two in range(2):
                            P.op("sp", lambda h, EB=EB, o0=o0, two=two, hd=hd: h.dma_start(out=EB[two * 64:(two + 1) * 64, :, :], in_=I.B2[hd, o0 + two:o0 + two + 13:2].rearrange("r k q -> k r q")),
                                 writes=[ek], dma=True)
                        P.op("dve", lambda h, EB=EB: h.tensor_tensor(EB[:], EB[:], cm[:].unsqueeze(1).to_broadcast([128, 7, 64]), ALU.add), reads=[ek, "cm"], writes=[ek])
                        P.op("act", lambda h, EB=EB: h.activation(EB[:], EB[:], AF.Exp), reads=[ek], writes=[ek])
                    P.op("act", lambda h: h.activation(nzs[:], nzs[:], AF.Silu), reads=["nzs"], writes=["nzs"])
                    for (dst, dk, gi_, ntok, doff) in ((nqn, "nqn", 0, TO, 64), (nkn, "nkn", 1, NKT, 0)):
                        for t0 in range(0, ntok, 512):
                            w = min(512, ntok - t0)
                            P.op("act", lambda h, dst=dst, t0=t0, w=w, doff=doff: h.activation(sq[:, 0:w], dst[:, doff + t0:doff + t0 + w], AF.Square), reads=[dk], writes=["sq"])
                            P.op("pe", lambda h, w=w: h.matmul(p_ms[:, 0:w], onesS[:], sq[:, 0:w], start=True, stop=True), reads=["onesS", "sq"], writes=["pod"])
                            P.op("act", lambda h, w=w: h.activation(sd[:, 0:w], p_ms[:, 0:w], AF.Sqrt, bias=K.eps_t[:, 0:1]), reads=["pod", "eps_t"], writes=["sd"])
                            P.op("dve", lambda h, w=w: h.reciprocal(sd[:, 0:w], sd[:, 0:w]), reads=["sd"], writes=["sd"])
                            P.op("dve", lambda h, dst=dst, t0=t0, w=w, gi_=gi_, doff=doff: h.scalar_tensor_tensor(dst[:, doff + t0:doff + t0 + w], dst[:, doff + t0:doff + t0 + w], gq[:, gi_:gi_ + 1], sd[:, 0:w], ALU.mult, ALU.mult),
                                 reads=[dk, "gq", "sd"], writes=[dk])
                            yield
                    P.op("dve", lambda h: h.tensor_scalar(nqs[:], nqn[:, 64:64 + QW], ab[:, 0:1], None, ALU.mult), reads=["nqn", "ab"], writes=["nqs"])
                    P.op("dve", lambda h: h.scalar_tensor_tensor(nqs[:], nqn[:, 0:QW], ab[:, 1:2], nqs[:], ALU.mult, ALU.add), reads=["nqn", "ab", "nqs"], writes=["nqs"])
                    for r in range(NQR):
                        rs = max(r - 4, 0)
                        d0 = rs - r + 7
                        EB, ek, j0 = (EBe, "EBe", d0 // 2) if d0 % 2 == 0 else (EBo, "EBo", (d0 - 1) // 2)
                        b = r % 2
                        slot = r % 4
                        for i in range(4):
                            tk0 = (rs + 2 * i) * 64
                            P.op("pe", lambda h, b=b, i=i, tk0=tk0, r=r: h.matmul(p_s[b][:, i, :], nkn[:, tk0:tk0 + 128], nqs[:, r * 64:(r + 1) * 64], start=True, stop=True),
                                 reads=["nkn", "nqs"], writes=["p_s%d" % b])
                        yield
                        P.op("act", lambda h, b=b: h.activation(pe_[b][:], p_s[b][:], AF.Exp), reads=["p_s%d" % b], writes=["pe%d" % b])
                        P.op("dve", lambda h, b=b, EB=EB, j0=j0: h.tensor_tensor(pt_[b][:], pe_[b][:], EB[:, j0:j0 + 4, :], ALU.mult), reads=["pe%d" % b, ek], writes=["pt%d" % b])
                        yield
                        for i in range(4):
                            kr = rs + 2 * i
                            vt = ve[:, kr // 2, :] if kr % 2 == 0 else vo[:, (kr - 1) // 2, :]
                            P.op("pe", lambda h, b=b, i=i, vt=vt, slot=slot: h.matmul(p_o[:, slot * 64:(slot + 1) * 64], vt, pt_[b][:, i, :], start=(i == 0), stop=(i == 3)),
                                 reads=["ve", "vo", "pt%d" % b], writes=["pod"])
                        for i in range(4):
                            P.op("pe", lambda h, b=b, i=i, slot=slot: h.matmul(p_d[:, slot * 64:(slot + 1) * 64], ones1[:], pt_[b][:, i, :], start=(i == 0), stop=(i == 3)),
                                 reads=["ones1", "pt%d" % b], writes=["pod"])
                        yield
                        if slot == 3 or r == NQR - 1:
                            w = (slot + 1) * 64
                            c0 = (r // 4) * 256
                            P.op("dve", lambda h, w=w: h.reciprocal(rd[:, 0:w], p_d[:, 0:w]), reads=["pod"], writes=["rd"])
                            P.op("dve", lambda h, w=w, c0=c0: h.tensor_tensor(Ys[:, c0:c0 + w], p_o[:, 0:w], rd[:, 0:w], ALU.mult), reads=["pod", "rd"], writes=["Ys"])
                    for tb in range(0, TO, 512):
                        P.op("dve", lambda h, tb=tb: h.tensor_scalar(yo[:], Ys[:, tb:tb + 512], ab[:, 0:1], None, ALU.mult), reads=["Ys", "ab"], writes=["yo"])
                        P.op("dve", lambda h, tb=tb: h.scalar_tensor_tensor(yo[:], Ys[:, tb + 64:tb + 64 + 512], ab[:, 1:2], yo[:], ALU.mult, ALU.add), reads=["Ys", "ab", "yo"], writes=["yo"])
                        P.op("pool", lambda h, tb=tb: h.tensor_tensor(yob[:], yo[:], nzs[:, tb:tb + 512], ALU.mult), reads=["yo", "nzs"], writes=["yob"])
                        P.op("sp", lambda h, tb=tb, r0=r0: h.dma_start(out=S.yT[0][D + r0:D + r0 + 128, tb:tb + 512], in_=yob[:]), reads=["yob"], dma=True)
                        yield

            gens = [gen_ml(), gen_na()]
            while gens:
                for g in list(gens):
                    try:
                        next(g)
                    except StopIteration:
                        gens.remove(g)
            P.flush()
            P.exclusive = set()


def phase_outproj(K, layer, x_src, dst):
    nc, P, I, S = K.nc, K.P, K.I, K.S
    NKO = CW // 128
    with Alloc(nc) as A_:
        oy = A_.sb("oy", [128, NKO, 512], BF16)
        ow0 = A_.sb("ow0", [128, NKO, 256], BF16)
        ow1 = A_.sb("ow1", [128, NKO, 256], BF16)
        og = A_.sb("og", [128, D], F32)
        ox0 = A_.sb("ox0", [128, 256], F32)
        ox1 = A_.sb("ox1", [128, 256], F32)
        oo0 = A_.sb("oo0", [128, 256], F32)
        oo1 = A_.sb("oo1", [128, 256], F32)
        ops0 = A_.ps("ops0", [128, 256], F32)
        ops1 = A_.ps("ops1", [128, 256], F32)
        ow = [ow0, ow1]
        ox = [ox0, ox1]
        oo = [oo0, oo1]
        ops = [ops0, ops1]
        P.op("sp", lambda h: h.dma_start(out=og[:], in_=S.modrep[layer][2]), writes=["og"], dma=True)
        it = 0
        pj = 0
        ntb = 1 if K.quick else TO // 512
        for tb in range(ntb):
            for q in range(4):
                P.op("sp", lambda h, tb=tb, q=q: h.dma_start(out=oy[:, q * 16:(q + 1) * 16, :], in_=S.yT[layer][q * 2048:(q + 1) * 2048, tb * 512:(tb + 1) * 512].rearrange("(k p) t -> p k t", p=128)),
                     writes=["oy"], dma=True)
            for cb in range(D // 256):
                wb = ow[it % 2]
                wk = "ow%d" % (it % 2)
                it += 1
                src = I.w_out[layer][:, cb * 256:(cb + 1) * 256].rearrange("(k p) c -> p k c", p=128)
                for q in range(4):
                    P.op("pool", lambda h, wb=wb, src=src, q=q: h.dma_start(out=wb[:, q * 16:(q + 1) * 16, :], in_=src[:, q * 16:(q + 1) * 16, :]), writes=[wk], dma=True)
                for tt in range(4):
                    ps = ops[pj % 2]
                    pk = "ops%d" % (pj % 2)
                    xt = ox[pj % 2]
                    xk = "ox%d" % (pj % 2)
                    ot_ = oo[pj % 2]
                    okk = "oo%d" % (pj % 2)
                    pj += 1
                    tok = tb * 512 + tt * 128
                    for k in range(NKO):
                        P.op("pe", lambda h, ps=ps, wb=wb, k=k, tt=tt: h.matmul(ps[:], oy[:, k, tt * 128:(tt + 1) * 128], wb[:, k, :], start=(k == 0), stop=(k == NKO - 1)),
                             reads=["oy", wk], writes=[pk])
                    P.op("sp", lambda h, xt=xt, tok=tok, cb=cb: h.dma_start(out=xt[:], in_=x_src[tok:tok + 128, cb * 256:(cb + 1) * 256]), writes=[xk], dma=True)
                    P.op("dve", lambda h, ps=ps, ot_=ot_, cb=cb: h.tensor_tensor(ot_[:], ps[:], og[:, cb * 256:(cb + 1) * 256], ALU.mult), reads=[pk, "og"], writes=[okk])
                    P.op("pool", lambda h, ot_=ot_, xt=xt: h.tensor_tensor(ot_[:], ot_[:], xt[:], ALU.add), reads=[okk, xk], writes=[okk])
                    P.op("sp", lambda h, ot_=ot_, tok=tok, cb=cb: h.dma_start(out=dst[tok:tok + 128, cb * 256:(cb + 1) * 256], in_=ot_[:]), reads=[okk], dma=True)
        P.flush()


def phase_l1_inproj(K):
    nc, P, I, S = K.nc, K.P, K.I, K.S
    with nc.sbuf_tensor(U("hT1"), [128, NK, 2048], BF16) as hT:
        for ps_i in range(1):
            tok0 = ps_i * 2048
            norm_pass(K, S.x1[tok0:tok0 + 2048, :], 2048, 1, hT)
            with Alloc(nc) as A_:
                lb0 = A_.sb("lb0", [128, 512], BF16)
                lb1 = A_.sb("lb1", [128, 512], BF16)
                lb2 = A_.sb("lb2", [128, 512], BF16)
                lb3 = A_.sb("lb3", [128, 512], BF16)
                lf0 = A_.sb("lf0", [128, 512], F32)
                lf1 = A_.sb("lf1", [128, 512], F32)
                lf2 = A_.sb("lf2", [128, 512], F32)
                lf3 = A_.sb("lf3", [128, 512], F32)
                stb = Stager([lb0, lb1, lb2, lb3], "lb")
                stf = Stager([lf0, lf1, lf2, lf3], "lf")

                def ev_gelu(dst):
                    def f(ps, pk, c, tt, w):
                        st, sk = stb.next()
                        t1, k1 = stf.next()
                        P.op("act", lambda h: h.activation(t1[:], ps[:], AF.Square), reads=[pk], writes=[k1])
                        P.op("dve", lambda h: h.tensor_scalar(t1[:], t1[:], 0.044715, 1.0, ALU.mult, ALU.add), reads=[k1], writes=[k1])
                        P.op("dve", lambda h: h.tensor_tensor(t1[:], t1[:], ps[:], ALU.mult), reads=[k1, pk], writes=[k1])
                        P.op("act", lambda h: h.activation(t1[:], t1[:], AF.Sigmoid, scale=1.5957691216057308), reads=[k1], writes=[k1])
                        P.op("dve", lambda h: h.tensor_tensor(st[:], t1[:], ps[:], ALU.mult), reads=[k1, pk], writes=[sk])
                        P.op("sp", lambda h: h.dma_start(out=dst[tok0 + tt * 128:tok0 + (tt + 1) * 128, c:c + w], in_=st[:, 0:w]), reads=[sk], dma=True)
                    return f

                def ev_silu(dst):
                    def f(ps, pk, c, tt, w):
                        st, sk = stb.next()
                        P.op("act", lambda h: h.activation(st[:], ps[:], AF.Silu), reads=[pk], writes=[sk])
                        P.op("sp", lambda h: h.dma_start(out=dst[tok0 + tt * 128:tok0 + (tt + 1) * 128, c:c + w], in_=st[:, 0:w]), reads=[sk], dma=True)
                    return f

                W = I.w_in1
                nt = range(1) if K.quick else range(16)
                jobs = [
                    dict(mode="TM", w=W[:, 0:CW], ncols=CW, tiles=nt, evac=ev_gelu(S.gu)),
                    dict(mode="TM", w=W[:, CW:2 * CW], ncols=CW, tiles=nt, evac=ev_gelu(S.gv)),
                    dict(mode="TM", w=W[:, 2 * CW:3 * CW], ncols=CW, tiles=nt, evac=ev_silu(S.sz)),
                ]
                inproj(K, hT, jobs)


def phase_gmlp(K):
    nc, P, I, S = K.nc, K.P, K.I, K.S
    with Alloc(nc) as A_:
        vg = A_.sb("vg", [128, CW], F32)
        wsf = A_.sb("wsf", [128, 8, 128], F32)
        wsb = A_.sb("wsb", [128, 8, 128], BF16)
        bsf = A_.sb("bsf", [128, 8], F32)
        gvt = A_.sb("gvt", [128, CW], BF16)
        gvs = A_.sb("gvs", [128, CW], BF16)
        gvn = A_.sb("gvn", [128, CW], BF16)
        gut = A_.sb("gut", [128, CW], BF16)
        szt = A_.sb("szt", [128, CW], BF16)
        ych = A_.sb("ych", [128, CW], BF16)
        yTc = A_.sb("yTc", [128, 64, 128], BF16)
        gt0 = A_.sb("gt0", [128, 512], F32)
        gt1 = A_.sb("gt1", [128, 512], F32)
        gsm = A_.sb("gsm", [128, 4], F32)
        gp0 = A_.ps("gp0", [128, 512], F32)
        gp1 = A_.ps("gp1", [128, 512], F32)
        gtr0 = A_.ps("gtr0", [128, 512], BF16)
        gtr1 = A_.ps("gtr1", [128, 512], BF16)
        gp = [gp0, gp1]
        gt = [gt0, gt1]
        gtr = [gtr0, gtr1]
        P.op("sp", lambda h: h.dma_start(out=vg[:], in_=I.vg_rep), writes=["vg"], dma=True)
        P.op("sp", lambda h: h.dma_start(out=wsf[:], in_=I.WsT.rearrange("g s t -> s g t")), writes=["wsf"], dma=True)
        P.op("dve", lambda h: h.tensor_copy(wsb[:], wsf[:]), reads=["wsf"], writes=["wsb"])
        P.op("sp", lambda h: h.dma_start(out=bsf[:], in_=I.bs_fm), writes=["bsf"], dma=True)
        nchunks = 1 if K.quick else NOWN
        pj = 0
        for c in range(nchunks):
            t0 = c * 128
            P.op("sp", lambda h, t0=t0: h.dma_start(out=gvt[:], in_=S.gv[t0:t0 + 128, :]), writes=["gvt"], dma=True)
            P.op("sp", lambda h, t0=t0: h.dma_start(out=gut[:], in_=S.gu[t0:t0 + 128, :]), writes=["gut"], dma=True)
            P.op("sp", lambda h, t0=t0: h.dma_start(out=szt[:], in_=S.sz[t0:t0 + 128, :]), writes=["szt"], dma=True)
            P.op("act", lambda h: h.activation(gvs[:], gvt[:], AF.Square), reads=["gvt"], writes=["gvs"])
            P.op("dve", lambda h: h.tensor_reduce(gsm[:, 0:1], gvs[:], AX.X, ALU.add), reads=["gvs"], writes=["gsm"])
            P.op("act", lambda h: h.activation(gsm[:, 1:2], gsm[:, 0:1], AF.Sqrt, bias=K.eps_t[:, 0:1], scale=1.0 / CW), reads=["gsm", "eps_t"], writes=["gsm"])
            P.op("dve", lambda h: h.reciprocal(gsm[:, 2:3], gsm[:, 1:2]), reads=["gsm"], writes=["gsm"])
            P.op("dve", lambda h: h.scalar_tensor_tensor(gvn[:], gvt[:], gsm[:, 2:3], vg[:], ALU.mult, ALU.mult), reads=["gvt", "gsm", "vg"], writes=["gvn"])
            for g in range(8):
                for hh_ in range(2):
                    cs_ = g * 1024 + hh_ * 512
                    ps = gp[pj % 2]
                    pk = "gp%d" % (pj % 2)
                    tt_ = gt[pj % 2]
                    tk = "gt%d" % (pj % 2)
                    pj += 1
                    P.op("pe", lambda h, ps=ps, g=g, cs_=cs_: h.matmul(ps[:], wsb[:, g, :], gvn[:, cs_:cs_ + 512], start=True, stop=True), reads=["wsb", "gvn"], writes=[pk])
                    P.op("dve", lambda h, ps=ps, tt_=tt_, g=g, cs_=cs_: h.scalar_tensor_tensor(tt_[:], ps[:], bsf[:, g:g + 1], gut[:, cs_:cs_ + 512], ALU.add, ALU.mult), reads=[pk, "bsf", "gut"], writes=[tk])
                    P.op("pool", lambda h, tt_=tt_, cs_=cs_: h.tensor_tensor(ych[:, cs_:cs_ + 512], tt_[:], szt[:, cs_:cs_ + 512], ALU.mult), reads=[tk, "szt"], writes=["ych"])
            for k4 in range(0, 64, 4):
                tp = gtr[(k4 // 4) % 2]
                tk = "gtr%d" % ((k4 // 4) % 2)
                for q in range(4):
                    P.op("pe", lambda h, tp=tp, q=q, k4=k4: h.transpose(tp[:, q * 128:(q + 1) * 128], ych[:, (k4 + q) * 128:(k4 + q + 1) * 128], K.ident_b[:]), reads=["ych", "ident_b"], writes=[tk])
                if (k4 // 4) % 2:
                    P.op("act", lambda h, tp=tp, k4=k4: h.copy(yTc[:, k4:k4 + 4, :], tp[:].rearrange("p (a b) -> p a b", a=4)), reads=[tk], writes=["yTc"])
                else:
                    P.op("dve", lambda h, tp=tp, k4=k4: h.tensor_copy(yTc[:, k4:k4 + 4, :], tp[:].rearrange("p (a b) -> p a b", a=4)), reads=[tk], writes=["yTc"])
            for q in range(4):
                P.op("sp", lambda h, t0=t0, q=q: h.dma_start(out=S.yT[1][q * 2048:(q + 1) * 2048, t0:t0 + 128].rearrange("(k p) t -> p k t", p=128), in_=yTc[:, q * 16:(q + 1) * 16, :]),
                     reads=["yTc"], dma=True)
        P.flush()


def prep_core(inp, b, hf):
    rev = hf == 1
    norm_g = [inp["norm_g0"], inp["norm_g1"]]
    ada_w = [inp["ada_w0"], inp["ada_w1"]]
    ada_b = [inp["ada_b0"], inp["ada_b1"]]
    w_out = [inp["w_out0"], inp["w_out1"]]
    m = {}
    xb = inp["x"][b]
    m["x"] = np.ascontiguousarray(xb[::-1] if rev else xb)
    m["c_fm"] = np.ascontiguousarray(inp["c"][b].reshape(NK, 128).T)
    for l in range(2):
        m["g_rep%d" % l] = np.ascontiguousarray(np.broadcast_to(norm_g[l][None, :], (128, D)))
        m["ada_w%d" % l] = ada_w[l]
        m["ada_b_rep%d" % l] = np.ascontiguousarray(np.broadcast_to(ada_b[l][None, :], (128, 3 * D)))
        m["w_out%d" % l] = w_out[l]
    m["w_in0"] = inp["w_in0"]
    m["w_in1"] = inp["w_in1"]
    m["ident"] = np.eye(128, dtype=np.float32)
    gperm = np.arange(32)
    if rev:
        gperm = np.concatenate([np.arange(16, 32), np.arange(0, 16)])
    m["w_gates"] = np.ascontiguousarray(inp["w_in0"][:, C_G:C_G + 32][:, gperm])
    m["gate_b"] = np.ascontiguousarray(inp["a_gate_b"][gperm].reshape(4, 8).T)
    cw = inp["a_conv_w"][::-1] if rev else inp["a_conv_w"]
    m["convw"] = np.ascontiguousarray(cw.T.reshape(32, 128, 3).transpose(1, 0, 2))
    m["ang_rep"] = np.ascontiguousarray(np.broadcast_to(inp["a_norm_g"].reshape(1, D), (128, D)))
    m["gqk"] = np.ascontiguousarray(np.stack([inp["b_q_gain"], inp["b_k_gain"]], axis=1))
    kc = np.arange(64)[:, None]
    qc = np.arange(64)[None, :]
    dci = np.clip(kc - qc + 15, 0, 30)
    rpb = inp["b_rpb"]
    if not rev:
        m["B2"] = np.ascontiguousarray(rpb[:, :, dci])
        cs = np.clip(qc - 8, 0, 48)
    else:
        dru = np.clip(13 - np.arange(15), 0, 14)
        m["B2"] = np.ascontiguousarray(rpb[:, dru][:, :, 30 - dci])
        cs = np.clip(qc - 7, 0, 48)
    cmask = np.where((kc >= cs) & (kc < cs + 16), 0.0, -30000.0).astype(np.float32)
    m["cmask"] = np.ascontiguousarray(np.concatenate([cmask, cmask], axis=0))
    s_ = np.arange(128)[:, None]
    t_ = np.arange(128)[None, :]
    m["masks"] = np.stack([(s_ <= t_), (s_ >= t_)]).astype(np.float32)
    m["ab"] = np.ascontiguousarray(np.broadcast_to(np.array([[0.0, 1.0]] if rev else [[1.0, 0.0]], np.float32), (128, 2)))
    m["vg_rep"] = np.ascontiguousarray(np.broadcast_to(inp["c_v_norm_g"][None, :], (128, CW)))
    ws = inp["c_w_s"]
    bs = inp["c_b_s"]
    if rev:
        ws = ws[:, ::-1, ::-1]
        bs = bs[:, ::-1]
    m["WsT"] = np.ascontiguousarray(ws.transpose(0, 2, 1))
    m["bs_fm"] = np.ascontiguousarray(bs.T)
    return m


def kernel(**inputs):
    inp = {k: np.asarray(v) for k, v in inputs.items()}
    nc = build()
    in_maps = [prep_core(inp, c // 2, c % 2) for c in range(8)]
    res = run_bass_kernel_spmd(nc, in_maps, core_ids=list(range(8)))
    out = np.empty((4, T, D), np.float32)
    for c in range(8):
        b, hf = c // 2, c % 2
        o = res.results[c]["out"]
        if hf == 0:
            out[b, :TO] = o
        else:
            out[b, TO:] = o[::-1]
    return out
```

```python
import numpy as np
from contextlib import ExitStack
import concourse.bass as bass
import concourse.mybir as mybir
from concourse.bass_utils import run_bass_kernel_spmd

F32 = mybir.dt.float32
BF16 = mybir.dt.bfloat16
AF = mybir.ActivationFunctionType
ALU = mybir.AluOpType
AX = mybir.AxisListType

D = 4096
TH = 2048
NK = 32
L0C = 32800
EPS = 1e-6
N_DMA_SEMS = 24
SAFE_SAME_ENGINE = True

C_AQ, C_AK, C_AV, C_AO, C_AZ, C_G, C_BQ, C_BK, C_BV, C_BZ = 0, 2048, 4096, 8192, 12288, 16384, 16416, 20512, 24608, 28704


class Prog:
    def __init__(self, nc):
        self.nc = nc
        self.eng = {"pe": nc.tensor, "act": nc.scalar, "dve": nc.vector, "pool": nc.gpsimd, "sp": nc.sync}
        self.esem = {e: nc.alloc_semaphore("es_" + e) for e in self.eng}
        self.ecnt = {e: 0 for e in self.eng}
        self.dsem = [nc.alloc_semaphore("ds%d" % i) for i in range(N_DMA_SEMS)]
        self.dcnt = [0] * N_DMA_SEMS
        self.dnext = 0
        self.seen = {e: {} for e in self.eng}
        self.ops = []
        self.last_writer = {}
        self.readers = {}
        self.barrier_set = []
        self.n_emitted = 0
        self.exclusive = set()

    def op(self, eng, fn, reads=(), writes=(), dma=False):
        idx = len(self.ops)
        deps = set()
        if self.exclusive:
            ex = [b for b in reads if b in self.exclusive]
            if ex:
                reads = [b for b in reads if b not in self.exclusive]
                writes = list(writes) + ex
        for b in reads:
            w = self.last_writer.get(b)
            if w is not None:
                deps.add(w)
        for b in writes:
            w = self.last_writer.get(b)
            if w is not None:
                deps.add(w)
            for r in self.readers.get(b, {}).values():
                deps.add(r)
        for b in reads:
            d = self.readers.setdefault(b, {})
            key = ("dma", idx) if dma else eng
            d[key] = idx
        for b in writes:
            self.last_writer[b] = idx
            self.readers[b] = {}
        deps.discard(idx)
        self.ops.append(dict(eng=eng, fn=fn, deps=deps, dma=dma, sig=None))
        return idx

    def flush(self):
        ops = self.ops
        need = [False] * len(ops)
        for o in ops:
            for d in o["deps"]:
                p = ops[d]
                if p["dma"]:
                    need[d] = True
                elif p["eng"] == o["eng"] and not o["dma"]:
                    if p["eng"] != "pe" and SAFE_SAME_ENGINE:
                        need[d] = True
                else:
                    need[d] = True
        last = {}
        for i, o in enumerate(ops):
            if o["dma"]:
                need[i] = True
            else:
                last[o["eng"]] = i
        for i in last.values():
            need[i] = True
        for i, o in enumerate(ops):
            e = o["eng"]
            h = self.eng[e]
            waits = list(self.barrier_set)
            for d in sorted(o["deps"]):
                p = ops[d]
                if not need[d]:
                    continue
                if (not p["dma"]) and p["eng"] == e and not o["dma"] and (e == "pe" or not SAFE_SAME_ENGINE):
                    continue
                waits.append(p["sig"])
            seen = self.seen[e]
            for (k, sem, val) in waits:
                if seen.get(k, 0) >= val:
                    continue
                h.wait_ge(sem, val)
                seen[k] = val
            ins = o["fn"](h)
            if need[i]:
                if o["dma"]:
                    s = self.dnext
                    self.dnext = (self.dnext + 1) % N_DMA_SEMS
                    self.dcnt[s] += 16
                    ins.then_inc(self.dsem[s], 16)
                    o["sig"] = (("d", s), self.dsem[s], self.dcnt[s])
                else:
                    self.ecnt[e] += 1
                    ins.then_inc(self.esem[e], 1)
                    o["sig"] = (("e", e), self.esem[e], self.ecnt[e])
            self.n_emitted += 1
        bs = {}
        for o in ops:
            if o["sig"] is not None:
                k, sem, val = o["sig"]
                if k not in bs or bs[k][2] < val:
                    bs[k] = (k, sem, val)
        for (k, sem, val) in self.barrier_set:
            if k not in bs or bs[k][2] < val:
                bs[k] = (k, sem, val)
        self.barrier_set = list(bs.values())
        self.ops = []
        self.last_writer = {}
        self.readers = {}

    def final_wait(self):
        for e, h in self.eng.items():
            seen = self.seen[e]
            for (k, sem, val) in self.barrier_set:
                if seen.get(k, 0) >= val:
                    continue
                h.wait_ge(sem, val)
                seen[k] = val


class Ctx:
    pass


class Alloc:
    def __init__(self, nc):
        self.nc = nc
        self.es = ExitStack()

    def __enter__(self):
        self.es.__enter__()
        return self

    def __exit__(self, *a):
        return self.es.__exit__(*a)

    def sb(self, name, shape, dt):
        return self.es.enter_context(self.nc.sbuf_tensor(U(name), shape, dt))

    def ps(self, name, shape, dt):
        return self.es.enter_context(self.nc.psum_tensor(U(name), shape, dt))


_UC = [0]


def U(name):
    _UC[0] += 1
    return "%s_%d" % (name, _UC[0])


T = 4096
TO = 2048
NOWN = 16
NCH = 32
HD = 8
NB = 32
CW = 8192


def build(taps=(), stop_after=None, quick=False):
    nc = bass.Bass("TRN2", target_bir_lowering=False)
    P = Prog(nc)
    K = Ctx()
    K.nc, K.P, K.taps, K.quick = nc, P, set(taps), quick

    order = ["mod", "in0", "na", "out0", "in1", "gmlp", "out1"]
    last = order.index(stop_after) if stop_after else len(order) - 1
    K.in_shapes = {}

    def din(name, shape, dt=F32, first="mod"):
        if order.index(first) > last:
            shape = [128, 128]
        K.in_shapes[name] = tuple(shape)
        return nc.dram_tensor(name, list(shape), dt, kind="ExternalInput").ap()

    def dscr(name, shape, dt=BF16):
        kind = "ExternalOutput" if name in K.taps else "Internal"
        return nc.dram_tensor(name, list(shape), dt, kind=kind).ap()

    I = Ctx()
    K.I = I
    I.x = din("x", [T, D])
    I.c_fm = din("c_fm", [128, NK])
    I.g_rep = [din("g_rep%d" % l, [128, D]) for l in range(2)]
    I.ada_w = [din("ada_w%d" % l, [D, 3 * D]) for l in range(2)]
    I.ada_b_rep = [din("ada_b_rep%d" % l, [128, 3 * D]) for l in range(2)]
    I.w_in0 = din("w_in0", [D, L0C], first="in0")
    I.ident = din("ident", [128, 128])
    I.w_gates = din("w_gates", [D, 32])
    I.ab = din("ab", [128, 2])
    I.convw = din("convw", [128, 32, 3])
    I.gate_b = din("gate_b", [8, 4])
    I.ang_rep = din("ang_rep", [128, D])
    I.gqk = din("gqk", [128, 2])
    I.B2 = din("B2", [NB, 15, 64, 64])
    I.cmask = din("cmask", [128, 64])
    I.masks = din("masks", [2, 128, 128])
    I.w_out = [din("w_out%d" % l, [CW, D], first=("out0", "out1")[l]) for l in range(2)]
    I.w_in1 = din("w_in1", [D, 3 * CW], first="in1")
    I.vg_rep = din("vg_rep", [128, CW])
    I.WsT = din("WsT", [8, 128, 128])
    I.bs_fm = din("bs_fm", [128, 8])
    K.out = nc.dram_tensor("out", [TO, D], F32, kind="ExternalOutput").ap()

    S = Ctx()
    K.S = S
    S.modrep = [dscr("modrep%d" % l, [3, 128, D], F32) for l in range(2)]
    S.aqT = dscr("aqT", [2048, T])
    S.akT = dscr("akT", [2048, T])
    S.av = dscr("av", [T, D])
    S.ao = dscr("ao", [T, D])
    S.az = dscr("az", [T, D])
    S.gT = dscr("gT", [32, T], F32)
    S.bqT = dscr("bqT", [D, T])
    S.bzT = dscr("bzT", [D, T])
    S.bkT = dscr("bkT", [D, T])
    S.bv = dscr("bv", [T, D])
    S.gsc = dscr("gsc", [2, 2, 8, T], F32)
    S.hfs = dscr("hfs", [TO, 512], F32)
    S.yT = [dscr("yT%d" % l, [CW, TO]) for l in range(2)]
    S.x1 = dscr("x1", [TO, D], F32)
    S.gu = dscr("gu", [TO, CW])
    S.gv = dscr("gv", [TO, CW])
    S.sz = dscr("sz", [TO, CW])

    with Alloc(nc) as A_:
        ident_f = A_.sb("ident_f", [128, 128], F32)
        ident_b = A_.sb("ident_b", [128, 128], BF16)
        eps_t = A_.sb("eps_t", [128, 1], F32)
        K.ident_f, K.ident_b, K.eps_t = ident_f, ident_b, eps_t
        P.op("sp", lambda h: h.dma_start(out=ident_f[:], in_=I.ident), writes=["ident_f"], dma=True)
        P.op("dve", lambda h: h.tensor_copy(ident_b[:], ident_f[:]), reads=["ident_f"], writes=["ident_b"])
        P.op("pool", lambda h: h.memset(eps_t[:], EPS), writes=["eps_t"])
        P.flush()
        stages = [
            ("mod", lambda: phase_mod(K)),
            ("in0", lambda: phase_l0_inproj(K)),
            ("na", lambda: phase_mixers(K)),
            ("out0", lambda: phase_outproj(K, 0, I.x, S.x1)),
            ("in1", lambda: phase_l1_inproj(K)),
            ("gmlp", lambda: phase_gmlp(K)),
            ("out1", lambda: phase_outproj(K, 1, S.x1, K.out)),
        ]
        for name, fn in stages:
            fn()
            if stop_after == name:
                break
    K.P.final_wait()
    nc._in_shapes = K.in_shapes
    return nc


def phase_mod(K):
    nc, P, I, S = K.nc, K.P, K.I, K.S
    with Alloc(nc) as A_:
        cf = A_.sb("cf", [128, NK], F32)
        cs = A_.sb("cs", [128, NK], F32)
        csb = A_.sb("csb", [128, NK, 128], BF16)
        mw0 = A_.sb("mw0", [128, NK, 512], BF16)
        mw1 = A_.sb("mw1", [128, NK, 512], BF16)
        mb = A_.sb("mb", [128, 512], F32)
        mg = A_.sb("mg", [128, 512], F32)
        mo0 = A_.sb("mo0", [128, 512], F32)
        mo1 = A_.sb("mo1", [128, 512], F32)
        mps0 = A_.ps("mps0", [128, 512], F32)
        mps1 = A_.ps("mps1", [128, 512], F32)
        mw = [mw0, mw1]
        mo = [mo0, mo1]
        mps = [mps0, mps1]
        P.op("sp", lambda h: h.dma_start(out=cf[:], in_=I.c_fm), writes=["cf"], dma=True)
        P.op("act", lambda h: h.activation(cs[:], cf[:], AF.Silu), reads=["cf"], writes=["cs"])
        P.op("dve", lambda h: h.tensor_copy(csb[:], cs[:].unsqueeze(2).to_broadcast([128, NK, 128])), reads=["cs"], writes=["csb"])
        it = 0
        for l in range(2):
            for n in range(24):
                wb = mw[it % 2]
                wk = "mw%d" % (it % 2)
                src = I.ada_w[l][:, n * 512:(n + 1) * 512].rearrange("(k p) c -> p k c", p=128)
                for q in range(4):
                    P.op("pool", lambda h, wb=wb, src=src, q=q: h.dma_start(out=wb[:, q * 8:(q + 1) * 8, :], in_=src[:, q * 8:(q + 1) * 8, :]),
                         writes=[wk], dma=True)
                ps = mps[it % 2]
                pk = "mps%d" % (it % 2)
                for k in range(NK):
                    P.op("pe", lambda h, ps=ps, wb=wb, k=k: h.matmul(ps[:], csb[:, k, :], wb[:, k, :], start=(k == 0), stop=(k == NK - 1)),
                         reads=[wk, "csb"], writes=[pk])
                P.op("sp", lambda h, l=l, n=n: h.dma_start(out=mb[:], in_=I.ada_b_rep[l][:, n * 512:(n + 1) * 512]), writes=["mb"], dma=True)
                o = mo[it % 2]
                ok = "mo%d" % (it % 2)
                which = n // 8
                cb = (n % 8) * 512
                P.op("dve", lambda h, ps=ps, o=o: h.tensor_tensor(o[:], ps[:], mb[:], ALU.add), reads=[pk, "mb"], writes=[ok])
                if which == 1:
                    P.op("sp", lambda h, l=l, cb=cb: h.dma_start(out=mg[:], in_=I.g_rep[l][:, cb:cb + 512]), writes=["mg"], dma=True)
                    P.op("dve", lambda h, o=o: h.scalar_tensor_tensor(o[:], o[:], 1.0, mg[:], ALU.add, ALU.mult), reads=[ok, "mg"], writes=[ok])
                P.op("sp", lambda h, o=o, l=l, which=which, cb=cb: h.dma_start(out=S.modrep[l][which, :, cb:cb + 512], in_=o[:]),
                     reads=[ok], dma=True)
                it += 1
        P.flush()


def norm_pass(K, x_dram, ntok, layer, hT):
    nc, P, S = K.nc, K.P, K.S
    ntile = ntok // 128
    with Alloc(nc) as A_:
        nG = A_.sb("nG", [128, D], F32)
        nS = A_.sb("nS", [128, D], F32)
        nx = A_.sb("nx", [128, D], F32)
        nsq = A_.sb("nsq", [128, D], BF16)
        nh = A_.sb("nh", [128, D], BF16)
        nss = A_.sb("nss", [128, 4], F32)
        ntp0 = A_.ps("ntp0", [128, 512], BF16)
        ntp1 = A_.ps("ntp1", [128, 512], BF16)
        ntp = [ntp0, ntp1]
        eps_t = K.eps_t
        P.op("sp", lambda h: h.dma_start(out=nS[:], in_=S.modrep[layer][0]), writes=["nS"], dma=True)
        P.op("sp", lambda h: h.dma_start(out=nG[:], in_=S.modrep[layer][1]), writes=["nG"], dma=True)
        j = 0
        for t in range(ntile):
            P.op("sp", lambda h, t=t: h.dma_start(out=nx[:], in_=x_dram[t * 128:(t + 1) * 128, :]), writes=["nx"], dma=True)
            P.op("act", lambda h: h.activation(nsq[:], nx[:], AF.Square), reads=["nx"], writes=["nsq"])
            P.op("dve", lambda h: h.tensor_reduce(nss[:, 0:1], nsq[:], AX.X, ALU.add), reads=["nsq"], writes=["nss"])
            P.op("act", lambda h: h.activation(nss[:, 1:2], nss[:, 0:1], AF.Sqrt, bias=eps_t[:, 0:1], scale=1.0 / D), reads=["nss", "eps_t"], writes=["nss"])
            P.op("dve", lambda h: h.reciprocal(nss[:, 2:3], nss[:, 1:2]), reads=["nss"], writes=["nss"])
            P.op("dve", lambda h: h.scalar_tensor_tensor(nx[:], nx[:], nss[:, 2:3], nG[:], ALU.mult, ALU.mult), reads=["nx", "nss", "nG"], writes=["nx"])
            P.op("pool", lambda h: h.tensor_tensor(nh[:], nx[:], nS[:], ALU.add), reads=["nx", "nS"], writes=["nh"])
            for kk in range(0, NK, 4):
                tp = ntp[j % 2]
                tk = "ntp%d" % (j % 2)
                for q in range(4):
                    P.op("pe", lambda h, tp=tp, q=q, kk=kk: h.transpose(tp[:, q * 128:(q + 1) * 128], nh[:, (kk + q) * 128:(kk + q + 1) * 128], K.ident_b[:]),
                         reads=["nh", "ident_b"], writes=[tk])
                if j % 2:
                    P.op("act", lambda h, tp=tp, kk=kk, t=t: h.copy(hT[:, kk:kk + 4, t * 128:(t + 1) * 128], tp[:].rearrange("p (a b) -> p a b", a=4)),
                         reads=[tk], writes=["hT"])
                else:
                    P.op("dve", lambda h, tp=tp, kk=kk, t=t: h.tensor_copy(hT[:, kk:kk + 4, t * 128:(t + 1) * 128], tp[:].rearrange("p (a b) -> p a b", a=4)),
                         reads=[tk], writes=["hT"])
                j += 1
        P.flush()


def inproj(K, hT, jobs):
    nc, P = K.nc, K.P
    with Alloc(nc) as A_:
        iw0 = A_.sb("iw0", [128, NK, 512], BF16)
        iw1 = A_.sb("iw1", [128, NK, 512], BF16)
        ips0 = A_.ps("ips0", [128, 512], F32)
        ips1 = A_.ps("ips1", [128, 512], F32)
        ips2 = A_.ps("ips2", [128, 512], F32)
        ips3 = A_.ps("ips3", [128, 512], F32)
        iw = [iw0, iw1]
        ips = [ips0, ips1, ips2, ips3]
        it = 0
        pj = 0
        for job in jobs:
            ncols = job["ncols"]
            for c0 in range(0, ncols, 512):
                cw = min(512, ncols - c0)
                wb = iw[it % 2]
                wk = "iw%d" % (it % 2)
                it += 1
                src = job["w"][:, c0:c0 + cw].rearrange("(k p) c -> p k c", p=128)
                for q in range(4):
                    P.op("pool", lambda h, wb=wb, src=src, q=q, cw=cw: h.dma_start(out=wb[:, q * 8:(q + 1) * 8, 0:cw], in_=src[:, q * 8:(q + 1) * 8, :]),
                         writes=[wk], dma=True)
                if job["mode"] == "FM":
                    for cc in range(0, cw, 128):
                        m = min(128, cw - cc)
                        for tt in job["tiles"]:
                            ps = ips[pj % 4]
                            pk = "ips%d" % (pj % 4)
                            pj += 1
                            for k in range(NK):
                                P.op("pe", lambda h, ps=ps, wb=wb, k=k, cc=cc, m=m, tt=tt: h.matmul(ps[0:m, :], wb[:, k, cc:cc + m], hT[:, k, tt * 512:(tt + 1) * 512], start=(k == 0), stop=(k == NK - 1)),
                                     reads=[wk, "hT"], writes=[pk])
                            job["evac"](ps, pk, c0 + cc, tt, m)
                else:
                    for tt in job["tiles"]:
                        ps = ips[pj % 4]
                        pk = "ips%d" % (pj % 4)
                        pj += 1
                        for k in range(NK):
                            P.op("pe", lambda h, ps=ps, wb=wb, k=k, cw=cw, tt=tt: h.matmul(ps[:, 0:cw], hT[:, k, tt * 128:(tt + 1) * 128], wb[:, k, 0:cw], start=(k == 0), stop=(k == NK - 1)),
                                 reads=[wk, "hT"], writes=[pk])
                        job["evac"](ps, pk, c0, tt, cw)
        P.flush()


class Stager:
    def __init__(self, tiles, tag):
        self.tiles, self.tag, self.i = tiles, tag, 0

    def next(self):
        n = len(self.tiles)
        t = self.tiles[self.i % n]
        k = "%s%d" % (self.tag, self.i % n)
        self.i += 1
        return t, k


def phase_l0_inproj(K):
    nc, P, I, S = K.nc, K.P, K.I, K.S
    with nc.sbuf_tensor(U("hT"), [128, NK, 2048], BF16) as hT:
        for ps_i in range(2):
            tok0 = ps_i * 2048
            norm_pass(K, I.x[tok0:tok0 + 2048, :], 2048, 0, hT)
            with Alloc(nc) as A_:
                sb0 = A_.sb("sb0", [128, 512], BF16)
                sb1 = A_.sb("sb1", [128, 512], BF16)
                sb2 = A_.sb("sb2", [128, 512], BF16)
                sb3 = A_.sb("sb3", [128, 512], BF16)
                sf0 = A_.sb("sf0", [128, 512], F32)
                sf1 = A_.sb("sf1", [128, 512], F32)
                stb = Stager([sb0, sb1, sb2, sb3], "sb")
                stf = Stager([sf0, sf1], "sf")
                cnt = [0]

                def ev_fm(dst, f32=False):
                    def f(ps, pk, c, tt, m):
                        st, sk = (stf if f32 else stb).next()
                        cnt[0] += 1
                        if cnt[0] % 2 and not f32:
                            P.op("act", lambda h: h.copy(st[0:m, :], ps[0:m, :]), reads=[pk], writes=[sk])
                        else:
                            P.op("dve", lambda h: h.tensor_copy(st[0:m, :], ps[0:m, :]), reads=[pk], writes=[sk])
                        P.op("sp", lambda h: h.dma_start(out=dst[c:c + m, tok0 + tt * 512:tok0 + (tt + 1) * 512], in_=st[0:m, :]), reads=[sk], dma=True)
                    return f

                def ev_tm(dst):
                    def f(ps, pk, c, tt, w):
                        st, sk = stb.next()
                        cnt[0] += 1
                        if cnt[0] % 2:
                            P.op("act", lambda h: h.copy(st[:, 0:w], ps[:, 0:w]), reads=[pk], writes=[sk])
                        else:
                            P.op("dve", lambda h: h.tensor_copy(st[:, 0:w], ps[:, 0:w]), reads=[pk], writes=[sk])
                        P.op("sp", lambda h: h.dma_start(out=dst[tok0 + tt * 128:tok0 + (tt + 1) * 128, c:c + w], in_=st[:, 0:w]), reads=[sk], dma=True)
                    return f

                W = I.w_in0
                if ps_i == 0:
                    jobs = [
                        dict(mode="FM", w=I.w_gates, ncols=32, tiles=range(4), evac=ev_fm(S.gT, True)),
                        dict(mode="FM", w=W[:, C_AQ:C_AQ + 2048], ncols=2048, tiles=range(4), evac=ev_fm(S.aqT)),
                        dict(mode="FM", w=W[:, C_AK:C_AK + 2048], ncols=2048, tiles=range(4), evac=ev_fm(S.akT)),
                        dict(mode="TM", w=W[:, C_AV:C_AV + 4096], ncols=4096, tiles=range(16), evac=ev_tm(S.av)),
                        dict(mode="TM", w=W[:, C_AO:C_AO + 4096], ncols=4096, tiles=range(16), evac=ev_tm(S.ao)),
                        dict(mode="TM", w=W[:, C_AZ:C_AZ + 4096], ncols=4096, tiles=range(16), evac=ev_tm(S.az)),
                        dict(mode="FM", w=W[:, C_BQ:C_BQ + 4096], ncols=4096, tiles=range(4), evac=ev_fm(S.bqT)),
                        dict(mode="FM", w=W[:, C_BK:C_BK + 4096], ncols=4096, tiles=range(4), evac=ev_fm(S.bkT)),
                        dict(mode="TM", w=W[:, C_BV:C_BV + 4096], ncols=4096, tiles=range(16), evac=ev_tm(S.bv)),
                        dict(mode="FM", w=W[:, C_BZ:C_BZ + 4096], ncols=4096, tiles=range(4), evac=ev_fm(S.bzT)),
                    ]
                else:
                    jobs = [
                        dict(mode="FM", w=I.w_gates, ncols=32, tiles=range(4), evac=ev_fm(S.gT, True)),
                        dict(mode="FM", w=W[:, C_AQ:C_AQ + 2048], ncols=2048, tiles=range(1), evac=ev_fm(S.aqT)),
                        dict(mode="FM", w=W[:, C_AK:C_AK + 2048], ncols=2048, tiles=range(4), evac=ev_fm(S.akT)),
                        dict(mode="TM", w=W[:, C_AV:C_AV + 4096], ncols=4096, tiles=range(16), evac=ev_tm(S.av)),
                        dict(mode="FM", w=W[:, C_BK:C_BK + 4096], ncols=4096, tiles=range(1), evac=ev_fm(S.bkT)),
                        dict(mode="TM", w=W[:, C_BV:C_BV + 4096], ncols=4096, tiles=range(4), evac=ev_tm(S.bv)),
                    ]
                inproj(K, hT, jobs)


NQR = 33
NKT = TO + 256


def phase_mixers(K):
    nc, P, I, S = K.nc, K.P, K.I, K.S
    heads = range(1) if K.quick else range(HD)
    with nc.sbuf_tensor(U("Ecol"), [128, 2, NCH, 8], F32) as Ecol:
        with Alloc(nc) as A_:
            gb = A_.sb("gb", [8, 4], F32)
            gi = A_.sb("gi", [8, T], F32)
            gf = A_.sb("gf", [8, T], F32)
            gG = A_.sb("gG", [8, T], F32)
            gbt = A_.sb("gbt", [8, T], F32)
            gM = A_.sb("gM", [8, T], F32)
            gMp = A_.sb("gMp", [8, T], F32)
            gz = A_.sb("gz", [8, T], F32)
            go1 = A_.sb("go1", [8, T], F32)
            go2 = A_.sb("go2", [8, T], F32)
            gE = A_.sb("gE", [8, T], F32)
            gps = A_.ps("gps", [128, 512], F32)
            P.op("sp", lambda h: h.dma_start(out=gb[:], in_=I.gate_b), writes=["gb"], dma=True)
            P.op("pool", lambda h: h.memset(gz[:], 0.0), writes=["gz"])
            for d in range(2):
                rv = (lambda ap: ap[:, ::-1]) if d == 1 else (lambda ap: ap)
                P.op("sp", lambda h, d=d: h.dma_start(out=gi[:], in_=S.gT[16 * d:16 * d + 8, :]), writes=["gi"], dma=True)
                P.op("sp", lambda h, d=d: h.dma_start(out=gf[:], in_=S.gT[16 * d + 8:16 * d + 16, :]), writes=["gf"], dma=True)
                P.op("dve", lambda h, d=d: h.tensor_scalar(gi[:], gi[:], gb[:, 2 * d:2 * d + 1], None, ALU.add), reads=["gi", "gb"], writes=["gi"])
                P.op("dve", lambda h, d=d: h.tensor_scalar(gf[:], gf[:], gb[:, 2 * d + 1:2 * d + 2], None, ALU.add), reads=["gf", "gb"], writes=["gf"])
                P.op("act", lambda h: h.activation(gf[:], gf[:], AF.Exp, scale=-1.0), reads=["gf"], writes=["gf"])
                P.op("act", lambda h: h.activation(gf[:], gf[:], AF.Ln, bias=1.0), reads=["gf"], writes=["gf"])
                P.op("dve", lambda h: h.tensor_scalar(gf[:], gf[:], -1.0, None, ALU.mult), reads=["gf"], writes=["gf"])
                P.op("dve", lambda h, rv=rv: h.tensor_tensor_scan(rv(gG[:, :]), rv(gf[:, :]), rv(gz[:, :]), 0.0, ALU.add, ALU.add), reads=["gf", "gz"], writes=["gG"])
                P.op("dve", lambda h: h.tensor_tensor(gbt[:], gi[:], gG[:], ALU.subtract), reads=["gi", "gG"], writes=["gbt"])
                P.op("dve", lambda h, rv=rv: h.tensor_tensor_scan(rv(gM[:, :]), rv(gbt[:, :]), rv(gbt[:, :]), 0.0, ALU.max, ALU.max), reads=["gbt"], writes=["gM"])
                M3 = gM[:, :].rearrange("p (c t) -> p c t", t=128)
                Mp3 = gMp[:, :].rearrange("p (c t) -> p c t", t=128)
                if d == 0:
                    P.op("pool", lambda h, Mp3=Mp3: h.memset(Mp3[:, 0:1, :], 0.0), writes=["gMp"])
                    P.op("dve", lambda h, M3=M3, Mp3=Mp3: h.tensor_copy(Mp3[:, 1:NCH, :], M3[:, 0:NCH - 1, 127:128].to_broadcast([8, NCH - 1, 128])), reads=["gM"], writes=["gMp"])
                else:
                    P.op("pool", lambda h, Mp3=Mp3: h.memset(Mp3[:, NCH - 1:NCH, :], 0.0), writes=["gMp"])
                    P.op("dve", lambda h, M3=M3, Mp3=Mp3: h.tensor_copy(Mp3[:, 0:NCH - 1, :], M3[:, 1:NCH, 0:1].to_broadcast([8, NCH - 1, 128])), reads=["gM"], writes=["gMp"])
                P.op("dve", lambda h: h.tensor_tensor(go1[:], gMp[:], gM[:], ALU.subtract), reads=["gMp", "gM"], writes=["go1"])
                P.op("act", lambda h: h.activation(go1[:], go1[:], AF.Exp), reads=["go1"], writes=["go1"])
                P.op("sp", lambda h, d=d: h.dma_start(out=S.gsc[d, 0], in_=go1[:]), reads=["go1"], dma=True)
                P.op("dve", lambda h: h.tensor_tensor(go2[:], gbt[:], gMp[:], ALU.subtract), reads=["gbt", "gMp"], writes=["go2"])
                P.op("act", lambda h: h.activation(go2[:], go2[:], AF.Exp), reads=["go2"], writes=["go2"])
                P.op("sp", lambda h, d=d: h.dma_start(out=S.gsc[d, 1], in_=go2[:]), reads=["go2"], dma=True)
                P.op("dve", lambda h: h.tensor_tensor(gE[:], gG[:], gM[:], ALU.add), reads=["gG", "gM"], writes=["gE"])
                P.op("act", lambda h: h.activation(gE[:], gE[:], AF.Exp, scale=-1.0), reads=["gE"], writes=["gE"])
                for c8 in range(0, NCH, 8):
                    for cc in range(8):
                        c = c8 + cc
                        P.op("pe", lambda h, c=c, cc=cc: h.transpose(gps[:, cc * 8:(cc + 1) * 8], gE[:, c * 128:(c + 1) * 128], K.ident_f[0:8, 0:8]),
                             reads=["gE", "ident_f"], writes=["gps"])
                    P.op("dve", lambda h, d=d, c8=c8: h.tensor_copy(Ecol[:, d, c8:c8 + 8, :], gps[:, 0:64].rearrange("p (c h) -> p c h", h=8)), reads=["gps"], writes=["Ecol"])
            P.flush()

        with Alloc(nc) as A_:
            qraw = A_.sb("qraw", [128, 2, TO + 514], BF16)
            kraw = A_.sb("kraw", [128, 2, T + 2], BF16)
            cv = A_.sb("cv", [128, 1024], F32)
            qc = A_.sb("qc", [128, 2, TO], BF16)
            kc = A_.sb("kc", [128, 2, T], BF16)
            vx = A_.sb("vx", [128, NCH, 516], BF16)
            rep = A_.sb("rep", [128, 2, 2, T], BF16)
            cw = A_.sb("cw", [128, 32, 3], F32)
            ang = A_.sb("ang", [128, 512], F32)
            mk = A_.sb("mk", [128, 2, 128], F32)
            Pst = A_.sb("Pst", [128, 2, 512], F32)
            Pn = A_.sb("Pn", [128, 2], F32)
            Cb = A_.sb("Cb", [128, 2, 512], BF16)
            nb = A_.sb("nb", [128, 2], BF16)
            qs = A_.sb("qs", [128, 2, 128], BF16)
            ks = A_.sb("ks", [128, 2, 128], BF16)
            ST = A_.sb("ST", [128, 128], BF16)
            kst = A_.sb("kst", [128, 256], BF16)
            sm = A_.sb("sm", [128, 8], F32)
            hh = A_.sb("hh", [128, 512], F32)
            hfl = A_.sb("hf", [128, 512], F32)
            hsq = A_.sb("hsq", [128, 512], BF16)
            ot = A_.sb("ot", [128, 512], BF16)
            zt = A_.sb("zt", [128, 512], BF16)
            of_ = A_.sb("of", [128, 512], F32)
            zf = A_.sb("zf", [128, 512], F32)
            yb = A_.sb("yb", [128, 512], BF16)
            yTt = A_.sb("yT", [128, 4, 128], BF16)
            p_num = A_.ps("p_num", [128, 512], F32)
            p_dc0 = A_.ps("p_dc0", [128, 512], F32)
            p_dc1 = A_.ps("p_dc1", [128, 512], F32)
            p_dc = [p_dc0, p_dc1]
            P.op("sp", lambda h: h.dma_start(out=cw[:], in_=I.convw), writes=["cw"], dma=True)
            P.op("sp", lambda h: h.dma_start(out=mk[:], in_=I.masks.rearrange("d p t -> p d t")), writes=["mk"], dma=True)
            P.exclusive = {"pmisc", "pbf", "pod"}
            pmisc = A_.ps("pmisc", [128, 512], F32)
            pbf = A_.ps("pbf", [128, 1024], BF16)
            p_st = pmisc[:, 0:128]
            p_den = pmisc[:, 128:136]
            p_dn = pmisc[:, 136:144]
            p_kt = pbf[:, 0:256]
            p_tr = pbf[:, 512:1024]
            QW = NQR * 64
            nqn = A_.sb("nqn", [128, 64 + TO + 64], BF16)
            nqs = A_.sb("nqs", [128, QW], BF16)
            nkn = A_.sb("nkn", [128, NKT], BF16)
            nzs = A_.sb("nzs", [128, TO], BF16)
            ve = A_.sb("ve", [128, 18, 128], BF16)
            vo = A_.sb("vo", [128, 17, 128], BF16)
            EBe = A_.sb("EBe", [128, 7, 64], F32)
            EBo = A_.sb("EBo", [128, 7, 64], F32)
            cm = A_.sb("cm", [128, 64], F32)
            gq = A_.sb("gq", [128, 2], F32)
            ab = A_.sb("ab", [128, 2], F32)
            onesS = A_.sb("onesS", [128, 128], BF16)
            ones1 = A_.sb("ones1", [128, 128], BF16)
            sq = A_.sb("sq", [128, 512], BF16)
            sd = A_.sb("sd", [128, 512], F32)
            pe0 = A_.sb("pe0", [128, 4, 64], F32)
            pe1 = A_.sb("pe1", [128, 4, 64], F32)
            pt0 = A_.sb("pt0", [128, 4, 64], BF16)
            pt1 = A_.sb("pt1", [128, 4, 64], BF16)
            rd = A_.sb("rd", [128, 512], F32)
            Ys = A_.sb("Ys", [128, QW], F32)
            yo = A_.sb("yo", [128, 512], F32)
            yob = A_.sb("yob", [128, 512], BF16)
            pod = A_.ps("pod", [128, 512], F32)
            p_o = pod[:, 0:256]
            p_d = pod[:, 256:512]
            p_ms = pod
            psb0 = A_.ps("psb0", [128, 8, 64], F32)
            psb1 = A_.ps("psb1", [128, 8, 64], F32)
            p_s = [psb0[:, 0:4, :], psb1[:, 0:4, :]]
            pe_ = [pe0, pe1]
            pt_ = [pt0, pt1]
            P.op("sp", lambda h: h.dma_start(out=cm[:], in_=I.cmask), writes=["cm"], dma=True)
            P.op("sp", lambda h: h.dma_start(out=gq[:], in_=I.gqk), writes=["gq"], dma=True)
            P.op("sp", lambda h: h.dma_start(out=ab[:], in_=I.ab), writes=["ab"], dma=True)
            P.op("dve", lambda h: h.tensor_scalar(gq[:, 0:1], gq[:, 0:1], 128.0 ** -0.5, None, ALU.mult), reads=["gq"], writes=["gq"])
            P.op("pool", lambda h: h.memset(onesS[:], 1.0 / 128.0), writes=["onesS"])
            P.op("pool", lambda h: h.memset(ones1[:], 1.0), writes=["ones1"])
            P.op("pool", lambda h: h.memset(nqn[:], 0.0), writes=["nqn"])

            def gen_ml():
                for hd in heads:
                    for (raw, rk, src) in ((qraw, "qraw", S.aqT), (kraw, "kraw", S.akT)):
                        P.op("pool", lambda h, raw=raw: h.memset(raw[:, :, 0:1], 0.0), writes=[rk])
                        if rk == "kraw":
                            P.op("pool", lambda h, raw=raw: h.memset(raw[:, :, T + 1:T + 2], 0.0), writes=[rk])
                        nv = (TO + 512) if rk == "qraw" else T
                        P.op("sp", lambda h, raw=raw, src=src, hd=hd, nv=nv: h.dma_start(out=raw[:, :, 1:nv + 1], in_=src[hd * 256:(hd + 1) * 256, 0:nv].rearrange("(c p) t -> p c t", p=128)),
                             writes=[rk], dma=True)
                    P.op("sp", lambda h, hd=hd: h.dma_start(out=vx[:, :, 0:512], in_=S.av[:, hd * 512:(hd + 1) * 512].rearrange("(c p) v -> p c v", p=128)),
                         writes=["vx"], dma=True)
                    P.op("pool", lambda h: h.memset(vx[:, :, 512:516], 1.0), writes=["vx"])
                    for d in range(2):
                        for j in range(2):
                            P.op("pool", lambda h, d=d, j=j, hd=hd: h.dma_start(out=rep[:, d, j, :], in_=S.gsc[d, j, hd:hd + 1, :].to_broadcast([128, T])),
                                 writes=["rep"], dma=True)
                    P.op("sp", lambda h, hd=hd: h.dma_start(out=ang[:], in_=I.ang_rep[:, hd * 512:(hd + 1) * 512]), writes=["ang"], dma=True)
                    for (raw, rk, dst, dk, cbase) in ((qraw, "qraw", qc, "qc", 0), (kraw, "kraw", kc, "kc", 16)):
                        for dc in range(2):
                            ci = cbase + hd * 2 + dc
                            Lc = TO if rk == "qraw" else T
                            for cb0 in range(0, Lc, 1024):
                                P.op("dve", lambda h, raw=raw, dc=dc, ci=ci, cb0=cb0: h.tensor_scalar(cv[:], raw[:, dc, cb0 + 1:cb0 + 1025], cw[:, ci, 1:2], None, ALU.mult), reads=[rk, "cw"], writes=["cv"])
                                P.op("dve", lambda h, raw=raw, dc=dc, ci=ci, cb0=cb0: h.scalar_tensor_tensor(cv[:], raw[:, dc, cb0:cb0 + 1024], cw[:, ci, 0:1], cv[:], ALU.mult, ALU.add), reads=[rk, "cw", "cv"], writes=["cv"])
                                P.op("dve", lambda h, raw=raw, dc=dc, ci=ci, cb0=cb0: h.scalar_tensor_tensor(cv[:], raw[:, dc, cb0 + 2:cb0 + 1026], cw[:, ci, 2:3], cv[:], ALU.mult, ALU.add), reads=[rk, "cw", "cv"], writes=["cv"])
                                P.op("act", lambda h, dst=dst, dc=dc, cb0=cb0: h.activation(dst[:, dc, cb0:cb0 + 1024], cv[:], AF.Silu), reads=["cv"], writes=[dk])
                                yield
                    for d in range(2):
                        order = range(NOWN) if d == 0 else range(NCH - 1, -1, -1)
                        P.op("pool", lambda h: h.memset(Pst[:], 0.0), writes=["Pst"])
                        P.op("pool", lambda h: h.memset(Pn[:], 0.0), writes=["Pn"])
                        P.op("pool", lambda h: h.memset(Cb[:], 0.0), writes=["Cb"])
                        P.op("pool", lambda h: h.memset(nb[:], 0.0), writes=["nb"])
                        prev_dec = None
                        for c in order:
                            t0 = c * 128
                            ir = rep[:, d, 0, t0:t0 + 128]
                            wr = rep[:, d, 1, t0:t0 + 128]
                            dec_col = (t0 + 127) if d == 0 else t0
                            dec = rep[:, d, 0, dec_col:dec_col + 1]
                            full = c < NOWN
                            if full:
                                P.op("dve", lambda h, ir=ir, t0=t0: h.scalar_tensor_tensor(qs[:], qc[:, :, t0:t0 + 128], 1.0 / 16.0, ir.unsqueeze(1).to_broadcast([128, 2, 128]), ALU.mult, ALU.mult),
                                     reads=["qc", "rep"], writes=["qs"])
                            P.op("pool", lambda h, wr=wr, t0=t0: h.tensor_tensor(ks[:], kc[:, :, t0:t0 + 128], wr.unsqueeze(1).to_broadcast([128, 2, 128]), ALU.mult),
                                 reads=["kc", "rep"], writes=["ks"])
                            yield
                            if full:
                                for dc in range(2):
                                    P.op("pe", lambda h, dc=dc: h.matmul(p_st[:], ks[:, dc, :], qs[:, dc, :], start=(dc == 0), stop=(dc == 1)), reads=["ks", "qs"], writes=["pmisc"])
                                P.op("dve", lambda h, d=d: h.tensor_tensor(ST[:], p_st[:], mk[:, d, :], ALU.mult), reads=["pmisc", "mk"], writes=["ST"])
                            yield
                            for dc in range(2):
                                P.op("pe", lambda h, dc=dc: h.transpose(p_kt[:, dc * 128:(dc + 1) * 128], ks[:, dc, :], K.ident_b[:]), reads=["ks", "ident_b"], writes=["pbf"])
                            P.op("act", lambda h: h.copy(kst[:], p_kt[:]), reads=["pbf"], writes=["kst"])
                            yield
                            if full:
                                P.op("pe", lambda h, c=c: h.matmul(p_num[:], ST[:], vx[:, c, 0:512], start=True, stop=False), reads=["ST", "vx"], writes=["p_num"])
                                for dc in range(2):
                                    P.op("pe", lambda h, dc=dc: h.matmul(p_num[:], qs[:, dc, :], Cb[:, dc, :], start=False, stop=(dc == 1)), reads=["qs", "Cb"], writes=["p_num"])
                                P.op("pe", lambda h, c=c: h.matmul(p_den[:, 0:1], ST[:], vx[:, c, 512:513], start=True, stop=False), reads=["ST", "vx"], writes=["pmisc"])
                                for dc in range(2):
                                    P.op("pe", lambda h, dc=dc: h.matmul(p_den[:, 0:1], qs[:, dc, :], nb[:, dc:dc + 1], start=False, stop=(dc == 1)), reads=["qs", "nb"], writes=["pmisc"])
                                P.op("act", lambda h: h.activation(sm[:, 5:6], p_den[:, 0:1], AF.Abs), reads=["pmisc"], writes=["sm"])
                                yield
                                P.op("dve", lambda h, d=d, c=c, hd=hd: h.tensor_tensor(sm[:, 0:1], sm[:, 5:6], Ecol[:, d, c, hd:hd + 1], ALU.max), reads=["sm", "Ecol"], writes=["sm"])
                                P.op("dve", lambda h: h.reciprocal(sm[:, 1:2], sm[:, 0:1]), reads=["sm"], writes=["sm"])
                                P.op("act", lambda h: h.activation(hh[:], p_num[:], AF.Copy, scale=sm[:, 1:2]), reads=["p_num", "sm"], writes=["hh"])
                                yield
                            for dc in range(2):
                                P.op("pe", lambda h, dc=dc, c=c: h.matmul(p_dc[dc][:], kst[:, dc * 128:(dc + 1) * 128], vx[:, c, 0:512], start=True, stop=True), reads=["kst", "vx"], writes=["p_dc%d" % dc])
                                P.op("pe", lambda h, dc=dc, c=c: h.matmul(p_dn[:, dc:dc + 1], kst[:, dc * 128:(dc + 1) * 128], vx[:, c, 512:513], start=True, stop=True), reads=["kst", "vx"], writes=["pmisc"])
                            pd = prev_dec if prev_dec is not None else 1.0
                            yield
                            for dc in range(2):
                                P.op("dve", lambda h, dc=dc, pd=pd: h.scalar_tensor_tensor(Pst[:, dc, :], Pst[:, dc, :], pd, p_dc[dc][:], ALU.mult, ALU.add), reads=["Pst", "rep", "p_dc%d" % dc], writes=["Pst"])
                                P.op("pool", lambda h, dc=dc, dec=dec: h.tensor_scalar(Cb[:, dc, :], Pst[:, dc, :], dec, None, ALU.mult), reads=["Pst", "rep"], writes=["Cb"])
                            P.op("dve", lambda h, pd=pd: h.scalar_tensor_tensor(Pn[:], Pn[:], pd, p_dn[:, 0:2], ALU.mult, ALU.add), reads=["Pn", "rep", "pmisc"], writes=["Pn"])
                            P.op("pool", lambda h, dec=dec: h.tensor_scalar(nb[:], Pn[:], dec, None, ALU.mult), reads=["Pn", "rep"], writes=["nb"])
                            prev_dec = dec
                            yield
                            if not full:
                                continue
                            if d == 0:
                                P.op("sp", lambda h, t0=t0: h.dma_start(out=S.hfs[t0:t0 + 128, :], in_=hh[:]), reads=["hh"], writes=["hfs_d%d" % c], dma=True)
                            else:
                                P.op("sp", lambda h, t0=t0: h.dma_start(out=hfl[:], in_=S.hfs[t0:t0 + 128, :]), reads=["hfs_d%d" % c], writes=["hf"], dma=True)
                                P.op("sp", lambda h, t0=t0, hd=hd: h.dma_start(out=ot[:], in_=S.ao[t0:t0 + 128, hd * 512:(hd + 1) * 512]), writes=["ot"], dma=True)
                                P.op("sp", lambda h, t0=t0, hd=hd: h.dma_start(out=zt[:], in_=S.az[t0:t0 + 128, hd * 512:(hd + 1) * 512]), writes=["zt"], dma=True)
                                P.op("dve", lambda h: h.tensor_tensor(hh[:], hh[:], hfl[:], ALU.add), reads=["hh", "hf"], writes=["hh"])
                                yield
                                P.op("act", lambda h: h.activation(hsq[:], hh[:], AF.Square), reads=["hh"], writes=["hsq"])
                                P.op("dve", lambda h: h.tensor_reduce(sm[:, 2:3], hsq[:], AX.X, ALU.add), reads=["hsq"], writes=["sm"])
                                P.op("act", lambda h: h.activation(sm[:, 3:4], sm[:, 2:3], AF.Sqrt, bias=K.eps_t[:, 0:1], scale=1.0 / 512), reads=["sm", "eps_t"], writes=["sm"])
                                P.op("dve", lambda h: h.reciprocal(sm[:, 4:5], sm[:, 3:4]), reads=["sm"], writes=["sm"])
                                yield
                                P.op("dve", lambda h: h.scalar_tensor_tensor(hh[:], hh[:], sm[:, 4:5], ang[:], ALU.mult, ALU.mult), reads=["hh", "sm", "ang"], writes=["hh"])
                                P.op("act", lambda h: h.activation(of_[:], ot[:], AF.Sigmoid), reads=["ot"], writes=["of"])
                                P.op("act", lambda h: h.activation(zf[:], zt[:], AF.Silu), reads=["zt"], writes=["zf"])
                                P.op("pool", lambda h: h.tensor_tensor(of_[:], of_[:], zf[:], ALU.mult), reads=["of", "zf"], writes=["of"])
                                P.op("dve", lambda h: h.tensor_tensor(yb[:], hh[:], of_[:], ALU.mult), reads=["hh", "of"], writes=["yb"])
                                yield
                                for q in range(4):
                                    P.op("pe", lambda h, q=q: h.transpose(p_tr[:, q * 128:(q + 1) * 128], yb[:, q * 128:(q + 1) * 128], K.ident_b[:]), reads=["yb", "ident_b"], writes=["pbf"])
                                P.op("act", lambda h: h.copy(yTt[:], p_tr[:].rearrange("p (a b) -> p a b", a=4)), reads=["pbf"], writes=["yTt"])
                                P.op("sp", lambda h, t0=t0, hd=hd: h.dma_start(out=S.yT[0][hd * 512:(hd + 1) * 512, t0:t0 + 128].rearrange("(a p) t -> p a t", p=128), in_=yTt[:]),
                                     reads=["yTt"], dma=True)

            def gen_na():
                heads = range(1) if K.quick else range(NB)
                for hd in heads:
                    r0 = hd * 128
                    P.op("sp", lambda h, r0=r0: h.dma_start(out=nqn[:, 64:64 + TO], in_=S.bqT[r0:r0 + 128, 0:TO]), writes=["nqn"], dma=True)
                    P.op("sp", lambda h, r0=r0: h.dma_start(out=nkn[:], in_=S.bkT[r0:r0 + 128, 0:NKT]), writes=["nkn"], dma=True)
                    P.op("sp", lambda h, r0=r0: h.dma_start(out=nzs[:], in_=S.bzT[r0:r0 + 128, 0:TO]), writes=["nzs"], dma=True)
                    P.op("sp", lambda h, r0=r0: h.dma_start(out=ve[:], in_=S.bv[0:18 * 128, r0:r0 + 128].rearrange("(t p) d -> p t d", p=128)), writes=["ve"], dma=True)
                    P.op("sp", lambda h, r0=r0: h.dma_start(out=vo[:], in_=S.bv[64:64 + 17 * 128, r0:r0 + 128].rearrange("(t p) d -> p t d", p=128)), writes=["vo"], dma=True)
                    for (EB, ek, o0) in ((EBe, "EBe", 0), (EBo, "EBo", 1)):
                        for two in range(2):
                            P.op("sp", lambda h, EB=EB, o0=o0, two=two, hd=hd: h.dma_start(out=EB[two * 64:(two + 1) * 64, :, :], in_=I.B2[hd, o0 + two:o0 + two + 13:2].rearrange("r k q -> k r q")),
                                 writes=[ek], dma=True)
                        P.op("dve", lambda h, EB=EB: h.tensor_tensor(EB[:], EB[:], cm[:].unsqueeze(1).to_broadcast([128, 7, 64]), ALU.add), reads=[ek, "cm"], writes=[ek])
                        P.op("act", lambda h, EB=EB: h.activation(EB[:], EB[:], AF.Exp), reads=[ek], writes=[ek])
                    P.op("act", lambda h: h.activation(nzs[:], nzs[:], AF.Silu), reads=["nzs"], writes=["nzs"])
                    for (dst, dk, gi_, ntok, doff) in ((nqn, "nqn", 0, TO, 64), (nkn, "nkn", 1, NKT, 0)):
                        for t0 in range(0, ntok, 512):
                            w = min(512, ntok - t0)
                            P.op("act", lambda h, dst=dst, t0=t0, w=w, doff=doff: h.activation(sq[:, 0:w], dst[:, doff + t0:doff + t0 + w], AF.Square), reads=[dk], writes=["sq"])
                            P.op("pe", lambda h, w=w: h.matmul(p_ms[:, 0:w], onesS[:], sq[:, 0:w], start=True, stop=True), reads=["onesS", "sq"], writes=["pod"])
                            P.op("act", lambda h, w=w: h.activation(sd[:, 0:w], p_ms[:, 0:w], AF.Sqrt, bias=K.eps_t[:, 0:1]), reads=["pod", "eps_t"], writes=["sd"])
                            P.op("dve", lambda h, w=w: h.reciprocal(sd[:, 0:w], sd[:, 0:w]), reads=["sd"], writes=["sd"])
                            P.op("dve", lambda h, dst=dst, t0=t0, w=w, gi_=gi_, doff=doff: h.scalar_tensor_tensor(dst[:, doff + t0:doff + t0 + w], dst[:, doff + t0:doff + t0 + w], gq[:, gi_:gi_ + 1], sd[:, 0:w], ALU.mult, ALU.mult),
                                 reads=[dk, "gq", "sd"], writes=[dk])
                            yield
                    P.op("dve", lambda h: h.tensor_scalar(nqs[:], nqn[:, 64:64 + QW], ab[:, 0:1], None, ALU.mult), reads=["nqn", "ab"], writes=["nqs"])
                    P.op("dve", lambda h: h.scalar_tensor_tensor(nqs[:], nqn[:, 0:QW], ab[:, 1:2], nqs[:], ALU.mult, ALU.add), reads=["nqn", "ab", "nqs"], writes=["nqs"])
                    for r in range(NQR):
                        rs = max(r - 4, 0)
                        d0 = rs - r + 7
                        EB, ek, j0 = (EBe, "EBe", d0 // 2) if d0 % 2 == 0 else (EBo, "EBo", (d0 - 1) // 2)
                        b = r % 2
                        slot = r % 4
                        for i in range(4):
                            tk0 = (rs + 2 * i) * 64
                            P.op("pe", lambda h, b=b, i=i, tk0=tk0, r=r: h.matmul(p_s[b][:, i, :], nkn[:, tk0:tk0 + 128], nqs[:, r * 64:(r + 1) * 64], start=True, stop=True),
                                 reads=["nkn", "nqs"], writes=["p_s%d" % b])
                        yield
                        P.op("act", lambda h, b=b: h.activation(pe_[b][:], p_s[b][:], AF.Exp), reads=["p_s%d" % b], writes=["pe%d" % b])
                        P.op("dve", lambda h, b=b, EB=EB, j0=j0: h.tensor_tensor(pt_[b][:], pe_[b][:], EB[:, j0:j0 + 4, :], ALU.mult), reads=["pe%d" % b, ek], writes=["pt%d" % b])
                        yield
                        for i in range(4):
                            kr = rs + 2 * i
                            vt = ve[:, kr // 2, :] if kr % 2 == 0 else vo[:, (kr - 1) // 2, :]
                            P.op("pe", lambda h, b=b, i=i, vt=vt, slot=slot: h.matmul(p_o[:, slot * 64:(slot + 1) * 64], vt, pt_[b][:, i, :], start=(i == 0), stop=(i == 3)),
                                 reads=["ve", "vo", "pt%d" % b], writes=["pod"])
                        for i in range(4):
                            P.op("pe", lambda h, b=b, i=i, slot=slot: h.matmul(p_d[:, slot * 64:(slot + 1) * 64], ones1[:], pt_[b][:, i, :], start=(i == 0), stop=(i == 3)),
                                 reads=["ones1", "pt%d" % b], writes=["pod"])
                        yield
                        if slot == 3 or r == NQR - 1:
                            w = (slot + 1) * 64
                            c0 = (r // 4) * 256
                            P.op("dve", lambda h, w=w: h.reciprocal(rd[:, 0:w], p_d[:, 0:w]), reads=["pod"], writes=["rd"])
                            P.op("dve", lambda h, w=w, c0=c0: h.tensor_tensor(Ys[:, c0:c0 + w], p_o[:, 0:w], rd[:, 0:w], ALU.mult), reads=["pod", "rd"], writes=["Ys"])
                    for tb in range(0, TO, 512):
                        P.op("dve", lambda h, tb=tb: h.tensor_scalar(yo[:], Ys[:, tb:tb + 512], ab[:, 0:1], None, ALU.mult), reads=["Ys", "ab"], writes=["yo"])
                        P.op("dve", lambda h, tb=tb: h.scalar_tensor_tensor(yo[:], Ys[:, tb + 64:tb + 64 + 512], ab[:, 1:2], yo[:], ALU.mult, ALU.add), reads=["Ys", "ab", "yo"], writes=["yo"])
                        P.op("pool", lambda h, tb=tb: h.tensor_tensor(yob[:], yo[:], nzs[:, tb:tb + 512], ALU.mult), reads=["yo", "nzs"], writes=["yob"])
                        P.op("sp", lambda h, tb=tb, r0=r0: h.dma_start(out=S.yT[0][D + r0:D + r0 + 128, tb:tb + 512], in_=yob[:]), reads=["yob"], dma=True)
                        yield

            gens = [gen_ml(), gen_na()]
            while gens:
                for g in list(gens):
                    try:
                        next(g)
                    except StopIteration:
                        gens.remove(g)
            P.flush()
            P.exclusive = set()


def phase_outproj(K, layer, x_src, dst):
    nc, P, I, S = K.nc, K.P, K.I, K.S
    NKO = CW // 128
    with Alloc(nc) as A_:
        oy = A_.sb("oy", [128, NKO, 512], BF16)
        ow0 = A_.sb("ow0", [128, NKO, 512], BF16)
        ow1 = A_.sb("ow1", [128, NKO, 512], BF16)
        og0 = A_.sb("og0", [128, 512], F32)
        og1 = A_.sb("og1", [128, 512], F32)
        ox0 = A_.sb("ox0", [128, 512], F32)
        ox1 = A_.sb("ox1", [128, 512], F32)
        oo0 = A_.sb("oo0", [128, 512], F32)
        oo1 = A_.sb("oo1", [128, 512], F32)
        ops0 = A_.ps("ops0", [128, 512], F32)
        ops1 = A_.ps("ops1", [128, 512], F32)
        ow = [ow0, ow1]
        og = [og0, og1]
        ox = [ox0, ox1]
        oo = [oo0, oo1]
        ops = [ops0, ops1]
        it = 0
        pj = 0
        ntb = 1 if K.quick else TO // 512
        for tb in range(ntb):
            for q in range(4):
                P.op("sp", lambda h, tb=tb, q=q: h.dma_start(out=oy[:, q * 16:(q + 1) * 16, :], in_=S.yT[layer][q * 2048:(q + 1) * 2048, tb * 512:(tb + 1) * 512].rearrange("(k p) t -> p k t", p=128)),
                     writes=["oy"], dma=True)
            for cb in range(D // 512):
                wb = ow[it % 2]
                wk = "ow%d" % (it % 2)
                gt_ = og[it % 2]
                gk = "og%d" % (it % 2)
                it += 1
                src = I.w_out[layer][:, cb * 512:(cb + 1) * 512].rearrange("(k p) c -> p k c", p=128)
                for q in range(4):
                    P.op("pool", lambda h, wb=wb, src=src, q=q: h.dma_start(out=wb[:, q * 16:(q + 1) * 16, :], in_=src[:, q * 16:(q + 1) * 16, :]), writes=[wk], dma=True)
                P.op("sp", lambda h, gt_=gt_, cb=cb: h.dma_start(out=gt_[:], in_=S.modrep[layer][2, :, cb * 512:(cb + 1) * 512]), writes=[gk], dma=True)
                for tt in range(4):
                    ps = ops[pj % 2]
                    pk = "ops%d" % (pj % 2)
                    xt = ox[pj % 2]
                    xk = "ox%d" % (pj % 2)
                    ot_ = oo[pj % 2]
                    okk = "oo%d" % (pj % 2)
                    pj += 1
                    tok = tb * 512 + tt * 128
                    for k in range(NKO):
                        P.op("pe", lambda h, ps=ps, wb=wb, k=k, tt=tt: h.matmul(ps[:], oy[:, k, tt * 128:(tt + 1) * 128], wb[:, k, :], start=(k == 0), stop=(k == NKO - 1)),
                             reads=["oy", wk], writes=[pk])
                    P.op("sp", lambda h, xt=xt, tok=tok, cb=cb: h.dma_start(out=xt[:], in_=x_src[tok:tok + 128, cb * 512:(cb + 1) * 512]), writes=[xk], dma=True)
                    P.op("dve", lambda h, ps=ps, ot_=ot_, gt_=gt_: h.tensor_tensor(ot_[:], ps[:], gt_[:], ALU.mult), reads=[pk, gk], writes=[okk])
                    P.op("pool", lambda h, ot_=ot_, xt=xt: h.tensor_tensor(ot_[:], ot_[:], xt[:], ALU.add), reads=[okk, xk], writes=[okk])
                    P.op("sp", lambda h, ot_=ot_, tok=tok, cb=cb: h.dma_start(out=dst[tok:tok + 128, cb * 512:(cb + 1) * 512], in_=ot_[:]), reads=[okk], dma=True)
        P.flush()


def phase_l1_inproj(K):
    nc, P, I, S = K.nc, K.P, K.I, K.S
    with nc.sbuf_tensor(U("hT1"), [128, NK, 2048], BF16) as hT:
        for ps_i in range(1):
            tok0 = ps_i * 2048
            norm_pass(K, S.x1[tok0:tok0 + 2048, :], 2048, 1, hT)
            with Alloc(nc) as A_:
                lb0 = A_.sb("lb0", [128, 512], BF16)
                lb1 = A_.sb("lb1", [128, 512], BF16)
                lb2 = A_.sb("lb2", [128, 512], BF16)
                lb3 = A_.sb("lb3", [128, 512], BF16)
                lf0 = A_.sb("lf0", [128, 512], F32)
                lf1 = A_.sb("lf1", [128, 512], F32)
                lf2 = A_.sb("lf2", [128, 512], F32)
                lf3 = A_.sb("lf3", [128, 512], F32)
                stb = Stager([lb0, lb1, lb2, lb3], "lb")
                stf = Stager([lf0, lf1, lf2, lf3], "lf")

                def ev_gelu(dst):
                    def f(ps, pk, c, tt, w):
                        st, sk = stb.next()
                        t1, k1 = stf.next()
                        P.op("act", lambda h: h.activation(t1[:], ps[:], AF.Square), reads=[pk], writes=[k1])
                        P.op("dve", lambda h: h.tensor_scalar(t1[:], t1[:], 0.044715, 1.0, ALU.mult, ALU.add), reads=[k1], writes=[k1])
                        P.op("dve", lambda h: h.tensor_tensor(t1[:], t1[:], ps[:], ALU.mult), reads=[k1, pk], writes=[k1])
                        P.op("act", lambda h: h.activation(t1[:], t1[:], AF.Sigmoid, scale=1.5957691216057308), reads=[k1], writes=[k1])
                        P.op("dve", lambda h: h.tensor_tensor(st[:], t1[:], ps[:], ALU.mult), reads=[k1, pk], writes=[sk])
                        P.op("sp", lambda h: h.dma_start(out=dst[tok0 + tt * 128:tok0 + (tt + 1) * 128, c:c + w], in_=st[:, 0:w]), reads=[sk], dma=True)
                    return f

                def ev_silu(dst):
                    def f(ps, pk, c, tt, w):
                        st, sk = stb.next()
                        P.op("act", lambda h: h.activation(st[:], ps[:], AF.Silu), reads=[pk], writes=[sk])
                        P.op("sp", lambda h: h.dma_start(out=dst[tok0 + tt * 128:tok0 + (tt + 1) * 128, c:c + w], in_=st[:, 0:w]), reads=[sk], dma=True)
                    return f

                W = I.w_in1
                nt = range(1) if K.quick else range(16)
                jobs = [
                    dict(mode="TM", w=W[:, 0:CW], ncols=CW, tiles=nt, evac=ev_gelu(S.gu)),
                    dict(mode="TM", w=W[:, CW:2 * CW], ncols=CW, tiles=nt, evac=ev_gelu(S.gv)),
                    dict(mode="TM", w=W[:, 2 * CW:3 * CW], ncols=CW, tiles=nt, evac=ev_silu(S.sz)),
                ]
                inproj(K, hT, jobs)


def phase_gmlp(K):
    nc, P, I, S = K.nc, K.P, K.I, K.S
    with Alloc(nc) as A_:
        vg = A_.sb("vg", [128, CW], F32)
        wsf = A_.sb("wsf", [128, 8, 128], F32)
        wsb = A_.sb("wsb", [128, 8, 128], BF16)
        bsf = A_.sb("bsf", [128, 8], F32)
        gvt = A_.sb("gvt", [128, CW], BF16)
        gvs = A_.sb("gvs", [128, CW], BF16)
        gvn = A_.sb("gvn", [128, CW], BF16)
        gut = A_.sb("gut", [128, CW], BF16)
        szt = A_.sb("szt", [128, CW], BF16)
        ych = A_.sb("ych", [128, CW], BF16)
        yTc = A_.sb("yTc", [128, 64, 128], BF16)
        gt0 = A_.sb("gt0", [128, 512], F32)
        gt1 = A_.sb("gt1", [128, 512], F32)
        gsm = A_.sb("gsm", [128, 4], F32)
        gp0 = A_.ps("gp0", [128, 512], F32)
        gp1 = A_.ps("gp1", [128, 512], F32)
        gtr0 = A_.ps("gtr0", [128, 512], BF16)
        gtr1 = A_.ps("gtr1", [128, 512], BF16)
        gp = [gp0, gp1]
        gt = [gt0, gt1]
        gtr = [gtr0, gtr1]
        P.op("sp", lambda h: h.dma_start(out=vg[:], in_=I.vg_rep), writes=["vg"], dma=True)
        P.op("sp", lambda h: h.dma_start(out=wsf[:], in_=I.WsT.rearrange("g s t -> s g t")), writes=["wsf"], dma=True)
        P.op("dve", lambda h: h.tensor_copy(wsb[:], wsf[:]), reads=["wsf"], writes=["wsb"])
        P.op("sp", lambda h: h.dma_start(out=bsf[:], in_=I.bs_fm), writes=["bsf"], dma=True)
        nchunks = 1 if K.quick else NOWN
        pj = 0
        for c in range(nchunks):
            t0 = c * 128
            P.op("sp", lambda h, t0=t0: h.dma_start(out=gvt[:], in_=S.gv[t0:t0 + 128, :]), writes=["gvt"], dma=True)
            P.op("sp", lambda h, t0=t0: h.dma_start(out=gut[:], in_=S.gu[t0:t0 + 128, :]), writes=["gut"], dma=True)
            P.op("sp", lambda h, t0=t0: h.dma_start(out=szt[:], in_=S.sz[t0:t0 + 128, :]), writes=["szt"], dma=True)
            P.op("act", lambda h: h.activation(gvs[:], gvt[:], AF.Square), reads=["gvt"], writes=["gvs"])
            P.op("dve", lambda h: h.tensor_reduce(gsm[:, 0:1], gvs[:], AX.X, ALU.add), reads=["gvs"], writes=["gsm"])
            P.op("act", lambda h: h.activation(gsm[:, 1:2], gsm[:, 0:1], AF.Sqrt, bias=K.eps_t[:, 0:1], scale=1.0 / CW), reads=["gsm", "eps_t"], writes=["gsm"])
            P.op("dve", lambda h: h.reciprocal(gsm[:, 2:3], gsm[:, 1:2]), reads=["gsm"], writes=["gsm"])
            P.op("dve", lambda h: h.scalar_tensor_tensor(gvn[:], gvt[:], gsm[:, 2:3], vg[:], ALU.mult, ALU.mult), reads=["gvt", "gsm", "vg"], writes=["gvn"])
            for g in range(8):
                for hh_ in range(2):
                    cs_ = g * 1024 + hh_ * 512
                    ps = gp[pj % 2]
                    pk = "gp%d" % (pj % 2)
                    tt_ = gt[pj % 2]
                    tk = "gt%d" % (pj % 2)
                    pj += 1
                    P.op("pe", lambda h, ps=ps, g=g, cs_=cs_: h.matmul(ps[:], wsb[:, g, :], gvn[:, cs_:cs_ + 512], start=True, stop=True), reads=["wsb", "gvn"], writes=[pk])
                    P.op("dve", lambda h, ps=ps, tt_=tt_, g=g, cs_=cs_: h.scalar_tensor_tensor(tt_[:], ps[:], bsf[:, g:g + 1], gut[:, cs_:cs_ + 512], ALU.add, ALU.mult), reads=[pk, "bsf", "gut"], writes=[tk])
                    P.op("pool", lambda h, tt_=tt_, cs_=cs_: h.tensor_tensor(ych[:, cs_:cs_ + 512], tt_[:], szt[:, cs_:cs_ + 512], ALU.mult), reads=[tk, "szt"], writes=["ych"])
            for k4 in range(0, 64, 4):
                tp = gtr[(k4 // 4) % 2]
                tk = "gtr%d" % ((k4 // 4) % 2)
                for q in range(4):
                    P.op("pe", lambda h, tp=tp, q=q, k4=k4: h.transpose(tp[:, q * 128:(q + 1) * 128], ych[:, (k4 + q) * 128:(k4 + q + 1) * 128], K.ident_b[:]), reads=["ych", "ident_b"], writes=[tk])
                if (k4 // 4) % 2:
                    P.op("act", lambda h, tp=tp, k4=k4: h.copy(yTc[:, k4:k4 + 4, :], tp[:].rearrange("p (a b) -> p a b", a=4)), reads=[tk], writes=["yTc"])
                else:
                    P.op("dve", lambda h, tp=tp, k4=k4: h.tensor_copy(yTc[:, k4:k4 + 4, :], tp[:].rearrange("p (a b) -> p a b", a=4)), reads=[tk], writes=["yTc"])
            for q in range(4):
                P.op("sp", lambda h, t0=t0, q=q: h.dma_start(out=S.yT[1][q * 2048:(q + 1) * 2048, t0:t0 + 128].rearrange("(k p) t -> p k t", p=128), in_=yTc[:, q * 16:(q + 1) * 16, :]),
                     reads=["yTc"], dma=True)
        P.flush()


def prep_core(inp, b, hf):
    rev = hf == 1
    norm_g = [inp["norm_g0"], inp["norm_g1"]]
    ada_w = [inp["ada_w0"], inp["ada_w1"]]
    ada_b = [inp["ada_b0"], inp["ada_b1"]]
    w_out = [inp["w_out0"], inp["w_out1"]]
    m = {}
    xb = inp["x"][b]
    m["x"] = np.ascontiguousarray(xb[::-1] if rev else xb)
    m["c_fm"] = np.ascontiguousarray(inp["c"][b].reshape(NK, 128).T)
    for l in range(2):
        m["g_rep%d" % l] = np.ascontiguousarray(np.broadcast_to(norm_g[l][None, :], (128, D)))
        m["ada_w%d" % l] = ada_w[l]
        m["ada_b_rep%d" % l] = np.ascontiguousarray(np.broadcast_to(ada_b[l][None, :], (128, 3 * D)))
        m["w_out%d" % l] = w_out[l]
    m["w_in0"] = inp["w_in0"]
    m["w_in1"] = inp["w_in1"]
    m["ident"] = np.eye(128, dtype=np.float32)
    gperm = np.arange(32)
    if rev:
        gperm = np.concatenate([np.arange(16, 32), np.arange(0, 16)])
    m["w_gates"] = np.ascontiguousarray(inp["w_in0"][:, C_G:C_G + 32][:, gperm])
    m["gate_b"] = np.ascontiguousarray(inp["a_gate_b"][gperm].reshape(4, 8).T)
    cw = inp["a_conv_w"][::-1] if rev else inp["a_conv_w"]
    m["convw"] = np.ascontiguousarray(cw.T.reshape(32, 128, 3).transpose(1, 0, 2))
    m["ang_rep"] = np.ascontiguousarray(np.broadcast_to(inp["a_norm_g"].reshape(1, D), (128, D)))
    m["gqk"] = np.ascontiguousarray(np.stack([inp["b_q_gain"], inp["b_k_gain"]], axis=1))
    kc = np.arange(64)[:, None]
    qc = np.arange(64)[None, :]
    dci = np.clip(kc - qc + 15, 0, 30)
    rpb = inp["b_rpb"]
    if not rev:
        m["B2"] = np.ascontiguousarray(rpb[:, :, dci])
        cs = np.clip(qc - 8, 0, 48)
    else:
        dru = np.clip(13 - np.arange(15), 0, 14)
        m["B2"] = np.ascontiguousarray(rpb[:, dru][:, :, 30 - dci])
        cs = np.clip(qc - 7, 0, 48)
    cmask = np.where((kc >= cs) & (kc < cs + 16), 0.0, -30000.0).astype(np.float32)
    m["cmask"] = np.ascontiguousarray(np.concatenate([cmask, cmask], axis=0))
    s_ = np.arange(128)[:, None]
    t_ = np.arange(128)[None, :]
    m["masks"] = np.stack([(s_ <= t_), (s_ >= t_)]).astype(np.float32)
    m["ab"] = np.ascontiguousarray(np.broadcast_to(np.array([[0.0, 1.0]] if rev else [[1.0, 0.0]], np.float32), (128, 2)))
    m["vg_rep"] = np.ascontiguousarray(np.broadcast_to(inp["c_v_norm_g"][None, :], (128, CW)))
    ws = inp["c_w_s"]
    bs = inp["c_b_s"]
    if rev:
        ws = ws[:, ::-1, ::-1]
        bs = bs[:, ::-1]
    m["WsT"] = np.ascontiguousarray(ws.transpose(0, 2, 1))
    m["bs_fm"] = np.ascontiguousarray(bs.T)
    return m


def kernel(**inputs):
    inp = {k: np.asarray(v) for k, v in inputs.items()}
    nc = build()
    in_maps = [prep_core(inp, c // 2, c % 2) for c in range(8)]
    res = run_bass_kernel_spmd(nc, in_maps, core_ids=list(range(8)))
    out = np.empty((4, T, D), np.float32)
    for c in range(8):
        b, hf = c // 2, c % 2
        o = res.results[c]["out"]
        if hf == 0:
            out[b, :TO] = o
        else:
            out[b, TO:] = o[::-1]
    return out
```

```python
import numpy as np
from contextlib import ExitStack
import concourse.bass as bass
import concourse.mybir as mybir
from concourse.bass_utils import run_bass_kernel_spmd

F32 = mybir.dt.float32
BF16 = mybir.dt.bfloat16
AF = mybir.ActivationFunctionType
ALU = mybir.AluOpType
AX = mybir.AxisListType

D = 4096
TH = 2048
NK = 32
L0C = 32800
EPS = 1e-6
N_DMA_SEMS = 24
SAFE_SAME_ENGINE = True

C_AQ, C_AK, C_AV, C_AO, C_AZ, C_G, C_BQ, C_BK, C_BV, C_BZ = 0, 2048, 4096, 8192, 12288, 16384, 16416, 20512, 24608, 28704


class Prog:
    def __init__(self, nc):
        self.nc = nc
        self.eng = {"pe": nc.tensor, "act": nc.scalar, "dve": nc.vector, "pool": nc.gpsimd, "sp": nc.sync}
        self.esem = {e: nc.alloc_semaphore("es_" + e) for e in self.eng}
        self.ecnt = {e: 0 for e in self.eng}
        self.dsem = [nc.alloc_semaphore("ds%d" % i) for i in range(N_DMA_SEMS)]
        self.dcnt = [0] * N_DMA_SEMS
        self.dnext = 0
        self.seen = {e: {} for e in self.eng}
        self.ops = []
        self.last_writer = {}
        self.readers = {}
        self.barrier_set = []
        self.n_emitted = 0
        self.exclusive = set()

    def op(self, eng, fn, reads=(), writes=(), dma=False):
        idx = len(self.ops)
        deps = set()
        if self.exclusive:
            ex = [b for b in reads if b in self.exclusive]
            if ex:
                reads = [b for b in reads if b not in self.exclusive]
                writes = list(writes) + ex
        for b in reads:
            w = self.last_writer.get(b)
            if w is not None:
                deps.add(w)
        for b in writes:
            w = self.last_writer.get(b)
            if w is not None:
                deps.add(w)
            for r in self.readers.get(b, {}).values():
                deps.add(r)
        for b in reads:
            d = self.readers.setdefault(b, {})
            key = ("dma", idx) if dma else eng
            d[key] = idx
        for b in writes:
            self.last_writer[b] = idx
            self.readers[b] = {}
        deps.discard(idx)
        self.ops.append(dict(eng=eng, fn=fn, deps=deps, dma=dma, sig=None))
        return idx

    def flush(self):
        ops = self.ops
        need = [False] * len(ops)
        for o in ops:
            for d in o["deps"]:
                p = ops[d]
                if p["dma"]:
                    need[d] = True
                elif p["eng"] == o["eng"] and not o["dma"]:
                    if p["eng"] != "pe" and SAFE_SAME_ENGINE:
                        need[d] = True
                else:
                    need[d] = True
        last = {}
        for i, o in enumerate(ops):
            if o["dma"]:
                need[i] = True
            else:
                last[o["eng"]] = i
        for i in last.values():
            need[i] = True
        for i, o in enumerate(ops):
            e = o["eng"]
            h = self.eng[e]
            waits = list(self.barrier_set)
            for d in sorted(o["deps"]):
                p = ops[d]
                if not need[d]:
                    continue
                if (not p["dma"]) and p["eng"] == e and not o["dma"] and (e == "pe" or not SAFE_SAME_ENGINE):
                    continue
                waits.append(p["sig"])
            seen = self.seen[e]
            for (k, sem, val) in waits:
                if seen.get(k, 0) >= val:
                    continue
                h.wait_ge(sem, val)
                seen[k] = val
            ins = o["fn"](h)
            if need[i]:
                if o["dma"]:
                    s = self.dnext
                    self.dnext = (self.dnext + 1) % N_DMA_SEMS
                    self.dcnt[s] += 16
                    ins.then_inc(self.dsem[s], 16)
                    o["sig"] = (("d", s), self.dsem[s], self.dcnt[s])
                else:
                    self.ecnt[e] += 1
                    ins.then_inc(self.esem[e], 1)
                    o["sig"] = (("e", e), self.esem[e], self.ecnt[e])
            self.n_emitted += 1
        bs = {}
        for o in ops:
            if o["sig"] is not None:
                k, sem, val = o["sig"]
                if k not in bs or bs[k][2] < val:
                    bs[k] = (k, sem, val)
        for (k, sem, val) in self.barrier_set:
            if k not in bs or bs[k][2] < val:
                bs[k] = (k, sem, val)
        self.barrier_set = list(bs.values())
        self.ops = []
        self.last_writer = {}
        self.readers = {}

    def final_wait(self):
        for e, h in self.eng.items():
            seen = self.seen[e]
            for (k, sem, val) in self.barrier_set:
                if seen.get(k, 0) >= val:
                    continue
                h.wait_ge(sem, val)
                seen[k] = val


class Ctx:
    pass


class Alloc:
    def __init__(self, nc):
        self.nc = nc
        self.es = ExitStack()

    def __enter__(self):
        self.es.__enter__()
        return self

    def __exit__(self, *a):
        return self.es.__exit__(*a)

    def sb(self, name, shape, dt):
        return self.es.enter_context(self.nc.sbuf_tensor(U(name), shape, dt))

    def ps(self, name, shape, dt):
        return self.es.enter_context(self.nc.psum_tensor(U(name), shape, dt))


_UC = [0]


def U(name):
    _UC[0] += 1
    return "%s_%d" % (name, _UC[0])


T = 4096
TO = 2048
NOWN = 16
NCH = 32
HD = 8
NB = 32
CW = 8192


def build(taps=(), stop_after=None, quick=False):
    nc = bass.Bass("TRN2", target_bir_lowering=False)
    P = Prog(nc)
    K = Ctx()
    K.nc, K.P, K.taps, K.quick = nc, P, set(taps), quick

    order = ["mod", "in0", "na", "out0", "in1", "gmlp", "out1"]
    last = order.index(stop_after) if stop_after else len(order) - 1
    K.in_shapes = {}

    def din(name, shape, dt=F32, first="mod"):
        if order.index(first) > last:
            shape = [128, 128]
        K.in_shapes[name] = tuple(shape)
        return nc.dram_tensor(name, list(shape), dt, kind="ExternalInput").ap()

    def dscr(name, shape, dt=BF16):
        kind = "ExternalOutput" if name in K.taps else "Internal"
        return nc.dram_tensor(name, list(shape), dt, kind=kind).ap()

    I = Ctx()
    K.I = I
    I.x = din("x", [T, D])
    I.c_fm = din("c_fm", [128, NK])
    I.g_rep = [din("g_rep%d" % l, [128, D]) for l in range(2)]
    I.ada_w = [din("ada_w%d" % l, [D, 3 * D]) for l in range(2)]
    I.ada_b_rep = [din("ada_b_rep%d" % l, [128, 3 * D]) for l in range(2)]
    I.w_in0 = din("w_in0", [D, L0C], first="in0")
    I.ident = din("ident", [128, 128])
    I.w_gates = din("w_gates", [D, 32])
    I.ab = din("ab", [128, 2])
    I.convw = din("convw", [128, 32, 3])
    I.gate_b = din("gate_b", [8, 4])
    I.ang_rep = din("ang_rep", [128, D])
    I.gqk = din("gqk", [128, 2])
    I.B2 = din("B2", [NB, 15, 64, 64])
    I.cmask = din("cmask", [128, 64])
    I.masks = din("masks", [2, 128, 128])
    I.w_out = [din("w_out%d" % l, [CW, D], first=("out0", "out1")[l]) for l in range(2)]
    I.w_in1 = din("w_in1", [D, 3 * CW], first="in1")
    I.vg_rep = din("vg_rep", [128, CW])
    I.WsT = din("WsT", [8, 128, 128])
    I.bs_fm = din("bs_fm", [128, 8])
    K.out = nc.dram_tensor("out", [TO, D], F32, kind="ExternalOutput").ap()

    S = Ctx()
    K.S = S
    S.modrep = [dscr("modrep%d" % l, [3, 128, D], F32) for l in range(2)]
    S.aqT = dscr("aqT", [2048, T])
    S.akT = dscr("akT", [2048, T])
    S.av = dscr("av", [T, D])
    S.ao = dscr("ao", [T, D])
    S.az = dscr("az", [T, D])
    S.gT = dscr("gT", [32, T], F32)
    S.bqT = dscr("bqT", [D, T])
    S.bzT = dscr("bzT", [D, T])
    S.bkT = dscr("bkT", [D, T])
    S.bv = dscr("bv", [T, D])
    S.gsc = dscr("gsc", [2, 2, 8, T], F32)
    S.hfs = dscr("hfs", [TO, 512], F32)
    S.yT = [dscr("yT%d" % l, [CW, TO]) for l in range(2)]
    S.x1 = dscr("x1", [TO, D], F32)
    S.gu = dscr("gu", [TO, CW])
    S.gv = dscr("gv", [TO, CW])
    S.sz = dscr("sz", [TO, CW])

    with Alloc(nc) as A_:
        ident_f = A_.sb("ident_f", [128, 128], F32)
        ident_b = A_.sb("ident_b", [128, 128], BF16)
        eps_t = A_.sb("eps_t", [128, 1], F32)
        K.ident_f, K.ident_b, K.eps_t = ident_f, ident_b, eps_t
        P.op("sp", lambda h: h.dma_start(out=ident_f[:], in_=I.ident), writes=["ident_f"], dma=True)
        P.op("dve", lambda h: h.tensor_copy(ident_b[:], ident_f[:]), reads=["ident_f"], writes=["ident_b"])
        P.op("pool", lambda h: h.memset(eps_t[:], EPS), writes=["eps_t"])
        P.flush()
        stages = [
            ("mod", lambda: phase_mod(K)),
            ("in0", lambda: phase_l0_inproj(K)),
            ("na", lambda: phase_mixers(K)),
            ("out0", lambda: phase_outproj(K, 0, I.x, S.x1)),
            ("in1", lambda: phase_l1_inproj(K)),
            ("gmlp", lambda: phase_gmlp(K)),
            ("out1", lambda: phase_outproj(K, 1, S.x1, K.out)),
        ]
        for name, fn in stages:
            fn()
            if stop_after == name:
                break
    K.P.final_wait()
    nc._in_shapes = K.in_shapes
    return nc


def phase_mod(K):
    nc, P, I, S = K.nc, K.P, K.I, K.S
    with Alloc(nc) as A_:
        cf = A_.sb("cf", [128, NK], F32)
        cs = A_.sb("cs", [128, NK], F32)
        csb = A_.sb("csb", [128, NK, 128], BF16)
        mw0 = A_.sb("mw0", [128, NK, 512], BF16)
        mw1 = A_.sb("mw1", [128, NK, 512], BF16)
        mb = A_.sb("mb", [128, 512], F32)
        mg = A_.sb("mg", [128, 512], F32)
        mo0 = A_.sb("mo0", [128, 512], F32)
        mo1 = A_.sb("mo1", [128, 512], F32)
        mps0 = A_.ps("mps0", [128, 512], F32)
        mps1 = A_.ps("mps1", [128, 512], F32)
        mw = [mw0, mw1]
        mo = [mo0, mo1]
        mps = [mps0, mps1]
        P.op("sp", lambda h: h.dma_start(out=cf[:], in_=I.c_fm), writes=["cf"], dma=True)
        P.op("act", lambda h: h.activation(cs[:], cf[:], AF.Silu), reads=["cf"], writes=["cs"])
        P.op("dve", lambda h: h.tensor_copy(csb[:], cs[:].unsqueeze(2).to_broadcast([128, NK, 128])), reads=["cs"], writes=["csb"])
        it = 0
        for l in range(2):
            for n in range(24):
                wb = mw[it % 2]
                wk = "mw%d" % (it % 2)
                src = I.ada_w[l][:, n * 512:(n + 1) * 512].rearrange("(k p) c -> p k c", p=128)
                for q in range(4):
                    P.op("pool", lambda h, wb=wb, src=src, q=q: h.dma_start(out=wb[:, q * 8:(q + 1) * 8, :], in_=src[:, q * 8:(q + 1) * 8, :]),
                         writes=[wk], dma=True)
                ps = mps[it % 2]
                pk = "mps%d" % (it % 2)
                for k in range(NK):
                    P.op("pe", lambda h, ps=ps, wb=wb, k=k: h.matmul(ps[:], csb[:, k, :], wb[:, k, :], start=(k == 0), stop=(k == NK - 1)),
                         reads=[wk, "csb"], writes=[pk])
                P.op("sp", lambda h, l=l, n=n: h.dma_start(out=mb[:], in_=I.ada_b_rep[l][:, n * 512:(n + 1) * 512]), writes=["mb"], dma=True)
                o = mo[it % 2]
                ok = "mo%d" % (it % 2)
                which = n // 8
                cb = (n % 8) * 512
                P.op("dve", lambda h, ps=ps, o=o: h.tensor_tensor(o[:], ps[:], mb[:], ALU.add), reads=[pk, "mb"], writes=[ok])
                if which == 1:
                    P.op("sp", lambda h, l=l, cb=cb: h.dma_start(out=mg[:], in_=I.g_rep[l][:, cb:cb + 512]), writes=["mg"], dma=True)
                    P.op("dve", lambda h, o=o: h.scalar_tensor_tensor(o[:], o[:], 1.0, mg[:], ALU.add, ALU.mult), reads=[ok, "mg"], writes=[ok])
                P.op("sp", lambda h, o=o, l=l, which=which, cb=cb: h.dma_start(out=S.modrep[l][which, :, cb:cb + 512], in_=o[:]),
                     reads=[ok], dma=True)
                it += 1
        P.flush()


def norm_pass(K, x_dram, ntok, layer, hT):
    nc, P, S = K.nc, K.P, K.S
    ntile = ntok // 128
    with Alloc(nc) as A_:
        nG = A_.sb("nG", [128, D], F32)
        nS = A_.sb("nS", [128, D], F32)
        nx = A_.sb("nx", [128, D], F32)
        nsq = A_.sb("nsq", [128, D], BF16)
        nh = A_.sb("nh", [128, D], BF16)
        nss = A_.sb("nss", [128, 4], F32)
        ntp0 = A_.ps("ntp0", [128, 512], BF16)
        ntp1 = A_.ps("ntp1", [128, 512], BF16)
        ntp = [ntp0, ntp1]
        eps_t = K.eps_t
        P.op("sp", lambda h: h.dma_start(out=nS[:], in_=S.modrep[layer][0]), writes=["nS"], dma=True)
        P.op("sp", lambda h: h.dma_start(out=nG[:], in_=S.modrep[layer][1]), writes=["nG"], dma=True)
        j = 0
        for t in range(ntile):
            P.op("sp", lambda h, t=t: h.dma_start(out=nx[:], in_=x_dram[t * 128:(t + 1) * 128, :]), writes=["nx"], dma=True)
            P.op("act", lambda h: h.activation(nsq[:], nx[:], AF.Square), reads=["nx"], writes=["nsq"])
            P.op("dve", lambda h: h.tensor_reduce(nss[:, 0:1], nsq[:], AX.X, ALU.add), reads=["nsq"], writes=["nss"])
            P.op("act", lambda h: h.activation(nss[:, 1:2], nss[:, 0:1], AF.Sqrt, bias=eps_t[:, 0:1], scale=1.0 / D), reads=["nss", "eps_t"], writes=["nss"])
            P.op("dve", lambda h: h.reciprocal(nss[:, 2:3], nss[:, 1:2]), reads=["nss"], writes=["nss"])
            P.op("dve", lambda h: h.scalar_tensor_tensor(nx[:], nx[:], nss[:, 2:3], nG[:], ALU.mult, ALU.mult), reads=["nx", "nss", "nG"], writes=["nx"])
            P.op("pool", lambda h: h.tensor_tensor(nh[:], nx[:], nS[:], ALU.add), reads=["nx", "nS"], writes=["nh"])
            for kk in range(0, NK, 4):
                tp = ntp[j % 2]
                tk = "ntp%d" % (j % 2)
                for q in range(4):
                    P.op("pe", lambda h, tp=tp, q=q, kk=kk: h.transpose(tp[:, q * 128:(q + 1) * 128], nh[:, (kk + q) * 128:(kk + q + 1) * 128], K.ident_b[:]),
                         reads=["nh", "ident_b"], writes=[tk])
                if j % 2:
                    P.op("act", lambda h, tp=tp, kk=kk, t=t: h.copy(hT[:, kk:kk + 4, t * 128:(t + 1) * 128], tp[:].rearrange("p (a b) -> p a b", a=4)),
                         reads=[tk], writes=["hT"])
                else:
                    P.op("dve", lambda h, tp=tp, kk=kk, t=t: h.tensor_copy(hT[:, kk:kk + 4, t * 128:(t + 1) * 128], tp[:].rearrange("p (a b) -> p a b", a=4)),
                         reads=[tk], writes=["hT"])
                j += 1
        P.flush()


def inproj(K, hT, jobs):
    nc, P = K.nc, K.P
    with Alloc(nc) as A_:
        iw0 = A_.sb("iw0", [128, NK, 512], BF16)
        iw1 = A_.sb("iw1", [128, NK, 512], BF16)
        ips0 = A_.ps("ips0", [128, 512], F32)
        ips1 = A_.ps("ips1", [128, 512], F32)
        ips2 = A_.ps("ips2", [128, 512], F32)
        ips3 = A_.ps("ips3", [128, 512], F32)
        iw = [iw0, iw1]
        ips = [ips0, ips1, ips2, ips3]
        it = 0
        pj = 0
        for job in jobs:
            ncols = job["ncols"]
            for c0 in range(0, ncols, 512):
                cw = min(512, ncols - c0)
                wb = iw[it % 2]
                wk = "iw%d" % (it % 2)
                it += 1
                src = job["w"][:, c0:c0 + cw].rearrange("(k p) c -> p k c", p=128)
                for q in range(4):
                    P.op("pool", lambda h, wb=wb, src=src, q=q, cw=cw: h.dma_start(out=wb[:, q * 8:(q + 1) * 8, 0:cw], in_=src[:, q * 8:(q + 1) * 8, :]),
                         writes=[wk], dma=True)
                if job["mode"] == "FM":
                    for cc in range(0, cw, 128):
                        m = min(128, cw - cc)
                        for tt in job["tiles"]:
                            ps = ips[pj % 4]
                            pk = "ips%d" % (pj % 4)
                            pj += 1
                            for k in range(NK):
                                P.op("pe", lambda h, ps=ps, wb=wb, k=k, cc=cc, m=m, tt=tt: h.matmul(ps[0:m, :], wb[:, k, cc:cc + m], hT[:, k, tt * 512:(tt + 1) * 512], start=(k == 0), stop=(k == NK - 1)),
                                     reads=[wk, "hT"], writes=[pk])
                            job["evac"](ps, pk, c0 + cc, tt, m)
                else:
                    for tt in job["tiles"]:
                        ps = ips[pj % 4]
                        pk = "ips%d" % (pj % 4)
                        pj += 1
                        for k in range(NK):
                            P.op("pe", lambda h, ps=ps, wb=wb, k=k, cw=cw, tt=tt: h.matmul(ps[:, 0:cw], hT[:, k, tt * 128:(tt + 1) * 128], wb[:, k, 0:cw], start=(k == 0), stop=(k == NK - 1)),
                                 reads=[wk, "hT"], writes=[pk])
                        job["evac"](ps, pk, c0, tt, cw)
        P.flush()


class Stager:
    def __init__(self, tiles, tag):
        self.tiles, self.tag, self.i = tiles, tag, 0

    def next(self):
        n = len(self.tiles)
        t = self.tiles[self.i % n]
        k = "%s%d" % (self.tag, self.i % n)
        self.i += 1
        return t, k


def phase_l0_inproj(K):
    nc, P, I, S = K.nc, K.P, K.I, K.S
    with nc.sbuf_tensor(U("hT"), [128, NK, 2048], BF16) as hT:
        for ps_i in range(2):
            tok0 = ps_i * 2048
            norm_pass(K, I.x[tok0:tok0 + 2048, :], 2048, 0, hT)
            with Alloc(nc) as A_:
                sb0 = A_.sb("sb0", [128, 512], BF16)
                sb1 = A_.sb("sb1", [128, 512], BF16)
                sb2 = A_.sb("sb2", [128, 512], BF16)
                sb3 = A_.sb("sb3", [128, 512], BF16)
                sf0 = A_.sb("sf0", [128, 512], F32)
                sf1 = A_.sb("sf1", [128, 512], F32)
                stb = Stager([sb0, sb1, sb2, sb3], "sb")
                stf = Stager([sf0, sf1], "sf")
                cnt = [0]

                def ev_fm(dst, f32=False):
                    def f(ps, pk, c, tt, m):
                        st, sk = (stf if f32 else stb).next()
                        cnt[0] += 1
                        if cnt[0] % 2 and not f32:
                            P.op("act", lambda h: h.copy(st[0:m, :], ps[0:m, :]), reads=[pk], writes=[sk])
                        else:
                            P.op("dve", lambda h: h.tensor_copy(st[0:m, :], ps[0:m, :]), reads=[pk], writes=[sk])
                        P.op("sp", lambda h: h.dma_start(out=dst[c:c + m, tok0 + tt * 512:tok0 + (tt + 1) * 512], in_=st[0:m, :]), reads=[sk], dma=True)
                    return f

                def ev_tm(dst):
                    def f(ps, pk, c, tt, w):
                        st, sk = stb.next()
                        cnt[0] += 1
                        if cnt[0] % 2:
                            P.op("act", lambda h: h.copy(st[:, 0:w], ps[:, 0:w]), reads=[pk], writes=[sk])
                        else:
                            P.op("dve", lambda h: h.tensor_copy(st[:, 0:w], ps[:, 0:w]), reads=[pk], writes=[sk])
                        P.op("sp", lambda h: h.dma_start(out=dst[tok0 + tt * 128:tok0 + (tt + 1) * 128, c:c + w], in_=st[:, 0:w]), reads=[sk], dma=True)
                    return f

                W = I.w_in0
                if ps_i == 0:
                    jobs = [
                        dict(mode="FM", w=I.w_gates, ncols=32, tiles=range(4), evac=ev_fm(S.gT, True)),
                        dict(mode="FM", w=W[:, C_AQ:C_AQ + 2048], ncols=2048, tiles=range(4), evac=ev_fm(S.aqT)),
                        dict(mode="FM", w=W[:, C_AK:C_AK + 2048], ncols=2048, tiles=range(4), evac=ev_fm(S.akT)),
                        dict(mode="TM", w=W[:, C_AV:C_AV + 4096], ncols=4096, tiles=range(16), evac=ev_tm(S.av)),
                        dict(mode="TM", w=W[:, C_AO:C_AO + 4096], ncols=4096, tiles=range(16), evac=ev_tm(S.ao)),
                        dict(mode="TM", w=W[:, C_AZ:C_AZ + 4096], ncols=4096, tiles=range(16), evac=ev_tm(S.az)),
                        dict(mode="FM", w=W[:, C_BQ:C_BQ + 4096], ncols=4096, tiles=range(4), evac=ev_fm(S.bqT)),
                        dict(mode="FM", w=W[:, C_BK:C_BK + 4096], ncols=4096, tiles=range(4), evac=ev_fm(S.bkT)),
                        dict(mode="TM", w=W[:, C_BV:C_BV + 4096], ncols=4096, tiles=range(16), evac=ev_tm(S.bv)),
                        dict(mode="FM", w=W[:, C_BZ:C_BZ + 4096], ncols=4096, tiles=range(4), evac=ev_fm(S.bzT)),
                    ]
                else:
                    jobs = [
                        dict(mode="FM", w=I.w_gates, ncols=32, tiles=range(4), evac=ev_fm(S.gT, True)),
                        dict(mode="FM", w=W[:, C_AQ:C_AQ + 2048], ncols=2048, tiles=range(1), evac=ev_fm(S.aqT)),
                        dict(mode="FM", w=W[:, C_AK:C_AK + 2048], ncols=2048, tiles=range(4), evac=ev_fm(S.akT)),
                        dict(mode="TM", w=W[:, C_AV:C_AV + 4096], ncols=4096, tiles=range(16), evac=ev_tm(S.av)),
                        dict(mode="FM", w=W[:, C_BK:C_BK + 4096], ncols=4096, tiles=range(1), evac=ev_fm(S.bkT)),
                        dict(mode="TM", w=W[:, C_BV:C_BV + 4096], ncols=4096, tiles=range(4), evac=ev_tm(S.bv)),
                    ]
                inproj(K, hT, jobs)


NQR = 33
NKT = TO + 256


def phase_mixers(K):
    nc, P, I, S = K.nc, K.P, K.I, K.S
    heads = range(1) if K.quick else range(HD)
    with nc.sbuf_tensor(U("Ecol"), [128, 2, NCH, 8], F32) as Ecol:
        with Alloc(nc) as A_:
            gb = A_.sb("gb", [8, 4], F32)
            gi = A_.sb("gi", [8, T], F32)
            gf = A_.sb("gf", [8, T], F32)
            gG = A_.sb("gG", [8, T], F32)
            gbt = A_.sb("gbt", [8, T], F32)
            gM = A_.sb("gM", [8, T], F32)
            gMp = A_.sb("gMp", [8, T], F32)
            gz = A_.sb("gz", [8, T], F32)
            go1 = A_.sb("go1", [8, T], F32)
            go2 = A_.sb("go2", [8, T], F32)
            gE = A_.sb("gE", [8, T], F32)
            gps = A_.ps("gps", [128, 512], F32)
            P.op("sp", lambda h: h.dma_start(out=gb[:], in_=I.gate_b), writes=["gb"], dma=True)
            P.op("pool", lambda h: h.memset(gz[:], 0.0), writes=["gz"])
            for d in range(2):
                rv = (lambda ap: ap[:, ::-1]) if d == 1 else (lambda ap: ap)
                P.op("sp", lambda h, d=d: h.dma_start(out=gi[:], in_=S.gT[16 * d:16 * d + 8, :]), writes=["gi"], dma=True)
                P.op("sp", lambda h, d=d: h.dma_start(out=gf[:], in_=S.gT[16 * d + 8:16 * d + 16, :]), writes=["gf"], dma=True)
                P.op("dve", lambda h, d=d: h.tensor_scalar(gi[:], gi[:], gb[:, 2 * d:2 * d + 1], None, ALU.add), reads=["gi", "gb"], writes=["gi"])
                P.op("dve", lambda h, d=d: h.tensor_scalar(gf[:], gf[:], gb[:, 2 * d + 1:2 * d + 2], None, ALU.add), reads=["gf", "gb"], writes=["gf"])
                P.op("act", lambda h: h.activation(gf[:], gf[:], AF.Exp, scale=-1.0), reads=["gf"], writes=["gf"])
                P.op("act", lambda h: h.activation(gf[:], gf[:], AF.Ln, bias=1.0), reads=["gf"], writes=["gf"])
                P.op("dve", lambda h: h.tensor_scalar(gf[:], gf[:], -1.0, None, ALU.mult), reads=["gf"], writes=["gf"])
                P.op("dve", lambda h, rv=rv: h.tensor_tensor_scan(rv(gG[:, :]), rv(gf[:, :]), rv(gz[:, :]), 0.0, ALU.add, ALU.add), reads=["gf", "gz"], writes=["gG"])
                P.op("dve", lambda h: h.tensor_tensor(gbt[:], gi[:], gG[:], ALU.subtract), reads=["gi", "gG"], writes=["gbt"])
                P.op("dve", lambda h, rv=rv: h.tensor_tensor_scan(rv(gM[:, :]), rv(gbt[:, :]), rv(gbt[:, :]), 0.0, ALU.max, ALU.max), reads=["gbt"], writes=["gM"])
                M3 = gM[:, :].rearrange("p (c t) -> p c t", t=128)
                Mp3 = gMp[:, :].rearrange("p (c t) -> p c t", t=128)
                if d == 0:
                    P.op("pool", lambda h, Mp3=Mp3: h.memset(Mp3[:, 0:1, :], 0.0), writes=["gMp"])
                    P.op("dve", lambda h, M3=M3, Mp3=Mp3: h.tensor_copy(Mp3[:, 1:NCH, :], M3[:, 0:NCH - 1, 127:128].to_broadcast([8, NCH - 1, 128])), reads=["gM"], writes=["gMp"])
                else:
                    P.op("pool", lambda h, Mp3=Mp3: h.memset(Mp3[:, NCH - 1:NCH, :], 0.0), writes=["gMp"])
                    P.op("dve", lambda h, M3=M3, Mp3=Mp3: h.tensor_copy(Mp3[:, 0:NCH - 1, :], M3[:, 1:NCH, 0:1].to_broadcast([8, NCH - 1, 128])), reads=["gM"], writes=["gMp"])
                P.op("dve", lambda h: h.tensor_tensor(go1[:], gMp[:], gM[:], ALU.subtract), reads=["gMp", "gM"], writes=["go1"])
                P.op("act", lambda h: h.activation(go1[:], go1[:], AF.Exp), reads=["go1"], writes=["go1"])
                P.op("sp", lambda h, d=d: h.dma_start(out=S.gsc[d, 0], in_=go1[:]), reads=["go1"], dma=True)
                P.op("dve", lambda h: h.tensor_tensor(go2[:], gbt[:], gMp[:], ALU.subtract), reads=["gbt", "gMp"], writes=["go2"])
                P.op("act", lambda h: h.activation(go2[:], go2[:], AF.Exp), reads=["go2"], writes=["go2"])
                P.op("sp", lambda h, d=d: h.dma_start(out=S.gsc[d, 1], in_=go2[:]), reads=["go2"], dma=True)
                P.op("dve", lambda h: h.tensor_tensor(gE[:], gG[:], gM[:], ALU.add), reads=["gG", "gM"], writes=["gE"])
                P.op("act", lambda h: h.activation(gE[:], gE[:], AF.Exp, scale=-1.0), reads=["gE"], writes=["gE"])
                for c8 in range(0, NCH, 8):
                    for cc in range(8):
                        c = c8 + cc
                        P.op("pe", lambda h, c=c, cc=cc: h.transpose(gps[:, cc * 8:(cc + 1) * 8], gE[:, c * 128:(c + 1) * 128], K.ident_f[0:8, 0:8]),
                             reads=["gE", "ident_f"], writes=["gps"])
                    P.op("dve", lambda h, d=d, c8=c8: h.tensor_copy(Ecol[:, d, c8:c8 + 8, :], gps[:, 0:64].rearrange("p (c h) -> p c h", h=8)), reads=["gps"], writes=["Ecol"])
            P.flush()

        with Alloc(nc) as A_:
            qraw = A_.sb("qraw", [128, 2, TO + 514], BF16)
            kraw = A_.sb("kraw", [128, 2, T + 2], BF16)
            cv = A_.sb("cv", [128, 1024], F32)
            qc = A_.sb("qc", [128, 2, TO], BF16)
            kc = A_.sb("kc", [128, 2, T], BF16)
            vx = A_.sb("vx", [128, NCH, 516], BF16)
            rep = A_.sb("rep", [128, 2, 2, T], BF16)
            cw = A_.sb("cw", [128, 32, 3], F32)
            ang = A_.sb("ang", [128, 512], F32)
            mk = A_.sb("mk", [128, 2, 128], F32)
            Pst = A_.sb("Pst", [128, 2, 512], F32)
            Pn = A_.sb("Pn", [128, 2], F32)
            Cb = A_.sb("Cb", [128, 2, 512], BF16)
            nb = A_.sb("nb", [128, 2], BF16)
            qs = A_.sb("qs", [128, 2, 128], BF16)
            ks = A_.sb("ks", [128, 2, 128], BF16)
            ST = A_.sb("ST", [128, 128], BF16)
            kst = A_.sb("kst", [128, 256], BF16)
            sm = A_.sb("sm", [128, 8], F32)
            hh = A_.sb("hh", [128, 512], F32)
            hfl = A_.sb("hf", [128, 512], F32)
            hsq = A_.sb("hsq", [128, 512], BF16)
            ot = A_.sb("ot", [128, 512], BF16)
            zt = A_.sb("zt", [128, 512], BF16)
            of_ = A_.sb("of", [128, 512], F32)
            zf = A_.sb("zf", [128, 512], F32)
            yb = A_.sb("yb", [128, 512], BF16)
            yTt = A_.sb("yT", [128, 4, 128], BF16)
            p_num = A_.ps("p_num", [128, 512], F32)
            p_dc0 = A_.ps("p_dc0", [128, 512], F32)
            p_dc1 = A_.ps("p_dc1", [128, 512], F32)
            p_dc = [p_dc0, p_dc1]
            P.op("sp", lambda h: h.dma_start(out=cw[:], in_=I.convw), writes=["cw"], dma=True)
            P.op("sp", lambda h: h.dma_start(out=mk[:], in_=I.masks.rearrange("d p t -> p d t")), writes=["mk"], dma=True)
            P.exclusive = {"pmisc", "pbf", "pod"}
            pmisc = A_.ps("pmisc", [128, 512], F32)
            pbf = A_.ps("pbf", [128, 1024], BF16)
            p_st = pmisc[:, 0:128]
            p_den = pmisc[:, 128:136]
            p_dn = pmisc[:, 136:144]
            p_kt = pbf[:, 0:256]
            p_tr = pbf[:, 512:1024]
            QW = NQR * 64
            nqn = A_.sb("nqn", [128, 64 + TO + 64], BF16)
            nqs = A_.sb("nqs", [128, QW], BF16)
            nkn = A_.sb("nkn", [128, NKT], BF16)
            nzs = A_.sb("nzs", [128, TO], BF16)
            ve = A_.sb("ve", [128, 18, 128], BF16)
            vo = A_.sb("vo", [128, 17, 128], BF16)
            EBe = A_.sb("EBe", [128, 7, 64], F32)
            EBo = A_.sb("EBo", [128, 7, 64], F32)
            cm = A_.sb("cm", [128, 64], F32)
            gq = A_.sb("gq", [128, 2], F32)
            ab = A_.sb("ab", [128, 2], F32)
            onesS = A_.sb("onesS", [128, 128], BF16)
            ones1 = A_.sb("ones1", [128, 128], BF16)
            sq = A_.sb("sq", [128, 512], BF16)
            sd = A_.sb("sd", [128, 512], F32)
            pe0 = A_.sb("pe0", [128, 4, 64], F32)
            pe1 = A_.sb("pe1", [128, 4, 64], F32)
            pt0 = A_.sb("pt0", [128, 4, 64], BF16)
            pt1 = A_.sb("pt1", [128, 4, 64], BF16)
            rd = A_.sb("rd", [128, 512], F32)
            Ys = A_.sb("Ys", [128, QW], F32)
            yo = A_.sb("yo", [128, 512], F32)
            yob = A_.sb("yob", [128, 512], BF16)
            pod = A_.ps("pod", [128, 512], F32)
            p_o = pod[:, 0:256]
            p_d = pod[:, 256:512]
            p_ms = pod
            psb0 = A_.ps("psb0", [128, 8, 64], F32)
            psb1 = A_.ps("psb1", [128, 8, 64], F32)
            p_s = [psb0[:, 0:4, :], psb1[:, 0:4, :]]
            pe_ = [pe0, pe1]
            pt_ = [pt0, pt1]
            P.op("sp", lambda h: h.dma_start(out=cm[:], in_=I.cmask), writes=["cm"], dma=True)
            P.op("sp", lambda h: h.dma_start(out=gq[:], in_=I.gqk), writes=["gq"], dma=True)
            P.op("sp", lambda h: h.dma_start(out=ab[:], in_=I.ab), writes=["ab"], dma=True)
            P.op("dve", lambda h: h.tensor_scalar(gq[:, 0:1], gq[:, 0:1], 128.0 ** -0.5, None, ALU.mult), reads=["gq"], writes=["gq"])
            P.op("pool", lambda h: h.memset(onesS[:], 1.0 / 128.0), writes=["onesS"])
            P.op("pool", lambda h: h.memset(ones1[:], 1.0), writes=["ones1"])
            P.op("pool", lambda h: h.memset(nqn[:], 0.0), writes=["nqn"])

            def gen_ml():
                for hd in heads:
                    for (raw, rk, src) in ((qraw, "qraw", S.aqT), (kraw, "kraw", S.akT)):
                        P.op("pool", lambda h, raw=raw: h.memset(raw[:, :, 0:1], 0.0), writes=[rk])
                        if rk == "kraw":
                            P.op("pool", lambda h, raw=raw: h.memset(raw[:, :, T + 1:T + 2], 0.0), writes=[rk])
                        nv = (TO + 512) if rk == "qraw" else T
                        P.op("sp", lambda h, raw=raw, src=src, hd=hd, nv=nv: h.dma_start(out=raw[:, :, 1:nv + 1], in_=src[hd * 256:(hd + 1) * 256, 0:nv].rearrange("(c p) t -> p c t", p=128)),
                             writes=[rk], dma=True)
                    P.op("sp", lambda h, hd=hd: h.dma_start(out=vx[:, :, 0:512], in_=S.av[:, hd * 512:(hd + 1) * 512].rearrange("(c p) v -> p c v", p=128)),
                         writes=["vx"], dma=True)
                    P.op("pool", lambda h: h.memset(vx[:, :, 512:516], 1.0), writes=["vx"])
                    for d in range(2):
                        for j in range(2):
                            P.op("pool", lambda h, d=d, j=j, hd=hd: h.dma_start(out=rep[:, d, j, :], in_=S.gsc[d, j, hd:hd + 1, :].to_broadcast([128, T])),
                                 writes=["rep"], dma=True)
                    P.op("sp", lambda h, hd=hd: h.dma_start(out=ang[:], in_=I.ang_rep[:, hd * 512:(hd + 1) * 512]), writes=["ang"], dma=True)
                    for (raw, rk, dst, dk, cbase) in ((qraw, "qraw", qc, "qc", 0), (kraw, "kraw", kc, "kc", 16)):
                        for dc in range(2):
                            ci = cbase + hd * 2 + dc
                            Lc = TO if rk == "qraw" else T
                            for cb0 in range(0, Lc, 1024):
                                P.op("dve", lambda h, raw=raw, dc=dc, ci=ci, cb0=cb0: h.tensor_scalar(cv[:], raw[:, dc, cb0 + 1:cb0 + 1025], cw[:, ci, 1:2], None, ALU.mult), reads=[rk, "cw"], writes=["cv"])
                                P.op("dve", lambda h, raw=raw, dc=dc, ci=ci, cb0=cb0: h.scalar_tensor_tensor(cv[:], raw[:, dc, cb0:cb0 + 1024], cw[:, ci, 0:1], cv[:], ALU.mult, ALU.add), reads=[rk, "cw", "cv"], writes=["cv"])
                                P.op("dve", lambda h, raw=raw, dc=dc, ci=ci, cb0=cb0: h.scalar_tensor_tensor(cv[:], raw[:, dc, cb0 + 2:cb0 + 1026], cw[:, ci, 2:3], cv[:], ALU.mult, ALU.add), reads=[rk, "cw", "cv"], writes=["cv"])
                                P.op("act", lambda h, dst=dst, dc=dc, cb0=cb0: h.activation(dst[:, dc, cb0:cb0 + 1024], cv[:], AF.Silu), reads=["cv"], writes=[dk])
                                yield
                    for d in range(2):
                        order = range(NOWN) if d == 0 else range(NCH - 1, -1, -1)
                        P.op("pool", lambda h: h.memset(Pst[:], 0.0), writes=["Pst"])
                        P.op("pool", lambda h: h.memset(Pn[:], 0.0), writes=["Pn"])
                        P.op("pool", lambda h: h.memset(Cb[:], 0.0), writes=["Cb"])
                        P.op("pool", lambda h: h.memset(nb[:], 0.0), writes=["nb"])
                        prev_dec = None
                        for c in order:
                            t0 = c * 128
                            ir = rep[:, d, 0, t0:t0 + 128]
                            wr = rep[:, d, 1, t0:t0 + 128]
                            dec_col = (t0 + 127) if d == 0 else t0
                            dec = rep[:, d, 0, dec_col:dec_col + 1]
                            full = c < NOWN
                            if full:
                                P.op("dve", lambda h, ir=ir, t0=t0: h.scalar_tensor_tensor(qs[:], qc[:, :, t0:t0 + 128], 1.0 / 16.0, ir.unsqueeze(1).to_broadcast([128, 2, 128]), ALU.mult, ALU.mult),
                                     reads=["qc", "rep"], writes=["qs"])
                            P.op("pool", lambda h, wr=wr, t0=t0: h.tensor_tensor(ks[:], kc[:, :, t0:t0 + 128], wr.unsqueeze(1).to_broadcast([128, 2, 128]), ALU.mult),
                                 reads=["kc", "rep"], writes=["ks"])
                            yield
                            if full:
                                for dc in range(2):
                                    P.op("pe", lambda h, dc=dc: h.matmul(p_st[:], ks[:, dc, :], qs[:, dc, :], start=(dc == 0), stop=(dc == 1)), reads=["ks", "qs"], writes=["pmisc"])
                                P.op("dve", lambda h, d=d: h.tensor_tensor(ST[:], p_st[:], mk[:, d, :], ALU.mult), reads=["pmisc", "mk"], writes=["ST"])
                            yield
                            for dc in range(2):
                                P.op("pe", lambda h, dc=dc: h.transpose(p_kt[:, dc * 128:(dc + 1) * 128], ks[:, dc, :], K.ident_b[:]), reads=["ks", "ident_b"], writes=["pbf"])
                            P.op("act", lambda h: h.copy(kst[:], p_kt[:]), reads=["pbf"], writes=["kst"])
                            yield
                            if full:
                                P.op("pe", lambda h, c=c: h.matmul(p_num[:], ST[:], vx[:, c, 0:512], start=True, stop=False), reads=["ST", "vx"], writes=["p_num"])
                                for dc in range(2):
                                    P.op("pe", lambda h, dc=dc: h.matmul(p_num[:], qs[:, dc, :], Cb[:, dc, :], start=False, stop=(dc == 1)), reads=["qs", "Cb"], writes=["p_num"])
                                P.op("pe", lambda h, c=c: h.matmul(p_den[:, 0:1], ST[:], vx[:, c, 512:513], start=True, stop=False), reads=["ST", "vx"], writes=["pmisc"])
                                for dc in range(2):
                                    P.op("pe", lambda h, dc=dc: h.matmul(p_den[:, 0:1], qs[:, dc, :], nb[:, dc:dc + 1], start=False, stop=(dc == 1)), reads=["qs", "nb"], writes=["pmisc"])
                                P.op("act", lambda h: h.activation(sm[:, 5:6], p_den[:, 0:1], AF.Abs), reads=["pmisc"], writes=["sm"])
                                yield
                                P.op("dve", lambda h, d=d, c=c, hd=hd: h.tensor_tensor(sm[:, 0:1], sm[:, 5:6], Ecol[:, d, c, hd:hd + 1], ALU.max), reads=["sm", "Ecol"], writes=["sm"])
                                P.op("dve", lambda h: h.reciprocal(sm[:, 1:2], sm[:, 0:1]), reads=["sm"], writes=["sm"])
                                P.op("act", lambda h: h.activation(hh[:], p_num[:], AF.Copy, scale=sm[:, 1:2]), reads=["p_num", "sm"], writes=["hh"])
                                yield
                            for dc in range(2):
                                P.op("pe", lambda h, dc=dc, c=c: h.matmul(p_dc[dc][:], kst[:, dc * 128:(dc + 1) * 128], vx[:, c, 0:512], start=True, stop=True), reads=["kst", "vx"], writes=["p_dc%d" % dc])
                                P.op("pe", lambda h, dc=dc, c=c: h.matmul(p_dn[:, dc:dc + 1], kst[:, dc * 128:(dc + 1) * 128], vx[:, c, 512:513], start=True, stop=True), reads=["kst", "vx"], writes=["pmisc"])
                            pd = prev_dec if prev_dec is not None else 1.0
                            yield
                            for dc in range(2):
                                P.op("dve", lambda h, dc=dc, pd=pd: h.scalar_tensor_tensor(Pst[:, dc, :], Pst[:, dc, :], pd, p_dc[dc][:], ALU.mult, ALU.add), reads=["Pst", "rep", "p_dc%d" % dc], writes=["Pst"])
                                P.op("dve", lambda h, dc=dc, dec=dec: h.tensor_scalar(Cb[:, dc, :], Pst[:, dc, :], dec, None, ALU.mult), reads=["Pst", "rep"], writes=["Cb"])
                            P.op("dve", lambda h, pd=pd: h.scalar_tensor_tensor(Pn[:], Pn[:], pd, p_dn[:, 0:2], ALU.mult, ALU.add), reads=["Pn", "rep", "pmisc"], writes=["Pn"])
                            P.op("dve", lambda h, dec=dec: h.tensor_scalar(nb[:], Pn[:], dec, None, ALU.mult), reads=["Pn", "rep"], writes=["nb"])
                            prev_dec = dec
                            yield
                            if not full:
                                continue
                            if d == 0:
                                P.op("sp", lambda h, t0=t0: h.dma_start(out=S.hfs[t0:t0 + 128, :], in_=hh[:]), reads=["hh"], writes=["hfs_d%d" % c], dma=True)
                            else:
                                P.op("sp", lambda h, t0=t0: h.dma_start(out=hfl[:], in_=S.hfs[t0:t0 + 128, :]), reads=["hfs_d%d" % c], writes=["hf"], dma=True)
                                P.op("sp", lambda h, t0=t0, hd=hd: h.dma_start(out=ot[:], in_=S.ao[t0:t0 + 128, hd * 512:(hd + 1) * 512]), writes=["ot"], dma=True)
                                P.op("sp", lambda h, t0=t0, hd=hd: h.dma_start(out=zt[:], in_=S.az[t0:t0 + 128, hd * 512:(hd + 1) * 512]), writes=["zt"], dma=True)
                                P.op("dve", lambda h: h.tensor_tensor(hh[:], hh[:], hfl[:], ALU.add), reads=["hh", "hf"], writes=["hh"])
                                yield
                                P.op("act", lambda h: h.activation(hsq[:], hh[:], AF.Square), reads=["hh"], writes=["hsq"])
                                P.op("dve", lambda h: h.tensor_reduce(sm[:, 2:3], hsq[:], AX.X, ALU.add), reads=["hsq"], writes=["sm"])
                                P.op("act", lambda h: h.activation(sm[:, 3:4], sm[:, 2:3], AF.Sqrt, bias=K.eps_t[:, 0:1], scale=1.0 / 512), reads=["sm", "eps_t"], writes=["sm"])
                                P.op("dve", lambda h: h.reciprocal(sm[:, 4:5], sm[:, 3:4]), reads=["sm"], writes=["sm"])
                                yield
                                P.op("dve", lambda h: h.scalar_tensor_tensor(hh[:], hh[:], sm[:, 4:5], ang[:], ALU.mult, ALU.mult), reads=["hh", "sm", "ang"], writes=["hh"])
                                P.op("act", lambda h: h.activation(of_[:], ot[:], AF.Sigmoid), reads=["ot"], writes=["of"])
                                P.op("act", lambda h: h.activation(zf[:], zt[:], AF.Silu), reads=["zt"], writes=["zf"])
                                P.op("pool", lambda h: h.tensor_tensor(of_[:], of_[:], zf[:], ALU.mult), reads=["of", "zf"], writes=["of"])
                                P.op("dve", lambda h: h.tensor_tensor(yb[:], hh[:], of_[:], ALU.mult), reads=["hh", "of"], writes=["yb"])
                                yield
                                for q in range(4):
                                    P.op("pe", lambda h, q=q: h.transpose(p_tr[:, q * 128:(q + 1) * 128], yb[:, q * 128:(q + 1) * 128], K.ident_b[:]), reads=["yb", "ident_b"], writes=["pbf"])
                                P.op("act", lambda h: h.copy(yTt[:], p_tr[:].rearrange("p (a b) -> p a b", a=4)), reads=["pbf"], writes=["yTt"])
                                P.op("sp", lambda h, t0=t0, hd=hd: h.dma_start(out=S.yT[0][hd * 512:(hd + 1) * 512, t0:t0 + 128].rearrange("(a p) t -> p a t", p=128), in_=yTt[:]),
                                     reads=["yTt"], dma=True)

            def gen_na():
                heads = range(1) if K.quick else range(NB)
                for hd in heads:
                    r0 = hd * 128
                    P.op("sp", lambda h, r0=r0: h.dma_start(out=nqn[:, 64:64 + TO], in_=S.bqT[r0:r0 + 128, 0:TO]), writes=["nqn"], dma=True)
                    P.op("sp", lambda h, r0=r0: h.dma_start(out=nkn[:], in_=S.bkT[r0:r0 + 128, 0:NKT]), writes=["nkn"], dma=True)
                    P.op("sp", lambda h, r0=r0: h.dma_start(out=nzs[:], in_=S.bzT[r0:r0 + 128, 0:TO]), writes=["nzs"], dma=True)
                    P.op("sp", lambda h, r0=r0: h.dma_start(out=ve[:], in_=S.bv[0:18 * 128, r0:r0 + 128].rearrange("(t p) d -> p t d", p=128)), writes=["ve"], dma=True)
                    P.op("sp", lambda h, r0=r0: h.dma_start(out=vo[:], in_=S.bv[64:64 + 17 * 128, r0:r0 + 128].rearrange("(t p) d -> p t d", p=128)), writes=["vo"], dma=True)
                    for (EB, ek, o0) in ((EBe, "EBe", 0), (EBo, "EBo", 1)):
                        for two in range(2):
                            P.op("sp", lambda h, EB=EB, o0=o0, two=two, hd=hd: h.dma_start(out=EB[two * 64:(two + 1) * 64, :, :], in_=I.B2[hd, o0 + two:o0 + two + 13:2].rearrange("r k q -> k r q")),
                                 writes=[ek], dma=True)
                        P.op("dve", lambda h, EB=EB: h.tensor_tensor(EB[:], EB[:], cm[:].unsqueeze(1).to_broadcast([128, 7, 64]), ALU.add), reads=[ek, "cm"], writes=[ek])
                        P.op("act", lambda h, EB=EB: h.activation(EB[:], EB[:], AF.Exp), reads=[ek], writes=[ek])
                    P.op("act", lambda h: h.activation(nzs[:], nzs[:], AF.Silu), reads=["nzs"], writes=["nzs"])
                    for (dst, dk, gi_, ntok, doff) in ((nqn, "nqn", 0, TO, 64), (nkn, "nkn", 1, NKT, 0)):
                        for t0 in range(0, ntok, 512):
                            w = min(512, ntok - t0)
                            P.op("act", lambda h, dst=dst, t0=t0, w=w, doff=doff: h.activation(sq[:, 0:w], dst[:, doff + t0:doff + t0 + w], AF.Square), reads=[dk], writes=["sq"])
                            P.op("pe", lambda h, w=w: h.matmul(p_ms[:, 0:w], onesS[:], sq[:, 0:w], start=True, stop=True), reads=["onesS", "sq"], writes=["pod"])
                            P.op("act", lambda h, w=w: h.activation(sd[:, 0:w], p_ms[:, 0:w], AF.Sqrt, bias=K.eps_t[:, 0:1]), reads=["pod", "eps_t"], writes=["sd"])
                            P.op("dve", lambda h, w=w: h.reciprocal(sd[:, 0:w], sd[:, 0:w]), reads=["sd"], writes=["sd"])
                            P.op("dve", lambda h, dst=dst, t0=t0, w=w, gi_=gi_, doff=doff: h.scalar_tensor_tensor(dst[:, doff + t0:doff + t0 + w], dst[:, doff + t0:doff + t0 + w], gq[:, gi_:gi_ + 1], sd[:, 0:w], ALU.mult, ALU.mult),
                                 reads=[dk, "gq", "sd"], writes=[dk])
                            yield
                    P.op("dve", lambda h: h.tensor_scalar(nqs[:], nqn[:, 64:64 + QW], ab[:, 0:1], None, ALU.mult), reads=["nqn", "ab"], writes=["nqs"])
                    P.op("dve", lambda h: h.scalar_tensor_tensor(nqs[:], nqn[:, 0:QW], ab[:, 1:2], nqs[:], ALU.mult, ALU.add), reads=["nqn", "ab", "nqs"], writes=["nqs"])
                    for r in range(NQR):
                        rs = max(r - 4, 0)
                        d0 = rs - r + 7
                        EB, ek, j0 = (EBe, "EBe", d0 // 2) if d0 % 2 == 0 else (EBo, "EBo", (d0 - 1) // 2)
                        b = r % 2
                        slot = r % 4
                        for i in range(4):
                            tk0 = (rs + 2 * i) * 64
                            P.op("pe", lambda h, b=b, i=i, tk0=tk0, r=r: h.matmul(p_s[b][:, i, :], nkn[:, tk0:tk0 + 128], nqs[:, r * 64:(r + 1) * 64], start=True, stop=True),
                                 reads=["nkn", "nqs"], writes=["p_s%d" % b])
                        yield
                        P.op("act", lambda h, b=b: h.activation(pe_[b][:], p_s[b][:], AF.Exp), reads=["p_s%d" % b], writes=["pe%d" % b])
                        P.op("dve", lambda h, b=b, EB=EB, j0=j0: h.tensor_tensor(pt_[b][:], pe_[b][:], EB[:, j0:j0 + 4, :], ALU.mult), reads=["pe%d" % b, ek], writes=["pt%d" % b])
                        yield
                        for i in range(4):
                            kr = rs + 2 * i
                            vt = ve[:, kr // 2, :] if kr % 2 == 0 else vo[:, (kr - 1) // 2, :]
                            P.op("pe", lambda h, b=b, i=i, vt=vt, slot=slot: h.matmul(p_o[:, slot * 64:(slot + 1) * 64], vt, pt_[b][:, i, :], start=(i == 0), stop=(i == 3)),
                                 reads=["ve", "vo", "pt%d" % b], writes=["pod"])
                        for i in range(4):
                            P.op("pe", lambda h, b=b, i=i, slot=slot: h.matmul(p_d[:, slot * 64:(slot + 1) * 64], ones1[:], pt_[b][:, i, :], start=(i == 0), stop=(i == 3)),
                                 reads=["ones1", "pt%d" % b], writes=["pod"])
                        yield
                        if slot == 3 or r == NQR - 1:
                            w = (slot + 1) * 64
                            c0 = (r // 4) * 256
                            P.op("dve", lambda h, w=w: h.reciprocal(rd[:, 0:w], p_d[:, 0:w]), reads=["pod"], writes=["rd"])
                            P.op("dve", lambda h, w=w, c0=c0: h.tensor_tensor(Ys[:, c0:c0 + w], p_o[:, 0:w], rd[:, 0:w], ALU.mult), reads=["pod", "rd"], writes=["Ys"])
                    for tb in range(0, TO, 512):
                        P.op("dve", lambda h, tb=tb: h.tensor_scalar(yo[:], Ys[:, tb:tb + 512], ab[:, 0:1], None, ALU.mult), reads=["Ys", "ab"], writes=["yo"])
                        P.op("dve", lambda h, tb=tb: h.scalar_tensor_tensor(yo[:], Ys[:, tb + 64:tb + 64 + 512], ab[:, 1:2], yo[:], ALU.mult, ALU.add), reads=["Ys", "ab", "yo"], writes=["yo"])
                        P.op("pool", lambda h, tb=tb: h.tensor_tensor(yob[:], yo[:], nzs[:, tb:tb + 512], ALU.mult), reads=["yo", "nzs"], writes=["yob"])
                        P.op("sp", lambda h, tb=tb, r0=r0: h.dma_start(out=S.yT[0][D + r0:D + r0 + 128, tb:tb + 512], in_=yob[:]), reads=["yob"], dma=True)
                        yield

            gens = [gen_ml(), gen_na()]
            while gens:
                for g in list(gens):
                    try:
                        next(g)
                    except StopIteration:
                        gens.remove(g)
            P.flush()
            P.exclusive = set()


def phase_outproj(K, layer, x_src, dst):
    nc, P, I, S = K.nc, K.P, K.I, K.S
    NKO = CW // 128
    with Alloc(nc) as A_:
        oy = A_.sb("oy", [128, NKO, 512], BF16)
        ow0 = A_.sb("ow0", [128, NKO, 512], BF16)
        ow1 = A_.sb("ow1", [128, NKO, 512], BF16)
        og0 = A_.sb("og0", [128, 512], F32)
        og1 = A_.sb("og1", [128, 512], F32)
        ox0 = A_.sb("ox0", [128, 512], F32)
        ox1 = A_.sb("ox1", [128, 512], F32)
        oo0 = A_.sb("oo0", [128, 512], F32)
        oo1 = A_.sb("oo1", [128, 512], F32)
        ops0 = A_.ps("ops0", [128, 512], F32)
        ops1 = A_.ps("ops1", [128, 512], F32)
        ow = [ow0, ow1]
        og = [og0, og1]
        ox = [ox0, ox1]
        oo = [oo0, oo1]
        ops = [ops0, ops1]
        it = 0
        pj = 0
        ntb = 1 if K.quick else TO // 512
        for tb in range(ntb):
            for q in range(4):
                P.op("sp", lambda h, tb=tb, q=q: h.dma_start(out=oy[:, q * 16:(q + 1) * 16, :], in_=S.yT[layer][q * 2048:(q + 1) * 2048, tb * 512:(tb + 1) * 512].rearrange("(k p) t -> p k t", p=128)),
                     writes=["oy"], dma=True)
            for cb in range(D // 512):
                wb = ow[it % 2]
                wk = "ow%d" % (it % 2)
                gt_ = og[it % 2]
                gk = "og%d" % (it % 2)
                it += 1
                src = I.w_out[layer][:, cb * 512:(cb + 1) * 512].rearrange("(k p) c -> p k c", p=128)
                for q in range(4):
                    P.op("pool", lambda h, wb=wb, src=src, q=q: h.dma_start(out=wb[:, q * 16:(q + 1) * 16, :], in_=src[:, q * 16:(q + 1) * 16, :]), writes=[wk], dma=True)
                P.op("sp", lambda h, gt_=gt_, cb=cb: h.dma_start(out=gt_[:], in_=S.modrep[layer][2, :, cb * 512:(cb + 1) * 512]), writes=[gk], dma=True)
                for tt in range(4):
                    ps = ops[pj % 2]
                    pk = "ops%d" % (pj % 2)
                    xt = ox[pj % 2]
                    xk = "ox%d" % (pj % 2)
                    ot_ = oo[pj % 2]
                    okk = "oo%d" % (pj % 2)
                    pj += 1
                    tok = tb * 512 + tt * 128
                    for k in range(NKO):
                        P.op("pe", lambda h, ps=ps, wb=wb, k=k, tt=tt: h.matmul(ps[:], oy[:, k, tt * 128:(tt + 1) * 128], wb[:, k, :], start=(k == 0), stop=(k == NKO - 1)),
                             reads=["oy", wk], writes=[pk])
                    P.op("sp", lambda h, xt=xt, tok=tok, cb=cb: h.dma_start(out=xt[:], in_=x_src[tok:tok + 128, cb * 512:(cb + 1) * 512]), writes=[xk], dma=True)
                    P.op("dve", lambda h, ps=ps, ot_=ot_, gt_=gt_: h.tensor_tensor(ot_[:], ps[:], gt_[:], ALU.mult), reads=[pk, gk], writes=[okk])
                    P.op("pool", lambda h, ot_=ot_, xt=xt: h.tensor_tensor(ot_[:], ot_[:], xt[:], ALU.add), reads=[okk, xk], writes=[okk])
                    P.op("sp", lambda h, ot_=ot_, tok=tok, cb=cb: h.dma_start(out=dst[tok:tok + 128, cb * 512:(cb + 1) * 512], in_=ot_[:]), reads=[okk], dma=True)
        P.flush()


def phase_l1_inproj(K):
    nc, P, I, S = K.nc, K.P, K.I, K.S
    with nc.sbuf_tensor(U("hT1"), [128, NK, 2048], BF16) as hT:
        for ps_i in range(1):
            tok0 = ps_i * 2048
            norm_pass(K, S.x1[tok0:tok0 + 2048, :], 2048, 1, hT)
            with Alloc(nc) as A_:
                lb0 = A_.sb("lb0", [128, 512], BF16)
                lb1 = A_.sb("lb1", [128, 512], BF16)
                lb2 = A_.sb("lb2", [128, 512], BF16)
                lb3 = A_.sb("lb3", [128, 512], BF16)
                lf0 = A_.sb("lf0", [128, 512], F32)
                lf1 = A_.sb("lf1", [128, 512], F32)
                lf2 = A_.sb("lf2", [128, 512], F32)
                lf3 = A_.sb("lf3", [128, 512], F32)
                stb = Stager([lb0, lb1, lb2, lb3], "lb")
                stf = Stager([lf0, lf1, lf2, lf3], "lf")

                def ev_gelu(dst):
                    def f(ps, pk, c, tt, w):
                        st, sk = stb.next()
                        t1, k1 = stf.next()
                        P.op("act", lambda h: h.activation(t1[:], ps[:], AF.Square), reads=[pk], writes=[k1])
                        P.op("dve", lambda h: h.tensor_scalar(t1[:], t1[:], 0.044715, 1.0, ALU.mult, ALU.add), reads=[k1], writes=[k1])
                        P.op("dve", lambda h: h.tensor_tensor(t1[:], t1[:], ps[:], ALU.mult), reads=[k1, pk], writes=[k1])
                        P.op("act", lambda h: h.activation(t1[:], t1[:], AF.Sigmoid, scale=1.5957691216057308), reads=[k1], writes=[k1])
                        P.op("dve", lambda h: h.tensor_tensor(st[:], t1[:], ps[:], ALU.mult), reads=[k1, pk], writes=[sk])
                        P.op("sp", lambda h: h.dma_start(out=dst[tok0 + tt * 128:tok0 + (tt + 1) * 128, c:c + w], in_=st[:, 0:w]), reads=[sk], dma=True)
                    return f

                def ev_silu(dst):
                    def f(ps, pk, c, tt, w):
                        st, sk = stb.next()
                        P.op("act", lambda h: h.activation(st[:], ps[:], AF.Silu), reads=[pk], writes=[sk])
                        P.op("sp", lambda h: h.dma_start(out=dst[tok0 + tt * 128:tok0 + (tt + 1) * 128, c:c + w], in_=st[:, 0:w]), reads=[sk], dma=True)
                    return f

                W = I.w_in1
                nt = range(1) if K.quick else range(16)
                jobs = [
                    dict(mode="TM", w=W[:, 0:CW], ncols=CW, tiles=nt, evac=ev_gelu(S.gu)),
                    dict(mode="TM", w=W[:, CW:2 * CW], ncols=CW, tiles=nt, evac=ev_gelu(S.gv)),
                    dict(mode="TM", w=W[:, 2 * CW:3 * CW], ncols=CW, tiles=nt, evac=ev_silu(S.sz)),
                ]
                inproj(K, hT, jobs)


def phase_gmlp(K):
    nc, P, I, S = K.nc, K.P, K.I, K.S
    with Alloc(nc) as A_:
        vg = A_.sb("vg", [128, CW], F32)
        wsf = A_.sb("wsf", [128, 8, 128], F32)
        wsb = A_.sb("wsb", [128, 8, 128], BF16)
        bsf = A_.sb("bsf", [128, 8], F32)
        gvt = A_.sb("gvt", [128, CW], BF16)
        gvs = A_.sb("gvs", [128, CW], BF16)
        gvn = A_.sb("gvn", [128, CW], BF16)
        gut = A_.sb("gut", [128, CW], BF16)
        szt = A_.sb("szt", [128, CW], BF16)
        ych = A_.sb("ych", [128, CW], BF16)
        yTc = A_.sb("yTc", [128, 64, 128], BF16)
        gt0 = A_.sb("gt0", [128, 512], F32)
        gt1 = A_.sb("gt1", [128, 512], F32)
        gsm = A_.sb("gsm", [128, 4], F32)
        gp0 = A_.ps("gp0", [128, 512], F32)
        gp1 = A_.ps("gp1", [128, 512], F32)
        gtr0 = A_.ps("gtr0", [128, 512], BF16)
        gtr1 = A_.ps("gtr1", [128, 512], BF16)
        gp = [gp0, gp1]
        gt = [gt0, gt1]
        gtr = [gtr0, gtr1]
        P.op("sp", lambda h: h.dma_start(out=vg[:], in_=I.vg_rep), writes=["vg"], dma=True)
        P.op("sp", lambda h: h.dma_start(out=wsf[:], in_=I.WsT.rearrange("g s t -> s g t")), writes=["wsf"], dma=True)
        P.op("dve", lambda h: h.tensor_copy(wsb[:], wsf[:]), reads=["wsf"], writes=["wsb"])
        P.op("sp", lambda h: h.dma_start(out=bsf[:], in_=I.bs_fm), writes=["bsf"], dma=True)
        nchunks = 1 if K.quick else NOWN
        pj = 0
        for c in range(nchunks):
            t0 = c * 128
            P.op("sp", lambda h, t0=t0: h.dma_start(out=gvt[:], in_=S.gv[t0:t0 + 128, :]), writes=["gvt"], dma=True)
            P.op("sp", lambda h, t0=t0: h.dma_start(out=gut[:], in_=S.gu[t0:t0 + 128, :]), writes=["gut"], dma=True)
            P.op("sp", lambda h, t0=t0: h.dma_start(out=szt[:], in_=S.sz[t0:t0 + 128, :]), writes=["szt"], dma=True)
            P.op("act", lambda h: h.activation(gvs[:], gvt[:], AF.Square), reads=["gvt"], writes=["gvs"])
            P.op("dve", lambda h: h.tensor_reduce(gsm[:, 0:1], gvs[:], AX.X, ALU.add), reads=["gvs"], writes=["gsm"])
            P.op("act", lambda h: h.activation(gsm[:, 1:2], gsm[:, 0:1], AF.Sqrt, bias=K.eps_t[:, 0:1], scale=1.0 / CW), reads=["gsm", "eps_t"], writes=["gsm"])
            P.op("dve", lambda h: h.reciprocal(gsm[:, 2:3], gsm[:, 1:2]), reads=["gsm"], writes=["gsm"])
            P.op("dve", lambda h: h.scalar_tensor_tensor(gvn[:], gvt[:], gsm[:, 2:3], vg[:], ALU.mult, ALU.mult), reads=["gvt", "gsm", "vg"], writes=["gvn"])
            for g in range(8):
                for hh_ in range(2):
                    cs_ = g * 1024 + hh_ * 512
                    ps = gp[pj % 2]
                    pk = "gp%d" % (pj % 2)
                    tt_ = gt[pj % 2]
                    tk = "gt%d" % (pj % 2)
                    pj += 1
                    P.op("pe", lambda h, ps=ps, g=g, cs_=cs_: h.matmul(ps[:], wsb[:, g, :], gvn[:, cs_:cs_ + 512], start=True, stop=True), reads=["wsb", "gvn"], writes=[pk])
                    P.op("dve", lambda h, ps=ps, tt_=tt_, g=g, cs_=cs_: h.scalar_tensor_tensor(tt_[:], ps[:], bsf[:, g:g + 1], gut[:, cs_:cs_ + 512], ALU.add, ALU.mult), reads=[pk, "bsf", "gut"], writes=[tk])
                    P.op("pool", lambda h, tt_=tt_, cs_=cs_: h.tensor_tensor(ych[:, cs_:cs_ + 512], tt_[:], szt[:, cs_:cs_ + 512], ALU.mult), reads=[tk, "szt"], writes=["ych"])
            for k4 in range(0, 64, 4):
                tp = gtr[(k4 // 4) % 2]
                tk = "gtr%d" % ((k4 // 4) % 2)
                for q in range(4):
                    P.op("pe", lambda h, tp=tp, q=q, k4=k4: h.transpose(tp[:, q * 128:(q + 1) * 128], ych[:, (k4 + q) * 128:(k4 + q + 1) * 128], K.ident_b[:]), reads=["ych", "ident_b"], writes=[tk])
                if (k4 // 4) % 2:
                    P.op("act", lambda h, tp=tp, k4=k4: h.copy(yTc[:, k4:k4 + 4, :], tp[:].rearrange("p (a b) -> p a b", a=4)), reads=[tk], writes=["yTc"])
                else:
                    P.op("dve", lambda h, tp=tp, k4=k4: h.tensor_copy(yTc[:, k4:k4 + 4, :], tp[:].rearrange("p (a b) -> p a b", a=4)), reads=[tk], writes=["yTc"])
            for q in range(4):
                P.op("sp", lambda h, t0=t0, q=q: h.dma_start(out=S.yT[1][q * 2048:(q + 1) * 2048, t0:t0 + 128].rearrange("(k p) t -> p k t", p=128), in_=yTc[:, q * 16:(q + 1) * 16, :]),
                     reads=["yTc"], dma=True)
        P.flush()


def prep_core(inp, b, hf):
    rev = hf == 1
    norm_g = [inp["norm_g0"], inp["norm_g1"]]
    ada_w = [inp["ada_w0"], inp["ada_w1"]]
    ada_b = [inp["ada_b0"], inp["ada_b1"]]
    w_out = [inp["w_out0"], inp["w_out1"]]
    m = {}
    xb = inp["x"][b]
    m["x"] = np.ascontiguousarray(xb[::-1] if rev else xb)
    m["c_fm"] = np.ascontiguousarray(inp["c"][b].reshape(NK, 128).T)
    for l in range(2):
        m["g_rep%d" % l] = np.ascontiguousarray(np.broadcast_to(norm_g[l][None, :], (128, D)))
        m["ada_w%d" % l] = ada_w[l]
        m["ada_b_rep%d" % l] = np.ascontiguousarray(np.broadcast_to(ada_b[l][None, :], (128, 3 * D)))
        m["w_out%d" % l] = w_out[l]
    m["w_in0"] = inp["w_in0"]
    m["w_in1"] = inp["w_in1"]
    m["ident"] = np.eye(128, dtype=np.float32)
    gperm = np.arange(32)
    if rev:
        gperm = np.concatenate([np.arange(16, 32), np.arange(0, 16)])
    m["w_gates"] = np.ascontiguousarray(inp["w_in0"][:, C_G:C_G + 32][:, gperm])
    m["gate_b"] = np.ascontiguousarray(inp["a_gate_b"][gperm].reshape(4, 8).T)
    cw = inp["a_conv_w"][::-1] if rev else inp["a_conv_w"]
    m["convw"] = np.ascontiguousarray(cw.T.reshape(32, 128, 3).transpose(1, 0, 2))
    m["ang_rep"] = np.ascontiguousarray(np.broadcast_to(inp["a_norm_g"].reshape(1, D), (128, D)))
    m["gqk"] = np.ascontiguousarray(np.stack([inp["b_q_gain"], inp["b_k_gain"]], axis=1))
    kc = np.arange(64)[:, None]
    qc = np.arange(64)[None, :]
    dci = np.clip(kc - qc + 15, 0, 30)
    rpb = inp["b_rpb"]
    if not rev:
        m["B2"] = np.ascontiguousarray(rpb[:, :, dci])
        cs = np.clip(qc - 8, 0, 48)
    else:
        dru = np.clip(13 - np.arange(15), 0, 14)
        m["B2"] = np.ascontiguousarray(rpb[:, dru][:, :, 30 - dci])
        cs = np.clip(qc - 7, 0, 48)
    cmask = np.where((kc >= cs) & (kc < cs + 16), 0.0, -30000.0).astype(np.float32)
    m["cmask"] = np.ascontiguousarray(np.concatenate([cmask, cmask], axis=0))
    s_ = np.arange(128)[:, None]
    t_ = np.arange(128)[None, :]
    m["masks"] = np.stack([(s_ <= t_), (s_ >= t_)]).astype(np.float32)
    m["ab"] = np.ascontiguousarray(np.broadcast_to(np.array([[0.0, 1.0]] if rev else [[1.0, 0.0]], np.float32), (128, 2)))
    m["vg_rep"] = np.ascontiguousarray(np.broadcast_to(inp["c_v_norm_g"][None, :], (128, CW)))
    ws = inp["c_w_s"]
    bs = inp["c_b_s"]
    if rev:
        ws = ws[:, ::-1, ::-1]
        bs = bs[:, ::-1]
    m["WsT"] = np.ascontiguousarray(ws.transpose(0, 2, 1))
    m["bs_fm"] = np.ascontiguousarray(bs.T)
    return m


def kernel(**inputs):
    inp = {k: np.asarray(v) for k, v in inputs.items()}
    nc = build()
    in_maps = [prep_core(inp, c // 2, c % 2) for c in range(8)]
    res = run_bass_kernel_spmd(nc, in_maps, core_ids=list(range(8)))
    out = np.empty((4, T, D), np.float32)
    for c in range(8):
        b, hf = c // 2, c % 2
        o = res.results[c]["out"]
        if hf == 0:
            out[b, :TO] = o
        else:
            out[b, TO:] = o[::-1]
    return out
```

```python
import numpy as np
from contextlib import ExitStack
import concourse.bass as bass
import concourse.mybir as mybir
from concourse.bass_utils import run_bass_kernel_spmd

F32 = mybir.dt.float32
BF16 = mybir.dt.bfloat16
AF = mybir.ActivationFunctionType
ALU = mybir.AluOpType
AX = mybir.AxisListType

D = 4096
TH = 2048
NK = 32
L0C = 32800
EPS = 1e-6
N_DMA_SEMS = 24
SAFE_SAME_ENGINE = True

C_AQ, C_AK, C_AV, C_AO, C_AZ, C_G, C_BQ, C_BK, C_BV, C_BZ = 0, 2048, 4096, 8192, 12288, 16384, 16416, 20512, 24608, 28704


class Prog:
    def __init__(self, nc):
        self.nc = nc
        self.eng = {"pe": nc.tensor, "act": nc.scalar, "dve": nc.vector, "pool": nc.gpsimd, "sp": nc.sync}
        self.esem = {e: nc.alloc_semaphore("es_" + e) for e in self.eng}
        self.ecnt = {e: 0 for e in self.eng}
        self.dsem = [nc.alloc_semaphore("ds%d" % i) for i in range(N_DMA_SEMS)]
        self.dcnt = [0] * N_DMA_SEMS
        self.dnext = 0
        self.seen = {e: {} for e in self.eng}
        self.ops = []
        self.last_writer = {}
        self.readers = {}
        self.barrier_set = []
        self.n_emitted = 0
        self.exclusive = set()

    def op(self, eng, fn, reads=(), writes=(), dma=False):
        idx = len(self.ops)
        deps = set()
        if self.exclusive:
            ex = [b for b in reads if b in self.exclusive]
            if ex:
                reads = [b for b in reads if b not in self.exclusive]
                writes = list(writes) + ex
        for b in reads:
            w = self.last_writer.get(b)
            if w is not None:
                deps.add(w)
        for b in writes:
            w = self.last_writer.get(b)
            if w is not None:
                deps.add(w)
            for r in self.readers.get(b, {}).values():
                deps.add(r)
        for b in reads:
            d = self.readers.setdefault(b, {})
            key = ("dma", idx) if dma else eng
            d[key] = idx
        for b in writes:
            self.last_writer[b] = idx
            self.readers[b] = {}
        deps.discard(idx)
        self.ops.append(dict(eng=eng, fn=fn, deps=deps, dma=dma, sig=None))
        return idx

    def flush(self):
        ops = self.ops
        need = [False] * len(ops)
        for o in ops:
            for d in o["deps"]:
                p = ops[d]
                if p["dma"]:
                    need[d] = True
                elif p["eng"] == o["eng"] and not o["dma"]:
                    if p["eng"] != "pe" and SAFE_SAME_ENGINE:
                        need[d] = True
                else:
                    need[d] = True
        last = {}
        for i, o in enumerate(ops):
            if o["dma"]:
                need[i] = True
            else:
                last[o["eng"]] = i
        for i in last.values():
            need[i] = True
        for i, o in enumerate(ops):
            e = o["eng"]
            h = self.eng[e]
            waits = list(self.barrier_set)
            for d in sorted(o["deps"]):
                p = ops[d]
                if not need[d]:
                    continue
                if (not p["dma"]) and p["eng"] == e and not o["dma"] and (e == "pe" or not SAFE_SAME_ENGINE):
                    continue
                waits.append(p["sig"])
            seen = self.seen[e]
            for (k, sem, val) in waits:
                if seen.get(k, 0) >= val:
                    continue
                h.wait_ge(sem, val)
                seen[k] = val
            ins = o["fn"](h)
            if need[i]:
                if o["dma"]:
                    s = self.dnext
                    self.dnext = (self.dnext + 1) % N_DMA_SEMS
                    self.dcnt[s] += 16
                    ins.then_inc(self.dsem[s], 16)
                    o["sig"] = (("d", s), self.dsem[s], self.dcnt[s])
                else:
                    self.ecnt[e] += 1
                    ins.then_inc(self.esem[e], 1)
                    o["sig"] = (("e", e), self.esem[e], self.ecnt[e])
            self.n_emitted += 1
        bs = {}
        for o in ops:
            if o["sig"] is not None:
                k, sem, val = o["sig"]
                if k not in bs or bs[k][2] < val:
                    bs[k] = (k, sem, val)
        for (k, sem, val) in self.barrier_set:
            if k not in bs or bs[k][2] < val:
                bs[k] = (k, sem, val)
        self.barrier_set = list(bs.values())
        self.ops = []
        self.last_writer = {}
        self.readers = {}

    def final_wait(self):
        for e, h in self.eng.items():
            seen = self.seen[e]
            for (k, sem, val) in self.barrier_set:
                if seen.get(k, 0) >= val:
                    continue
                h.wait_ge(sem, val)
                seen[k] = val


class Ctx:
    pass


class Alloc:
    def __init__(self, nc):
        self.nc = nc
        self.es = ExitStack()

    def __enter__(self):
        self.es.__enter__()
        return self

    def __exit__(self, *a):
        return self.es.__exit__(*a)

    def sb(self, name, shape, dt):
        return self.es.enter_context(self.nc.sbuf_tensor(U(name), shape, dt))

    def ps(self, name, shape, dt):
        return self.es.enter_context(self.nc.psum_tensor(U(name), shape, dt))


_UC = [0]


def U(name):
    _UC[0] += 1
    return "%s_%d" % (name, _UC[0])


T = 4096
TO = 2048
NOWN = 16
NCH = 32
HD = 8
NB = 32
CW = 8192


def build(taps=(), stop_after=None, quick=False):
    nc = bass.Bass("TRN2", target_bir_lowering=False)
    P = Prog(nc)
    K = Ctx()
    K.nc, K.P, K.taps, K.quick = nc, P, set(taps), quick

    order = ["mod", "in0", "na", "out0", "in1", "gmlp", "out1"]
    last = order.index(stop_after) if stop_after else len(order) - 1
    K.in_shapes = {}

    def din(name, shape, dt=F32, first="mod"):
        if order.index(first) > last:
            shape = [128, 128]
        K.in_shapes[name] = tuple(shape)
        return nc.dram_tensor(name, list(shape), dt, kind="ExternalInput").ap()

    def dscr(name, shape, dt=BF16):
        kind = "ExternalOutput" if name in K.taps else "Internal"
        return nc.dram_tensor(name, list(shape), dt, kind=kind).ap()

    I = Ctx()
    K.I = I
    I.x = din("x", [T, D])
    I.c_fm = din("c_fm", [128, NK])
    I.g_rep = [din("g_rep%d" % l, [128, D]) for l in range(2)]
    I.ada_w = [din("ada_w%d" % l, [D, 3 * D]) for l in range(2)]
    I.ada_b_rep = [din("ada_b_rep%d" % l, [128, 3 * D]) for l in range(2)]
    I.w_in0 = din("w_in0", [D, L0C], first="in0")
    I.ident = din("ident", [128, 128])
    I.w_gates = din("w_gates", [D, 32])
    I.ab = din("ab", [128, 2])
    I.convw = din("convw", [128, 32, 3])
    I.gate_b = din("gate_b", [8, 4])
    I.ang_rep = din("ang_rep", [128, D])
    I.gqk = din("gqk", [128, 2])
    I.B2 = din("B2", [NB, 15, 64, 64])
    I.cmask = din("cmask", [128, 64])
    I.masks = din("masks", [2, 128, 128])
    I.w_out = [din("w_out%d" % l, [CW, D], first=("out0", "out1")[l]) for l in range(2)]
    I.w_in1 = din("w_in1", [D, 3 * CW], first="in1")
    I.vg_rep = din("vg_rep", [128, CW])
    I.WsT = din("WsT", [8, 128, 128])
    I.bs_fm = din("bs_fm", [128, 8])
    K.out = nc.dram_tensor("out", [TO, D], F32, kind="ExternalOutput").ap()

    S = Ctx()
    K.S = S
    S.modrep = [dscr("modrep%d" % l, [3, 128, D], F32) for l in range(2)]
    S.aqT = dscr("aqT", [2048, T])
    S.akT = dscr("akT", [2048, T])
    S.av = dscr("av", [T, D])
    S.ao = dscr("ao", [T, D])
    S.az = dscr("az", [T, D])
    S.gT = dscr("gT", [32, T], F32)
    S.bqT = dscr("bqT", [D, T])
    S.bzT = dscr("bzT", [D, T])
    S.bkT = dscr("bkT", [D, T])
    S.bv = dscr("bv", [T, D])
    S.gsc = dscr("gsc", [2, 2, 8, T], F32)
    S.hfs = dscr("hfs", [TO, 512], F32)
    S.yT = [dscr("yT%d" % l, [CW, TO]) for l in range(2)]
    S.x1 = dscr("x1", [TO, D], F32)
    S.gu = dscr("gu", [TO, CW])
    S.gv = dscr("gv", [TO, CW])
    S.sz = dscr("sz", [TO, CW])

    with Alloc(nc) as A_:
        ident_f = A_.sb("ident_f", [128, 128], F32)
        ident_b = A_.sb("ident_b", [128, 128], BF16)
        eps_t = A_.sb("eps_t", [128, 1], F32)
        K.ident_f, K.ident_b, K.eps_t = ident_f, ident_b, eps_t
        P.op("sp", lambda h: h.dma_start(out=ident_f[:], in_=I.ident), writes=["ident_f"], dma=True)
        P.op("dve", lambda h: h.tensor_copy(ident_b[:], ident_f[:]), reads=["ident_f"], writes=["ident_b"])
        P.op("pool", lambda h: h.memset(eps_t[:], EPS), writes=["eps_t"])
        P.flush()
        stages = [
            ("mod", lambda: phase_mod(K)),
            ("in0", lambda: phase_l0_inproj(K)),
            ("na", lambda: phase_mixers(K)),
            ("out0", lambda: phase_outproj(K, 0, I.x, S.x1)),
            ("in1", lambda: phase_l1_inproj(K)),
            ("gmlp", lambda: phase_gmlp(K)),
            ("out1", lambda: phase_outproj(K, 1, S.x1, K.out)),
        ]
        for name, fn in stages:
            fn()
            if stop_after == name:
                break
    K.P.final_wait()
    nc._in_shapes = K.in_shapes
    return nc


def phase_mod(K):
    nc, P, I, S = K.nc, K.P, K.I, K.S
    with Alloc(nc) as A_:
        cf = A_.sb("cf", [128, NK], F32)
        cs = A_.sb("cs", [128, NK], F32)
        csb = A_.sb("csb", [128, NK, 128], BF16)
        mw0 = A_.sb("mw0", [128, NK, 512], BF16)
        mw1 = A_.sb("mw1", [128, NK, 512], BF16)
        mb = A_.sb("mb", [128, 512], F32)
        mg = A_.sb("mg", [128, 512], F32)
        mo0 = A_.sb("mo0", [128, 512], F32)
        mo1 = A_.sb("mo1", [128, 512], F32)
        mps0 = A_.ps("mps0", [128, 512], F32)
        mps1 = A_.ps("mps1", [128, 512], F32)
        mw = [mw0, mw1]
        mo = [mo0, mo1]
        mps = [mps0, mps1]
        P.op("sp", lambda h: h.dma_start(out=cf[:], in_=I.c_fm), writes=["cf"], dma=True)
        P.op("act", lambda h: h.activation(cs[:], cf[:], AF.Silu), reads=["cf"], writes=["cs"])
        P.op("dve", lambda h: h.tensor_copy(csb[:], cs[:].unsqueeze(2).to_broadcast([128, NK, 128])), reads=["cs"], writes=["csb"])
        it = 0
        for l in range(2):
            for n in range(24):
                wb = mw[it % 2]
                wk = "mw%d" % (it % 2)
                src = I.ada_w[l][:, n * 512:(n + 1) * 512].rearrange("(k p) c -> p k c", p=128)
                for q in range(4):
                    P.op("pool", lambda h, wb=wb, src=src, q=q: h.dma_start(out=wb[:, q * 8:(q + 1) * 8, :], in_=src[:, q * 8:(q + 1) * 8, :]),
                         writes=[wk], dma=True)
                ps = mps[it % 2]
                pk = "mps%d" % (it % 2)
                for k in range(NK):
                    P.op("pe", lambda h, ps=ps, wb=wb, k=k: h.matmul(ps[:], csb[:, k, :], wb[:, k, :], start=(k == 0), stop=(k == NK - 1)),
                         reads=[wk, "csb"], writes=[pk])
                P.op("sp", lambda h, l=l, n=n: h.dma_start(out=mb[:], in_=I.ada_b_rep[l][:, n * 512:(n + 1) * 512]), writes=["mb"], dma=True)
                o = mo[it % 2]
                ok = "mo%d" % (it % 2)
                which = n // 8
                cb = (n % 8) * 512
                P.op("dve", lambda h, ps=ps, o=o: h.tensor_tensor(o[:], ps[:], mb[:], ALU.add), reads=[pk, "mb"], writes=[ok])
                if which == 1:
                    P.op("sp", lambda h, l=l, cb=cb: h.dma_start(out=mg[:], in_=I.g_rep[l][:, cb:cb + 512]), writes=["mg"], dma=True)
                    P.op("dve", lambda h, o=o: h.scalar_tensor_tensor(o[:], o[:], 1.0, mg[:], ALU.add, ALU.mult), reads=[ok, "mg"], writes=[ok])
                P.op("sp", lambda h, o=o, l=l, which=which, cb=cb: h.dma_start(out=S.modrep[l][which, :, cb:cb + 512], in_=o[:]),
                     reads=[ok], dma=True)
                it += 1
        P.flush()


def norm_pass(K, x_dram, ntok, layer, hT):
    nc, P, S = K.nc, K.P, K.S
    ntile = ntok // 128
    with Alloc(nc) as A_:
        nG = A_.sb("nG", [128, D], F32)
        nS = A_.sb("nS", [128, D], F32)
        nx = A_.sb("nx", [128, D], F32)
        nsq = A_.sb("nsq", [128, D], BF16)
        nh = A_.sb("nh", [128, D], BF16)
        nss = A_.sb("nss", [128, 4], F32)
        ntp0 = A_.ps("ntp0", [128, 512], BF16)
        ntp1 = A_.ps("ntp1", [128, 512], BF16)
        ntp = [ntp0, ntp1]
        eps_t = K.eps_t
        P.op("sp", lambda h: h.dma_start(out=nS[:], in_=S.modrep[layer][0]), writes=["nS"], dma=True)
        P.op("sp", lambda h: h.dma_start(out=nG[:], in_=S.modrep[layer][1]), writes=["nG"], dma=True)
        j = 0
        for t in range(ntile):
            P.op("sp", lambda h, t=t: h.dma_start(out=nx[:], in_=x_dram[t * 128:(t + 1) * 128, :]), writes=["nx"], dma=True)
            P.op("act", lambda h: h.activation(nsq[:], nx[:], AF.Square), reads=["nx"], writes=["nsq"])
            P.op("dve", lambda h: h.tensor_reduce(nss[:, 0:1], nsq[:], AX.X, ALU.add), reads=["nsq"], writes=["nss"])
            P.op("act", lambda h: h.activation(nss[:, 1:2], nss[:, 0:1], AF.Sqrt, bias=eps_t[:, 0:1], scale=1.0 / D), reads=["nss", "eps_t"], writes=["nss"])
            P.op("dve", lambda h: h.reciprocal(nss[:, 2:3], nss[:, 1:2]), reads=["nss"], writes=["nss"])
            P.op("dve", lambda h: h.scalar_tensor_tensor(nx[:], nx[:], nss[:, 2:3], nG[:], ALU.mult, ALU.mult), reads=["nx", "nss", "nG"], writes=["nx"])
            P.op("pool", lambda h: h.tensor_tensor(nh[:], nx[:], nS[:], ALU.add), reads=["nx", "nS"], writes=["nh"])
            for kk in range(0, NK, 4):
                tp = ntp[j % 2]
                tk = "ntp%d" % (j % 2)
                for q in range(4):
                    P.op("pe", lambda h, tp=tp, q=q, kk=kk: h.transpose(tp[:, q * 128:(q + 1) * 128], nh[:, (kk + q) * 128:(kk + q + 1) * 128], K.ident_b[:]),
                         reads=["nh", "ident_b"], writes=[tk])
                if j % 2:
                    P.op("act", lambda h, tp=tp, kk=kk, t=t: h.copy(hT[:, kk:kk + 4, t * 128:(t + 1) * 128], tp[:].rearrange("p (a b) -> p a b", a=4)),
                         reads=[tk], writes=["hT"])
                else:
                    P.op("dve", lambda h, tp=tp, kk=kk, t=t: h.tensor_copy(hT[:, kk:kk + 4, t * 128:(t + 1) * 128], tp[:].rearrange("p (a b) -> p a b", a=4)),
                         reads=[tk], writes=["hT"])
                j += 1
        P.flush()


def inproj(K, hT, jobs):
    nc, P = K.nc, K.P
    with Alloc(nc) as A_:
        iw0 = A_.sb("iw0", [128, NK, 512], BF16)
        iw1 = A_.sb("iw1", [128, NK, 512], BF16)
        ips0 = A_.ps("ips0", [128, 512], F32)
        ips1 = A_.ps("ips1", [128, 512], F32)
        ips2 = A_.ps("ips2", [128, 512], F32)
        ips3 = A_.ps("ips3", [128, 512], F32)
        iw = [iw0, iw1]
        ips = [ips0, ips1, ips2, ips3]
        it = 0
        pj = 0
        for job in jobs:
            ncols = job["ncols"]
            for c0 in range(0, ncols, 512):
                cw = min(512, ncols - c0)
                wb = iw[it % 2]
                wk = "iw%d" % (it % 2)
                it += 1
                src = job["w"][:, c0:c0 + cw].rearrange("(k p) c -> p k c", p=128)
                for q in range(4):
                    P.op("pool", lambda h, wb=wb, src=src, q=q, cw=cw: h.dma_start(out=wb[:, q * 8:(q + 1) * 8, 0:cw], in_=src[:, q * 8:(q + 1) * 8, :]),
                         writes=[wk], dma=True)
                if job["mode"] == "FM":
                    for cc in range(0, cw, 128):
                        m = min(128, cw - cc)
                        for tt in job["tiles"]:
                            ps = ips[pj % 4]
                            pk = "ips%d" % (pj % 4)
                            pj += 1
                            for k in range(NK):
                                P.op("pe", lambda h, ps=ps, wb=wb, k=k, cc=cc, m=m, tt=tt: h.matmul(ps[0:m, :], wb[:, k, cc:cc + m], hT[:, k, tt * 512:(tt + 1) * 512], start=(k == 0), stop=(k == NK - 1)),
                                     reads=[wk, "hT"], writes=[pk])
                            job["evac"](ps, pk, c0 + cc, tt, m)
                else:
                    for tt in job["tiles"]:
                        ps = ips[pj % 4]
                        pk = "ips%d" % (pj % 4)
                        pj += 1
                        for k in range(NK):
                            P.op("pe", lambda h, ps=ps, wb=wb, k=k, cw=cw, tt=tt: h.matmul(ps[:, 0:cw], hT[:, k, tt * 128:(tt + 1) * 128], wb[:, k, 0:cw], start=(k == 0), stop=(k == NK - 1)),
                                 reads=[wk, "hT"], writes=[pk])
                        job["evac"](ps, pk, c0, tt, cw)
        P.flush()


class Stager:
    def __init__(self, tiles, tag):
        self.tiles, self.tag, self.i = tiles, tag, 0

    def next(self):
        n = len(self.tiles)
        t = self.tiles[self.i % n]
        k = "%s%d" % (self.tag, self.i % n)
        self.i += 1
        return t, k


def phase_l0_inproj(K):
    nc, P, I, S = K.nc, K.P, K.I, K.S
    with nc.sbuf_tensor(U("hT"), [128, NK, 2048], BF16) as hT:
        for ps_i in range(2):
            tok0 = ps_i * 2048
            norm_pass(K, I.x[tok0:tok0 + 2048, :], 2048, 0, hT)
            with Alloc(nc) as A_:
                sb0 = A_.sb("sb0", [128, 512], BF16)
                sb1 = A_.sb("sb1", [128, 512], BF16)
                sb2 = A_.sb("sb2", [128, 512], BF16)
                sb3 = A_.sb("sb3", [128, 512], BF16)
                sf0 = A_.sb("sf0", [128, 512], F32)
                sf1 = A_.sb("sf1", [128, 512], F32)
                stb = Stager([sb0, sb1, sb2, sb3], "sb")
                stf = Stager([sf0, sf1], "sf")
                cnt = [0]

                def ev_fm(dst, f32=False):
                    def f(ps, pk, c, tt, m):
                        st, sk = (stf if f32 else stb).next()
                        cnt[0] += 1
                        if cnt[0] % 2 and not f32:
                            P.op("act", lambda h: h.copy(st[0:m, :], ps[0:m, :]), reads=[pk], writes=[sk])
                        else:
                            P.op("dve", lambda h: h.tensor_copy(st[0:m, :], ps[0:m, :]), reads=[pk], writes=[sk])
                        P.op("sp", lambda h: h.dma_start(out=dst[c:c + m, tok0 + tt * 512:tok0 + (tt + 1) * 512], in_=st[0:m, :]), reads=[sk], dma=True)
                    return f

                def ev_tm(dst):
                    def f(ps, pk, c, tt, w):
                        st, sk = stb.next()
                        cnt[0] += 1
                        if cnt[0] % 2:
                            P.op("act", lambda h: h.copy(st[:, 0:w], ps[:, 0:w]), reads=[pk], writes=[sk])
                        else:
                            P.op("dve", lambda h: h.tensor_copy(st[:, 0:w], ps[:, 0:w]), reads=[pk], writes=[sk])
                        P.op("sp", lambda h: h.dma_start(out=dst[tok0 + tt * 128:tok0 + (tt + 1) * 128, c:c + w], in_=st[:, 0:w]), reads=[sk], dma=True)
                    return f

                W = I.w_in0
                if ps_i == 0:
                    jobs = [
                        dict(mode="FM", w=I.w_gates, ncols=32, tiles=range(4), evac=ev_fm(S.gT, True)),
                        dict(mode="FM", w=W[:, C_AQ:C_AQ + 2048], ncols=2048, tiles=range(4), evac=ev_fm(S.aqT)),
                        dict(mode="FM", w=W[:, C_AK:C_AK + 2048], ncols=2048, tiles=range(4), evac=ev_fm(S.akT)),
                        dict(mode="TM", w=W[:, C_AV:C_AV + 4096], ncols=4096, tiles=range(16), evac=ev_tm(S.av)),
                        dict(mode="TM", w=W[:, C_AO:C_AO + 4096], ncols=4096, tiles=range(16), evac=ev_tm(S.ao)),
                        dict(mode="TM", w=W[:, C_AZ:C_AZ + 4096], ncols=4096, tiles=range(16), evac=ev_tm(S.az)),
                        dict(mode="FM", w=W[:, C_BQ:C_BQ + 4096], ncols=4096, tiles=range(4), evac=ev_fm(S.bqT)),
                        dict(mode="FM", w=W[:, C_BK:C_BK + 4096], ncols=4096, tiles=range(4), evac=ev_fm(S.bkT)),
                        dict(mode="TM", w=W[:, C_BV:C_BV + 4096], ncols=4096, tiles=range(16), evac=ev_tm(S.bv)),
                        dict(mode="FM", w=W[:, C_BZ:C_BZ + 4096], ncols=4096, tiles=range(4), evac=ev_fm(S.bzT)),
                    ]
                else:
                    jobs = [
                        dict(mode="FM", w=I.w_gates, ncols=32, tiles=range(4), evac=ev_fm(S.gT, True)),
                        dict(mode="FM", w=W[:, C_AQ:C_AQ + 2048], ncols=2048, tiles=range(1), evac=ev_fm(S.aqT)),
                        dict(mode="FM", w=W[:, C_AK:C_AK + 2048], ncols=2048, tiles=range(4), evac=ev_fm(S.akT)),
                        dict(mode="TM", w=W[:, C_AV:C_AV + 4096], ncols=4096, tiles=range(16), evac=ev_tm(S.av)),
                        dict(mode="FM", w=W[:, C_BK:C_BK + 4096], ncols=4096, tiles=range(1), evac=ev_fm(S.bkT)),
                        dict(mode="TM", w=W[:, C_BV:C_BV + 4096], ncols=4096, tiles=range(4), evac=ev_tm(S.bv)),
                    ]
                inproj(K, hT, jobs)


NQR = 33
NKT = TO + 256


def phase_mixers(K):
    nc, P, I, S = K.nc, K.P, K.I, K.S
    heads = range(1) if K.quick else range(HD)
    with nc.sbuf_tensor(U("Ecol"), [128, 2, NCH, 8], F32) as Ecol:
        with Alloc(nc) as A_:
            gb = A_.sb("gb", [8, 4], F32)
            gi = A_.sb("gi", [8, T], F32)
            gf = A_.sb("gf", [8, T], F32)
            gG = A_.sb("gG", [8, T], F32)
            gbt = A_.sb("gbt", [8, T], F32)
            gM = A_.sb("gM", [8, T], F32)
            gMp = A_.sb("gMp", [8, T], F32)
            gz = A_.sb("gz", [8, T], F32)
            go1 = A_.sb("go1", [8, T], F32)
            go2 = A_.sb("go2", [8, T], F32)
            gE = A_.sb("gE", [8, T], F32)
            gps = A_.ps("gps", [128, 512], F32)
            P.op("sp", lambda h: h.dma_start(out=gb[:], in_=I.gate_b), writes=["gb"], dma=True)
            P.op("pool", lambda h: h.memset(gz[:], 0.0), writes=["gz"])
            for d in range(2):
                rv = (lambda ap: ap[:, ::-1]) if d == 1 else (lambda ap: ap)
                P.op("sp", lambda h, d=d: h.dma_start(out=gi[:], in_=S.gT[16 * d:16 * d + 8, :]), writes=["gi"], dma=True)
                P.op("sp", lambda h, d=d: h.dma_start(out=gf[:], in_=S.gT[16 * d + 8:16 * d + 16, :]), writes=["gf"], dma=True)
                P.op("dve", lambda h, d=d: h.tensor_scalar(gi[:], gi[:], gb[:, 2 * d:2 * d + 1], None, ALU.add), reads=["gi", "gb"], writes=["gi"])
                P.op("dve", lambda h, d=d: h.tensor_scalar(gf[:], gf[:], gb[:, 2 * d + 1:2 * d + 2], None, ALU.add), reads=["gf", "gb"], writes=["gf"])
                P.op("act", lambda h: h.activation(gf[:], gf[:], AF.Exp, scale=-1.0), reads=["gf"], writes=["gf"])
                P.op("act", lambda h: h.activation(gf[:], gf[:], AF.Ln, bias=1.0), reads=["gf"], writes=["gf"])
                P.op("dve", lambda h: h.tensor_scalar(gf[:], gf[:], -1.0, None, ALU.mult), reads=["gf"], writes=["gf"])
                P.op("dve", lambda h, rv=rv: h.tensor_tensor_scan(rv(gG[:, :]), rv(gf[:, :]), rv(gz[:, :]), 0.0, ALU.add, ALU.add), reads=["gf", "gz"], writes=["gG"])
                P.op("dve", lambda h: h.tensor_tensor(gbt[:], gi[:], gG[:], ALU.subtract), reads=["gi", "gG"], writes=["gbt"])
                P.op("dve", lambda h, rv=rv: h.tensor_tensor_scan(rv(gM[:, :]), rv(gbt[:, :]), rv(gbt[:, :]), 0.0, ALU.max, ALU.max), reads=["gbt"], writes=["gM"])
                M3 = gM[:, :].rearrange("p (c t) -> p c t", t=128)
                Mp3 = gMp[:, :].rearrange("p (c t) -> p c t", t=128)
                if d == 0:
                    P.op("pool", lambda h, Mp3=Mp3: h.memset(Mp3[:, 0:1, :], 0.0), writes=["gMp"])
                    P.op("dve", lambda h, M3=M3, Mp3=Mp3: h.tensor_copy(Mp3[:, 1:NCH, :], M3[:, 0:NCH - 1, 127:128].to_broadcast([8, NCH - 1, 128])), reads=["gM"], writes=["gMp"])
                else:
                    P.op("pool", lambda h, Mp3=Mp3: h.memset(Mp3[:, NCH - 1:NCH, :], 0.0), writes=["gMp"])
                    P.op("dve", lambda h, M3=M3, Mp3=Mp3: h.tensor_copy(Mp3[:, 0:NCH - 1, :], M3[:, 1:NCH, 0:1].to_broadcast([8, NCH - 1, 128])), reads=["gM"], writes=["gMp"])
                P.op("dve", lambda h: h.tensor_tensor(go1[:], gMp[:], gM[:], ALU.subtract), reads=["gMp", "gM"], writes=["go1"])
                P.op("act", lambda h: h.activation(go1[:], go1[:], AF.Exp), reads=["go1"], writes=["go1"])
                P.op("sp", lambda h, d=d: h.dma_start(out=S.gsc[d, 0], in_=go1[:]), reads=["go1"], dma=True)
                P.op("dve", lambda h: h.tensor_tensor(go2[:], gbt[:], gMp[:], ALU.subtract), reads=["gbt", "gMp"], writes=["go2"])
                P.op("act", lambda h: h.activation(go2[:], go2[:], AF.Exp), reads=["go2"], writes=["go2"])
                P.op("sp", lambda h, d=d: h.dma_start(out=S.gsc[d, 1], in_=go2[:]), reads=["go2"], dma=True)
                P.op("dve", lambda h: h.tensor_tensor(gE[:], gG[:], gM[:], ALU.add), reads=["gG", "gM"], writes=["gE"])
                P.op("act", lambda h: h.activation(gE[:], gE[:], AF.Exp, scale=-1.0), reads=["gE"], writes=["gE"])
                for c8 in range(0, NCH, 8):
                    for cc in range(8):
                        c = c8 + cc
                        P.op("pe", lambda h, c=c, cc=cc: h.transpose(gps[:, cc * 8:(cc + 1) * 8], gE[:, c * 128:(c + 1) * 128], K.ident_f[0:8, 0:8]),
                             reads=["gE", "ident_f"], writes=["gps"])
                    P.op("dve", lambda h, d=d, c8=c8: h.tensor_copy(Ecol[:, d, c8:c8 + 8, :], gps[:, 0:64].rearrange("p (c h) -> p c h", h=8)), reads=["gps"], writes=["Ecol"])
            P.flush()

        with Alloc(nc) as A_:
            qraw = A_.sb("qraw", [128, 2, TO + 514], BF16)
            kraw = A_.sb("kraw", [128, 2, T + 2], BF16)
            cv = A_.sb("cv", [128, 1024], F32)
            qc = A_.sb("qc", [128, 2, TO], BF16)
            kc = A_.sb("kc", [128, 2, T], BF16)
            vx = A_.sb("vx", [128, NCH, 516], BF16)
            rep = A_.sb("rep", [128, 2, 2, T], BF16)
            cw = A_.sb("cw", [128, 32, 3], F32)
            ang = A_.sb("ang", [128, 512], F32)
            mk = A_.sb("mk", [128, 2, 128], F32)
            Pst = A_.sb("Pst", [128, 2, 512], F32)
            Pn = A_.sb("Pn", [128, 2], F32)
            Cb = A_.sb("Cb", [128, 2, 512], BF16)
            nb = A_.sb("nb", [128, 2], BF16)
            qs = A_.sb("qs", [128, 2, 128], BF16)
            ks = A_.sb("ks", [128, 2, 128], BF16)
            ST = A_.sb("ST", [128, 128], BF16)
            kst = A_.sb("kst", [128, 256], BF16)
            sm = A_.sb("sm", [128, 8], F32)
            hh = A_.sb("hh", [128, 512], F32)
            hfl = A_.sb("hf", [128, 512], F32)
            hsq = A_.sb("hsq", [128, 512], BF16)
            ot = A_.sb("ot", [128, 512], BF16)
            zt = A_.sb("zt", [128, 512], BF16)
            of_ = A_.sb("of", [128, 512], F32)
            zf = A_.sb("zf", [128, 512], F32)
            yb = A_.sb("yb", [128, 512], BF16)
            yTt = A_.sb("yT", [128, 4, 128], BF16)
            p_num = A_.ps("p_num", [128, 512], F32)
            p_dc0 = A_.ps("p_dc0", [128, 512], F32)
            p_dc1 = A_.ps("p_dc1", [128, 512], F32)
            p_dc = [p_dc0, p_dc1]
            P.op("sp", lambda h: h.dma_start(out=cw[:], in_=I.convw), writes=["cw"], dma=True)
            P.op("sp", lambda h: h.dma_start(out=mk[:], in_=I.masks.rearrange("d p t -> p d t")), writes=["mk"], dma=True)
            P.exclusive = {"pmisc", "pbf", "pod"}
            pmisc = A_.ps("pmisc", [128, 512], F32)
            pbf = A_.ps("pbf", [128, 1024], BF16)
            p_st = pmisc[:, 0:128]
            p_den = pmisc[:, 128:136]
            p_dn = pmisc[:, 136:144]
            p_kt = pbf[:, 0:256]
            p_tr = pbf[:, 512:1024]
            QW = NQR * 64
            nqn = A_.sb("nqn", [128, 64 + TO + 64], BF16)
            nqs = A_.sb("nqs", [128, QW], BF16)
            nkn = A_.sb("nkn", [128, NKT], BF16)
            nzs = A_.sb("nzs", [128, TO], BF16)
            ve = A_.sb("ve", [128, 18, 128], BF16)
            vo = A_.sb("vo", [128, 17, 128], BF16)
            EBe = A_.sb("EBe", [128, 7, 64], F32)
            EBo = A_.sb("EBo", [128, 7, 64], F32)
            cm = A_.sb("cm", [128, 64], F32)
            gq = A_.sb("gq", [128, 2], F32)
            ab = A_.sb("ab", [128, 2], F32)
            onesS = A_.sb("onesS", [128, 128], BF16)
            ones1 = A_.sb("ones1", [128, 128], BF16)
            sq = A_.sb("sq", [128, 512], BF16)
            sd = A_.sb("sd", [128, 512], F32)
            pe0 = A_.sb("pe0", [128, 4, 64], F32)
            pe1 = A_.sb("pe1", [128, 4, 64], F32)
            pt0 = A_.sb("pt0", [128, 4, 64], BF16)
            pt1 = A_.sb("pt1", [128, 4, 64], BF16)
            rd = A_.sb("rd", [128, 512], F32)
            Ys = A_.sb("Ys", [128, QW], F32)
            yo = A_.sb("yo", [128, 512], F32)
            yob = A_.sb("yob", [128, 512], BF16)
            pod = A_.ps("pod", [128, 512], F32)
            p_o = pod[:, 0:256]
            p_d = pod[:, 256:512]
            p_ms = pod
            psb0 = A_.ps("psb0", [128, 8, 64], F32)
            psb1 = A_.ps("psb1", [128, 8, 64], F32)
            p_s = [psb0[:, 0:4, :], psb1[:, 0:4, :]]
            pe_ = [pe0, pe1]
            pt_ = [pt0, pt1]
            P.op("sp", lambda h: h.dma_start(out=cm[:], in_=I.cmask), writes=["cm"], dma=True)
            P.op("sp", lambda h: h.dma_start(out=gq[:], in_=I.gqk), writes=["gq"], dma=True)
            P.op("sp", lambda h: h.dma_start(out=ab[:], in_=I.ab), writes=["ab"], dma=True)
            P.op("dve", lambda h: h.tensor_scalar(gq[:, 0:1], gq[:, 0:1], 128.0 ** -0.5, None, ALU.mult), reads=["gq"], writes=["gq"])
            P.op("pool", lambda h: h.memset(onesS[:], 1.0 / 128.0), writes=["onesS"])
            P.op("pool", lambda h: h.memset(ones1[:], 1.0), writes=["ones1"])
            P.op("pool", lambda h: h.memset(nqn[:], 0.0), writes=["nqn"])

            def gen_ml():
                for hd in heads:
                    for (raw, rk, src) in ((qraw, "qraw", S.aqT), (kraw, "kraw", S.akT)):
                        P.op("pool", lambda h, raw=raw: h.memset(raw[:, :, 0:1], 0.0), writes=[rk])
                        if rk == "kraw":
                            P.op("pool", lambda h, raw=raw: h.memset(raw[:, :, T + 1:T + 2], 0.0), writes=[rk])
                        nv = (TO + 512) if rk == "qraw" else T
                        P.op("sp", lambda h, raw=raw, src=src, hd=hd, nv=nv: h.dma_start(out=raw[:, :, 1:nv + 1], in_=src[hd * 256:(hd + 1) * 256, 0:nv].rearrange("(c p) t -> p c t", p=128)),
                             writes=[rk], dma=True)
                    P.op("sp", lambda h, hd=hd: h.dma_start(out=vx[:, :, 0:512], in_=S.av[:, hd * 512:(hd + 1) * 512].rearrange("(c p) v -> p c v", p=128)),
                         writes=["vx"], dma=True)
                    P.op("pool", lambda h: h.memset(vx[:, :, 512:516], 1.0), writes=["vx"])
                    for d in range(2):
                        for j in range(2):
                            P.op("pool", lambda h, d=d, j=j, hd=hd: h.dma_start(out=rep[:, d, j, :], in_=S.gsc[d, j, hd:hd + 1, :].to_broadcast([128, T])),
                                 writes=["rep"], dma=True)
                    P.op("sp", lambda h, hd=hd: h.dma_start(out=ang[:], in_=I.ang_rep[:, hd * 512:(hd + 1) * 512]), writes=["ang"], dma=True)
                    for (raw, rk, dst, dk, cbase) in ((qraw, "qraw", qc, "qc", 0), (kraw, "kraw", kc, "kc", 16)):
                        for dc in range(2):
                            ci = cbase + hd * 2 + dc
                            Lc = TO if rk == "qraw" else T
                            for cb0 in range(0, Lc, 1024):
                                P.op("dve", lambda h, raw=raw, dc=dc, ci=ci, cb0=cb0: h.tensor_scalar(cv[:], raw[:, dc, cb0 + 1:cb0 + 1025], cw[:, ci, 1:2], None, ALU.mult), reads=[rk, "cw"], writes=["cv"])
                                P.op("dve", lambda h, raw=raw, dc=dc, ci=ci, cb0=cb0: h.scalar_tensor_tensor(cv[:], raw[:, dc, cb0:cb0 + 1024], cw[:, ci, 0:1], cv[:], ALU.mult, ALU.add), reads=[rk, "cw", "cv"], writes=["cv"])
                                P.op("dve", lambda h, raw=raw, dc=dc, ci=ci, cb0=cb0: h.scalar_tensor_tensor(cv[:], raw[:, dc, cb0 + 2:cb0 + 1026], cw[:, ci, 2:3], cv[:], ALU.mult, ALU.add), reads=[rk, "cw", "cv"], writes=["cv"])
                                P.op("act", lambda h, dst=dst, dc=dc, cb0=cb0: h.activation(dst[:, dc, cb0:cb0 + 1024], cv[:], AF.Silu), reads=["cv"], writes=[dk])
                                yield
                    for d in range(2):
                        order = range(NOWN) if d == 0 else range(NCH - 1, -1, -1)
                        P.op("pool", lambda h: h.memset(Pst[:], 0.0), writes=["Pst"])
                        P.op("pool", lambda h: h.memset(Pn[:], 0.0), writes=["Pn"])
                        P.op("pool", lambda h: h.memset(Cb[:], 0.0), writes=["Cb"])
                        P.op("pool", lambda h: h.memset(nb[:], 0.0), writes=["nb"])
                        prev_dec = None
                        for c in order:
                            t0 = c * 128
                            ir = rep[:, d, 0, t0:t0 + 128]
                            wr = rep[:, d, 1, t0:t0 + 128]
                            dec_col = (t0 + 127) if d == 0 else t0
                            dec = rep[:, d, 0, dec_col:dec_col + 1]
                            full = c < NOWN
                            if full:
                                P.op("dve", lambda h, ir=ir, t0=t0: h.scalar_tensor_tensor(qs[:], qc[:, :, t0:t0 + 128], 1.0 / 16.0, ir.unsqueeze(1).to_broadcast([128, 2, 128]), ALU.mult, ALU.mult),
                                     reads=["qc", "rep"], writes=["qs"])
                            P.op("pool", lambda h, wr=wr, t0=t0: h.tensor_tensor(ks[:], kc[:, :, t0:t0 + 128], wr.unsqueeze(1).to_broadcast([128, 2, 128]), ALU.mult),
                                 reads=["kc", "rep"], writes=["ks"])
                            yield
                            if full:
                                for dc in range(2):
                                    P.op("pe", lambda h, dc=dc: h.matmul(p_st[:], ks[:, dc, :], qs[:, dc, :], start=(dc == 0), stop=(dc == 1)), reads=["ks", "qs"], writes=["pmisc"])
                                P.op("dve", lambda h, d=d: h.tensor_tensor(ST[:], p_st[:], mk[:, d, :], ALU.mult), reads=["pmisc", "mk"], writes=["ST"])
                            yield
                            for dc in range(2):
                                P.op("pe", lambda h, dc=dc: h.transpose(p_kt[:, dc * 128:(dc + 1) * 128], ks[:, dc, :], K.ident_b[:]), reads=["ks", "ident_b"], writes=["pbf"])
                            P.op("act", lambda h: h.copy(kst[:], p_kt[:]), reads=["pbf"], writes=["kst"])
                            yield
                            if full:
                                P.op("pe", lambda h, c=c: h.matmul(p_num[:], ST[:], vx[:, c, 0:512], start=True, stop=False), reads=["ST", "vx"], writes=["p_num"])
                                for dc in range(2):
                                    P.op("pe", lambda h, dc=dc: h.matmul(p_num[:], qs[:, dc, :], Cb[:, dc, :], start=False, stop=(dc == 1)), reads=["qs", "Cb"], writes=["p_num"])
                                P.op("pe", lambda h, c=c: h.matmul(p_den[:, 0:1], ST[:], vx[:, c, 512:513], start=True, stop=False), reads=["ST", "vx"], writes=["pmisc"])
                                for dc in range(2):
                                    P.op("pe", lambda h, dc=dc: h.matmul(p_den[:, 0:1], qs[:, dc, :], nb[:, dc:dc + 1], start=False, stop=(dc == 1)), reads=["qs", "nb"], writes=["pmisc"])
                                P.op("act", lambda h: h.activation(sm[:, 5:6], p_den[:, 0:1], AF.Abs), reads=["pmisc"], writes=["sm"])
                                yield
                                P.op("dve", lambda h, d=d, c=c, hd=hd: h.tensor_tensor(sm[:, 0:1], sm[:, 5:6], Ecol[:, d, c, hd:hd + 1], ALU.max), reads=["sm", "Ecol"], writes=["sm"])
                                P.op("dve", lambda h: h.reciprocal(sm[:, 1:2], sm[:, 0:1]), reads=["sm"], writes=["sm"])
                                P.op("act", lambda h: h.activation(hh[:], p_num[:], AF.Copy, scale=sm[:, 1:2]), reads=["p_num", "sm"], writes=["hh"])
                                yield
                            for dc in range(2):
                                P.op("pe", lambda h, dc=dc, c=c: h.matmul(p_dc[dc][:], kst[:, dc * 128:(dc + 1) * 128], vx[:, c, 0:512], start=True, stop=True), reads=["kst", "vx"], writes=["p_dc%d" % dc])
                                P.op("pe", lambda h, dc=dc, c=c: h.matmul(p_dn[:, dc:dc + 1], kst[:, dc * 128:(dc + 1) * 128], vx[:, c, 512:513], start=True, stop=True), reads=["kst", "vx"], writes=["pmisc"])
                            pd = prev_dec if prev_dec is not None else 1.0
                            yield
                            for dc in range(2):
                                P.op("dve", lambda h, dc=dc, pd=pd: h.scalar_tensor_tensor(Pst[:, dc, :], Pst[:, dc, :], pd, p_dc[dc][:], ALU.mult, ALU.add), reads=["Pst", "rep", "p_dc%d" % dc], writes=["Pst"])
                                P.op("dve", lambda h, dc=dc, dec=dec: h.tensor_scalar(Cb[:, dc, :], Pst[:, dc, :], dec, None, ALU.mult), reads=["Pst", "rep"], writes=["Cb"])
                            P.op("dve", lambda h, pd=pd: h.scalar_tensor_tensor(Pn[:], Pn[:], pd, p_dn[:, 0:2], ALU.mult, ALU.add), reads=["Pn", "rep", "pmisc"], writes=["Pn"])
                            P.op("dve", lambda h, dec=dec: h.tensor_scalar(nb[:], Pn[:], dec, None, ALU.mult), reads=["Pn", "rep"], writes=["nb"])
                            prev_dec = dec
                            yield
                            if not full:
                                continue
                            if d == 0:
                                P.op("sp", lambda h, t0=t0: h.dma_start(out=S.hfs[t0:t0 + 128, :], in_=hh[:]), reads=["hh"], writes=["hfs_d%d" % c], dma=True)
                            else:
                                P.op("sp", lambda h, t0=t0: h.dma_start(out=hfl[:], in_=S.hfs[t0:t0 + 128, :]), reads=["hfs_d%d" % c], writes=["hf"], dma=True)
                                P.op("sp", lambda h, t0=t0, hd=hd: h.dma_start(out=ot[:], in_=S.ao[t0:t0 + 128, hd * 512:(hd + 1) * 512]), writes=["ot"], dma=True)
                                P.op("sp", lambda h, t0=t0, hd=hd: h.dma_start(out=zt[:], in_=S.az[t0:t0 + 128, hd * 512:(hd + 1) * 512]), writes=["zt"], dma=True)
                                P.op("dve", lambda h: h.tensor_tensor(hh[:], hh[:], hfl[:], ALU.add), reads=["hh", "hf"], writes=["hh"])
                                yield
                                P.op("act", lambda h: h.activation(hsq[:], hh[:], AF.Square), reads=["hh"], writes=["hsq"])
                                P.op("dve", lambda h: h.tensor_reduce(sm[:, 2:3], hsq[:], AX.X, ALU.add), reads=["hsq"], writes=["sm"])
                                P.op("act", lambda h: h.activation(sm[:, 3:4], sm[:, 2:3], AF.Sqrt, bias=K.eps_t[:, 0:1], scale=1.0 / 512), reads=["sm", "eps_t"], writes=["sm"])
                                P.op("dve", lambda h: h.reciprocal(sm[:, 4:5], sm[:, 3:4]), reads=["sm"], writes=["sm"])
                                yield
                                P.op("dve", lambda h: h.scalar_tensor_tensor(hh[:], hh[:], sm[:, 4:5], ang[:], ALU.mult, ALU.mult), reads=["hh", "sm", "ang"], writes=["hh"])
                                P.op("act", lambda h: h.activation(of_[:], ot[:], AF.Sigmoid), reads=["ot"], writes=["of"])
                                P.op("act", lambda h: h.activation(zf[:], zt[:], AF.Silu), reads=["zt"], writes=["zf"])
                                P.op("pool", lambda h: h.tensor_tensor(of_[:], of_[:], zf[:], ALU.mult), reads=["of", "zf"], writes=["of"])
                                P.op("dve", lambda h: h.tensor_tensor(yb[:], hh[:], of_[:], ALU.mult), reads=["hh", "of"], writes=["yb"])
                                yield
                                for q in range(4):
                                    P.op("pe", lambda h, q=q: h.transpose(p_tr[:, q * 128:(q + 1) * 128], yb[:, q * 128:(q + 1) * 128], K.ident_b[:]), reads=["yb", "ident_b"], writes=["pbf"])
                                P.op("act", lambda h: h.copy(yTt[:], p_tr[:].rearrange("p (a b) -> p a b", a=4)), reads=["pbf"], writes=["yTt"])
                                P.op("sp", lambda h, t0=t0, hd=hd: h.dma_start(out=S.yT[0][hd * 512:(hd + 1) * 512, t0:t0 + 128].rearrange("(a p) t -> p a t", p=128), in_=yTt[:]),
                                     reads=["yTt"], dma=True)

            def gen_na():
                heads = range(1) if K.quick else range(NB)
                for hd in heads:
                    r0 = hd * 128
                    P.op("sp", lambda h, r0=r0: h.dma_start(out=nqn[:, 64:64 + TO], in_=S.bqT[r0:r0 + 128, 0:TO]), writes=["nqn"], dma=True)
                    P.op("sp", lambda h, r0=r0: h.dma_start(out=nkn[:], in_=S.bkT[r0:r0 + 128, 0:NKT]), writes=["nkn"], dma=True)
                    P.op("sp", lambda h, r0=r0: h.dma_start(out=nzs[:], in_=S.bzT[r0:r0 + 128, 0:TO]), writes=["nzs"], dma=True)
                    P.op("sp", lambda h, r0=r0: h.dma_start(out=ve[:], in_=S.bv[0:18 * 128, r0:r0 + 128].rearrange("(t p) d -> p t d", p=128)), writes=["ve"], dma=True)
                    P.op("sp", lambda h, r0=r0: h.dma_start(out=vo[:], in_=S.bv[64:64 + 17 * 128, r0:r0 + 128].rearrange("(t p) d -> p t d", p=128)), writes=["vo"], dma=True)
                    for (EB, ek, o0) in ((EBe, "EBe", 0), (EBo, "EBo", 1)):
                        for two in range(2):
                            P.op("sp", lambda h, EB=EB, o0=o0, two=two, hd=hd: h.dma_start(out=EB[two * 64:(two + 1) * 64, :, :], in_=I.B2[hd, o0 + two:o0 + two + 13:2].rearrange("r k q -> k r q")),
                                 writes=[ek], dma=True)
                        P.op("dve", lambda h, EB=EB: h.tensor_tensor(EB[:], EB[:], cm[:].unsqueeze(1).to_broadcast([128, 7, 64]), ALU.add), reads=[ek, "cm"], writes=[ek])
                        P.op("act", lambda h, EB=EB: h.activation(EB[:], EB[:], AF.Exp), reads=[ek], writes=[ek])
                    P.op("act", lambda h: h.activation(nzs[:], nzs[:], AF.Silu), reads=["nzs"], writes=["nzs"])
                    for (dst, dk, gi_, ntok, doff) in ((nqn, "nqn", 0, TO, 64), (nkn, "nkn", 1, NKT, 0)):
                        for t0 in range(0, ntok, 512):
                            w = min(512, ntok - t0)
                            P.op("act", lambda h, dst=dst, t0=t0, w=w, doff=doff: h.activation(sq[:, 0:w], dst[:, doff + t0:doff + t0 + w], AF.Square), reads=[dk], writes=["sq"])
                            P.op("pe", lambda h, w=w: h.matmul(p_ms[:, 0:w], onesS[:], sq[:, 0:w], start=True, stop=True), reads=["onesS", "sq"], writes=["pod"])
                            P.op("act", lambda h, w=w: h.activation(sd[:, 0:w], p_ms[:, 0:w], AF.Sqrt, bias=K.eps_t[:, 0:1]), reads=["pod", "eps_t"], writes=["sd"])
                            P.op("dve", lambda h, w=w: h.reciprocal(sd[:, 0:w], sd[:, 0:w]), reads=["sd"], writes=["sd"])
                            P.op("dve", lambda h, dst=dst, t0=t0, w=w, gi_=gi_, doff=doff: h.scalar_tensor_tensor(dst[:, doff + t0:doff + t0 + w], dst[:, doff + t0:doff + t0 + w], gq[:, gi_:gi_ + 1], sd[:, 0:w], ALU.mult, ALU.mult),
                                 reads=[dk, "gq", "sd"], writes=[dk])
                            yield
                    P.op("dve", lambda h: h.tensor_scalar(nqs[:], nqn[:, 64:64 + QW], ab[:, 0:1], None, ALU.mult), reads=["nqn", "ab"], writes=["nqs"])
                    P.op("dve", lambda h: h.scalar_tensor_tensor(nqs[:], nqn[:, 0:QW], ab[:, 1:2], nqs[:], ALU.mult, ALU.add), reads=["nqn", "ab", "nqs"], writes=["nqs"])
                    for r in range(NQR):
                        rs = max(r - 4, 0)
                        d0 = rs - r + 7
                        EB, ek, j0 = (EBe, "EBe", d0 // 2) if d0 % 2 == 0 else (EBo, "EBo", (d0 - 1) // 2)
                        b = r % 2
                        slot = r % 4
                        for i in range(4):
                            tk0 = (rs + 2 * i) * 64
                            P.op("pe", lambda h, b=b, i=i, tk0=tk0, r=r: h.matmul(p_s[b][:, i, :], nkn[:, tk0:tk0 + 128], nqs[:, r * 64:(r + 1) * 64], start=True, stop=True),
                                 reads=["nkn", "nqs"], writes=["p_s%d" % b])
                        yield
                        P.op("act", lambda h, b=b: h.activation(pe_[b][:], p_s[b][:], AF.Exp), reads=["p_s%d" % b], writes=["pe%d" % b])
                        P.op("dve", lambda h, b=b, EB=EB, j0=j0: h.tensor_tensor(pt_[b][:], pe_[b][:], EB[:, j0:j0 + 4, :], ALU.mult), reads=["pe%d" % b, ek], writes=["pt%d" % b])
                        yield
                        for i in range(4):
                            kr = rs + 2 * i
                            vt = ve[:, kr // 2, :] if kr % 2 == 0 else vo[:, (kr - 1) // 2, :]
                            P.op("pe", lambda h, b=b, i=i, vt=vt, slot=slot: h.matmul(p_o[:, slot * 64:(slot + 1) * 64], vt, pt_[b][:, i, :], start=(i == 0), stop=(i == 3)),
                                 reads=["ve", "vo", "pt%d" % b], writes=["pod"])
                        for i in range(4):
                            P.op("pe", lambda h, b=b, i=i, slot=slot: h.matmul(p_d[:, slot * 64:(slot + 1) * 64], ones1[:], pt_[b][:, i, :], start=(i == 0), stop=(i == 3)),
                                 reads=["ones1", "pt%d" % b], writes=["pod"])
                        yield
                        if slot == 3 or r == NQR - 1:
                            w = (slot + 1) * 64
                            c0 = (r // 4) * 256
                            P.op("dve", lambda h, w=w: h.reciprocal(rd[:, 0:w], p_d[:, 0:w]), reads=["pod"], writes=["rd"])
                            P.op("dve", lambda h, w=w, c0=c0: h.tensor_tensor(Ys[:, c0:c0 + w], p_o[:, 0:w], rd[:, 0:w], ALU.mult), reads=["pod", "rd"], writes=["Ys"])
                    for tb in range(0, TO, 512):
                        P.op("dve", lambda h, tb=tb: h.tensor_scalar(yo[:], Ys[:, tb:tb + 512], ab[:, 0:1], None, ALU.mult), reads=["Ys", "ab"], writes=["yo"])
                        P.op("dve", lambda h, tb=tb: h.scalar_tensor_tensor(yo[:], Ys[:, tb + 64:tb + 64 + 512], ab[:, 1:2], yo[:], ALU.mult, ALU.add), reads=["Ys", "ab", "yo"], writes=["yo"])
                        P.op("pool", lambda h, tb=tb: h.tensor_tensor(yob[:], yo[:], nzs[:, tb:tb + 512], ALU.mult), reads=["yo", "nzs"], writes=["yob"])
                        P.op("sp", lambda h, tb=tb, r0=r0: h.dma_start(out=S.yT[0][D + r0:D + r0 + 128, tb:tb + 512], in_=yob[:]), reads=["yob"], dma=True)
                        yield

            gens = [gen_ml(), gen_na()]
            while gens:
                for g in list(gens):
                    try:
                        next(g)
                    except StopIteration:
                        gens.remove(g)
            P.flush()
            P.exclusive = set()


def phase_outproj(K, layer, x_src, dst):
    nc, P, I, S = K.nc, K.P, K.I, K.S
    NKH = CW // 128 // 2
    with Alloc(nc) as A_:
        oy = A_.sb("oy", [128, NKH, TO], BF16)
        ow0 = A_.sb("ow0", [128, NKH, 512], BF16)
        ow1 = A_.sb("ow1", [128, NKH, 512], BF16)
        og0 = A_.sb("og0", [128, 512], F32)
        og1 = A_.sb("og1", [128, 512], F32)
        ox0 = A_.sb("ox0", [128, 512], F32)
        ox1 = A_.sb("ox1", [128, 512], F32)
        oo0 = A_.sb("oo0", [128, 512], F32)
        oo1 = A_.sb("oo1", [128, 512], F32)
        ops0 = A_.ps("ops0", [128, 512], F32)
        ops1 = A_.ps("ops1", [128, 512], F32)
        ow = [ow0, ow1]
        og = [og0, og1]
        ox = [ox0, ox1]
        oo = [oo0, oo1]
        ops = [ops0, ops1]
        it = 0
        pj = 0
        ntt = 1 if K.quick else TO // 128
        for kh in range(2):
            r0 = kh * (CW // 2)
            for q in range(4):
                P.op("sp", lambda h, r0=r0, q=q: h.dma_start(out=oy[:, q * 8:(q + 1) * 8, :], in_=S.yT[layer][r0 + q * 1024:r0 + (q + 1) * 1024, :].rearrange("(k p) t -> p k t", p=128)),
                     writes=["oy"], dma=True)
            prev = x_src if kh == 0 else dst
            for cb in range(D // 512):
                wb = ow[it % 2]
                wk = "ow%d" % (it % 2)
                gt_ = og[it % 2]
                gk = "og%d" % (it % 2)
                it += 1
                src = I.w_out[layer][r0:r0 + CW // 2, cb * 512:(cb + 1) * 512].rearrange("(k p) c -> p k c", p=128)
                for q in range(2):
                    P.op("pool", lambda h, wb=wb, src=src, q=q: h.dma_start(out=wb[:, q * 16:(q + 1) * 16, :], in_=src[:, q * 16:(q + 1) * 16, :]), writes=[wk], dma=True)
                P.op("sp", lambda h, gt_=gt_, cb=cb: h.dma_start(out=gt_[:], in_=S.modrep[layer][2, :, cb * 512:(cb + 1) * 512]), writes=[gk], dma=True)
                for tt in range(ntt):
                    ps = ops[pj % 2]
                    pk = "ops%d" % (pj % 2)
                    xt = ox[pj % 2]
                    xk = "ox%d" % (pj % 2)
                    ot_ = oo[pj % 2]
                    okk = "oo%d" % (pj % 2)
                    pj += 1
                    tok = tt * 128
                    dk_ = "dst_%d_%d" % (tt, cb)
                    for k in range(NKH):
                        P.op("pe", lambda h, ps=ps, wb=wb, k=k, tt=tt: h.matmul(ps[:], oy[:, k, tt * 128:(tt + 1) * 128], wb[:, k, :], start=(k == 0), stop=(k == NKH - 1)),
                             reads=["oy", wk], writes=[pk])
                    P.op("sp", lambda h, xt=xt, tok=tok, cb=cb, prev=prev: h.dma_start(out=xt[:], in_=prev[tok:tok + 128, cb * 512:(cb + 1) * 512]),
                         reads=([dk_] if kh == 1 else []), writes=[xk], dma=True)
                    P.op("dve", lambda h, ps=ps, ot_=ot_, gt_=gt_: h.tensor_tensor(ot_[:], ps[:], gt_[:], ALU.mult), reads=[pk, gk], writes=[okk])
                    P.op("pool", lambda h, ot_=ot_, xt=xt: h.tensor_tensor(ot_[:], ot_[:], xt[:], ALU.add), reads=[okk, xk], writes=[okk])
                    P.op("sp", lambda h, ot_=ot_, tok=tok, cb=cb: h.dma_start(out=dst[tok:tok + 128, cb * 512:(cb + 1) * 512], in_=ot_[:]), reads=[okk], writes=[dk_], dma=True)
        P.flush()


def phase_l1_inproj(K):
    nc, P, I, S = K.nc, K.P, K.I, K.S
    with nc.sbuf_tensor(U("hT1"), [128, NK, 2048], BF16) as hT:
        for ps_i in range(1):
            tok0 = ps_i * 2048
            norm_pass(K, S.x1[tok0:tok0 + 2048, :], 2048, 1, hT)
            with Alloc(nc) as A_:
                lb0 = A_.sb("lb0", [128, 512], BF16)
                lb1 = A_.sb("lb1", [128, 512], BF16)
                lb2 = A_.sb("lb2", [128, 512], BF16)
                lb3 = A_.sb("lb3", [128, 512], BF16)
                lf0 = A_.sb("lf0", [128, 512], F32)
                lf1 = A_.sb("lf1", [128, 512], F32)
                lf2 = A_.sb("lf2", [128, 512], F32)
                lf3 = A_.sb("lf3", [128, 512], F32)
                stb = Stager([lb0, lb1, lb2, lb3], "lb")
                stf = Stager([lf0, lf1, lf2, lf3], "lf")

                def ev_gelu(dst):
                    def f(ps, pk, c, tt, w):
                        st, sk = stb.next()
                        t1, k1 = stf.next()
                        P.op("act", lambda h: h.activation(t1[:], ps[:], AF.Square), reads=[pk], writes=[k1])
                        P.op("dve", lambda h: h.tensor_scalar(t1[:], t1[:], 0.044715, 1.0, ALU.mult, ALU.add), reads=[k1], writes=[k1])
                        P.op("dve", lambda h: h.tensor_tensor(t1[:], t1[:], ps[:], ALU.mult), reads=[k1, pk], writes=[k1])
                        P.op("act", lambda h: h.activation(t1[:], t1[:], AF.Sigmoid, scale=1.5957691216057308), reads=[k1], writes=[k1])
                        P.op("dve", lambda h: h.tensor_tensor(st[:], t1[:], ps[:], ALU.mult), reads=[k1, pk], writes=[sk])
                        P.op("sp", lambda h: h.dma_start(out=dst[tok0 + tt * 128:tok0 + (tt + 1) * 128, c:c + w], in_=st[:, 0:w]), reads=[sk], dma=True)
                    return f

                def ev_silu(dst):
                    def f(ps, pk, c, tt, w):
                        st, sk = stb.next()
                        P.op("act", lambda h: h.activation(st[:], ps[:], AF.Silu), reads=[pk], writes=[sk])
                        P.op("sp", lambda h: h.dma_start(out=dst[tok0 + tt * 128:tok0 + (tt + 1) * 128, c:c + w], in_=st[:, 0:w]), reads=[sk], dma=True)
                    return f

                W = I.w_in1
                nt = range(1) if K.quick else range(16)
                jobs = [
                    dict(mode="TM", w=W[:, 0:CW], ncols=CW, tiles=nt, evac=ev_gelu(S.gu)),
                    dict(mode="TM", w=W[:, CW:2 * CW], ncols=CW, tiles=nt, evac=ev_gelu(S.gv)),
                    dict(mode="TM", w=W[:, 2 * CW:3 * CW], ncols=CW, tiles=nt, evac=ev_silu(S.sz)),
                ]
                inproj(K, hT, jobs)


def phase_gmlp(K):
    nc, P, I, S = K.nc, K.P, K.I, K.S
    with Alloc(nc) as A_:
        vg = A_.sb("vg", [128, CW], F32)
        wsf = A_.sb("wsf", [128, 8, 128], F32)
        wsb = A_.sb("wsb", [128, 8, 128], BF16)
        bsf = A_.sb("bsf", [128, 8], F32)
        gvt = A_.sb("gvt", [128, CW], BF16)
        gvs = A_.sb("gvs", [128, CW], BF16)
        gvn = A_.sb("gvn", [128, CW], BF16)
        gut = A_.sb("gut", [128, CW], BF16)
        szt = A_.sb("szt", [128, CW], BF16)
        ych = A_.sb("ych", [128, CW], BF16)
        yTc = A_.sb("yTc", [128, 64, 128], BF16)
        gt0 = A_.sb("gt0", [128, 512], F32)
        gt1 = A_.sb("gt1", [128, 512], F32)
        gsm = A_.sb("gsm", [128, 4], F32)
        gp0 = A_.ps("gp0", [128, 512], F32)
        gp1 = A_.ps("gp1", [128, 512], F32)
        gtr0 = A_.ps("gtr0", [128, 512], BF16)
        gtr1 = A_.ps("gtr1", [128, 512], BF16)
        gp = [gp0, gp1]
        gt = [gt0, gt1]
        gtr = [gtr0, gtr1]
        P.op("sp", lambda h: h.dma_start(out=vg[:], in_=I.vg_rep), writes=["vg"], dma=True)
        P.op("sp", lambda h: h.dma_start(out=wsf[:], in_=I.WsT.rearrange("g s t -> s g t")), writes=["wsf"], dma=True)
        P.op("dve", lambda h: h.tensor_copy(wsb[:], wsf[:]), reads=["wsf"], writes=["wsb"])
        P.op("sp", lambda h: h.dma_start(out=bsf[:], in_=I.bs_fm), writes=["bsf"], dma=True)
        nchunks = 1 if K.quick else NOWN
        pj = 0
        for c in range(nchunks):
            t0 = c * 128
            P.op("sp", lambda h, t0=t0: h.dma_start(out=gvt[:], in_=S.gv[t0:t0 + 128, :]), writes=["gvt"], dma=True)
            P.op("sp", lambda h, t0=t0: h.dma_start(out=gut[:], in_=S.gu[t0:t0 + 128, :]), writes=["gut"], dma=True)
            P.op("sp", lambda h, t0=t0: h.dma_start(out=szt[:], in_=S.sz[t0:t0 + 128, :]), writes=["szt"], dma=True)
            P.op("act", lambda h: h.activation(gvs[:], gvt[:], AF.Square), reads=["gvt"], writes=["gvs"])
            P.op("dve", lambda h: h.tensor_reduce(gsm[:, 0:1], gvs[:], AX.X, ALU.add), reads=["gvs"], writes=["gsm"])
            P.op("act", lambda h: h.activation(gsm[:, 1:2], gsm[:, 0:1], AF.Sqrt, bias=K.eps_t[:, 0:1], scale=1.0 / CW), reads=["gsm", "eps_t"], writes=["gsm"])
            P.op("dve", lambda h: h.reciprocal(gsm[:, 2:3], gsm[:, 1:2]), reads=["gsm"], writes=["gsm"])
            P.op("dve", lambda h: h.scalar_tensor_tensor(gvn[:], gvt[:], gsm[:, 2:3], vg[:], ALU.mult, ALU.mult), reads=["gvt", "gsm", "vg"], writes=["gvn"])
            for g in range(8):
                for hh_ in range(2):
                    cs_ = g * 1024 + hh_ * 512
                    ps = gp[pj % 2]
                    pk = "gp%d" % (pj % 2)
                    tt_ = gt[pj % 2]
                    tk = "gt%d" % (pj % 2)
                    pj += 1
                    P.op("pe", lambda h, ps=ps, g=g, cs_=cs_: h.matmul(ps[:], wsb[:, g, :], gvn[:, cs_:cs_ + 512], start=True, stop=True), reads=["wsb", "gvn"], writes=[pk])
                    P.op("dve", lambda h, ps=ps, tt_=tt_, g=g, cs_=cs_: h.scalar_tensor_tensor(tt_[:], ps[:], bsf[:, g:g + 1], gut[:, cs_:cs_ + 512], ALU.add, ALU.mult), reads=[pk, "bsf", "gut"], writes=[tk])
                    P.op("pool", lambda h, tt_=tt_, cs_=cs_: h.tensor_tensor(ych[:, cs_:cs_ + 512], tt_[:], szt[:, cs_:cs_ + 512], ALU.mult), reads=[tk, "szt"], writes=["ych"])
            for k4 in range(0, 64, 4):
                tp = gtr[(k4 // 4) % 2]
                tk = "gtr%d" % ((k4 // 4) % 2)
                for q in range(4):
                    P.op("pe", lambda h, tp=tp, q=q, k4=k4: h.transpose(tp[:, q * 128:(q + 1) * 128], ych[:, (k4 + q) * 128:(k4 + q + 1) * 128], K.ident_b[:]), reads=["ych", "ident_b"], writes=[tk])
                if (k4 // 4) % 2:
                    P.op("act", lambda h, tp=tp, k4=k4: h.copy(yTc[:, k4:k4 + 4, :], tp[:].rearrange("p (a b) -> p a b", a=4)), reads=[tk], writes=["yTc"])
                else:
                    P.op("dve", lambda h, tp=tp, k4=k4: h.tensor_copy(yTc[:, k4:k4 + 4, :], tp[:].rearrange("p (a b) -> p a b", a=4)), reads=[tk], writes=["yTc"])
            for q in range(4):
                P.op("sp", lambda h, t0=t0, q=q: h.dma_start(out=S.yT[1][q * 2048:(q + 1) * 2048, t0:t0 + 128].rearrange("(k p) t -> p k t", p=128), in_=yTc[:, q * 16:(q + 1) * 16, :]),
                     reads=["yTc"], dma=True)
        P.flush()


def prep_core(inp, b, hf):
    rev = hf == 1
    norm_g = [inp["norm_g0"], inp["norm_g1"]]
    ada_w = [inp["ada_w0"], inp["ada_w1"]]
    ada_b = [inp["ada_b0"], inp["ada_b1"]]
    w_out = [inp["w_out0"], inp["w_out1"]]
    m = {}
    xb = inp["x"][b]
    m["x"] = np.ascontiguousarray(xb[::-1] if rev else xb)
    m["c_fm"] = np.ascontiguousarray(inp["c"][b].reshape(NK, 128).T)
    for l in range(2):
        m["g_rep%d" % l] = np.ascontiguousarray(np.broadcast_to(norm_g[l][None, :], (128, D)))
        m["ada_w%d" % l] = ada_w[l]
        m["ada_b_rep%d" % l] = np.ascontiguousarray(np.broadcast_to(ada_b[l][None, :], (128, 3 * D)))
        m["w_out%d" % l] = w_out[l]
    m["w_in0"] = inp["w_in0"]
    m["w_in1"] = inp["w_in1"]
    m["ident"] = np.eye(128, dtype=np.float32)
    gperm = np.arange(32)
    if rev:
        gperm = np.concatenate([np.arange(16, 32), np.arange(0, 16)])
    m["w_gates"] = np.ascontiguousarray(inp["w_in0"][:, C_G:C_G + 32][:, gperm])
    m["gate_b"] = np.ascontiguousarray(inp["a_gate_b"][gperm].reshape(4, 8).T)
    cw = inp["a_conv_w"][::-1] if rev else inp["a_conv_w"]
    m["convw"] = np.ascontiguousarray(cw.T.reshape(32, 128, 3).transpose(1, 0, 2))
    m["ang_rep"] = np.ascontiguousarray(np.broadcast_to(inp["a_norm_g"].reshape(1, D), (128, D)))
    m["gqk"] = np.ascontiguousarray(np.stack([inp["b_q_gain"], inp["b_k_gain"]], axis=1))
    kc = np.arange(64)[:, None]
    qc = np.arange(64)[None, :]
    dci = np.clip(kc - qc + 15, 0, 30)
    rpb = inp["b_rpb"]
    if not rev:
        m["B2"] = np.ascontiguousarray(rpb[:, :, dci])
        cs = np.clip(qc - 8, 0, 48)
    else:
        dru = np.clip(13 - np.arange(15), 0, 14)
        m["B2"] = np.ascontiguousarray(rpb[:, dru][:, :, 30 - dci])
        cs = np.clip(qc - 7, 0, 48)
    cmask = np.where((kc >= cs) & (kc < cs + 16), 0.0, -30000.0).astype(np.float32)
    m["cmask"] = np.ascontiguousarray(np.concatenate([cmask, cmask], axis=0))
    s_ = np.arange(128)[:, None]
    t_ = np.arange(128)[None, :]
    m["masks"] = np.stack([(s_ <= t_), (s_ >= t_)]).astype(np.float32)
    m["ab"] = np.ascontiguousarray(np.broadcast_to(np.array([[0.0, 1.0]] if rev else [[1.0, 0.0]], np.float32), (128, 2)))
    m["vg_rep"] = np.ascontiguousarray(np.broadcast_to(inp["c_v_norm_g"][None, :], (128, CW)))
    ws = inp["c_w_s"]
    bs = inp["c_b_s"]
    if rev:
        ws = ws[:, ::-1, ::-1]
        bs = bs[:, ::-1]
    m["WsT"] = np.ascontiguousarray(ws.transpose(0, 2, 1))
    m["bs_fm"] = np.ascontiguousarray(bs.T)
    return m


def kernel(**inputs):
    inp = {k: np.asarray(v) for k, v in inputs.items()}
    nc = build()
    in_maps = [prep_core(inp, c // 2, c % 2) for c in range(8)]
    res = run_bass_kernel_spmd(nc, in_maps, core_ids=list(range(8)))
    out = np.empty((4, T, D), np.float32)
    for c in range(8):
        b, hf = c // 2, c % 2
        o = res.results[c]["out"]
        if hf == 0:
            out[b, :TO] = o
        else:
            out[b, TO:] = o[::-1]
    return out
```
